# Optimizing a Trainium2 kernel written in Bass

```python
import math
import jax, jax.numpy as jnp
from jax import lax
import numpy as np

D_MODEL = 1024
BATCH = 8
SEQ = 4096
DEPTH = 1

D_FF = 2816
DA_HEADS = 4
DA_QK = 64
DA_V = 2 * DA_QK
DA_WIDTH = DA_HEADS * DA_V
Q_BLOCK = 128
RET_HEADS = 4
RET_DK = 128
RET_DV = 128
RET_WIDTH = RET_HEADS * RET_DV
RET_CHUNK = 128
SPLIT_SIZES = (
    DA_HEADS * 2 * DA_QK,
    DA_HEADS * 2 * DA_QK,
    DA_WIDTH,
    RET_HEADS * RET_DK,
    RET_HEADS * RET_DK,
    RET_WIDTH,
    RET_WIDTH,
    D_MODEL,
    D_MODEL,
)
D_IN = sum(SPLIT_SIZES)
NORM_EPS = 1e-6
SUBLN_EPS = 1e-5
NEG_INF = -1e30

kernel_name = "hybrid_diffattn_retnet_macaron"


def rmsnorm(x, g, eps=NORM_EPS):
    xf = x.astype(jnp.float32)
    y = xf * lax.rsqrt(jnp.mean(xf * xf, axis=-1, keepdims=True) + eps)
    return (y * g.astype(jnp.float32)).astype(x.dtype)


def swiglu(h, w_gate, w_up, w_down):
    return (jax.nn.silu(h @ w_gate) * (h @ w_up)) @ w_down


def diff_attention(q, k, v, lq1, lk1, lq2, lk2, subln_g, lambda_init):
    B, T, _ = q.shape
    dtype = q.dtype
    q = q.reshape(B, T, 2 * DA_HEADS, DA_QK)
    k = k.reshape(B, T, 2 * DA_HEADS, DA_QK)
    v = v.reshape(B, T, DA_HEADS, DA_V).astype(jnp.float32)
    lam = (jnp.exp(jnp.sum(lq1.astype(jnp.float32) * lk1.astype(jnp.float32)))
           - jnp.exp(jnp.sum(lq2.astype(jnp.float32) * lk2.astype(jnp.float32)))
           + lambda_init)
    scale = DA_QK ** -0.5
    nb = T // Q_BLOCK
    qb = jnp.moveaxis(q.reshape(B, nb, Q_BLOCK, 2 * DA_HEADS, DA_QK), 1, 0)
    kpos = jnp.arange(T)

    def block(args):
        q_blk, i = args
        qpos = i * Q_BLOCK + jnp.arange(Q_BLOCK)
        mask = kpos[None, :] <= qpos[:, None]
        s = jnp.einsum('bqmd,bkmd->bmqk', q_blk, k).astype(jnp.float32) * scale
        s = jnp.where(mask[None, None], s, NEG_INF)
        p = jax.nn.softmax(s, axis=-1).reshape(B, DA_HEADS, 2, Q_BLOCK, T)
        a = p[:, :, 0] - lam * p[:, :, 1]
        return jnp.einsum('bhqk,bkhe->bqhe', a, v)

    o = lax.map(block, (qb, jnp.arange(nb, dtype=jnp.int32)))
    o = jnp.moveaxis(o, 0, 1).reshape(B, T, DA_HEADS, DA_V)
    o = rmsnorm(o, subln_g, SUBLN_EPS).astype(jnp.float32) * (1.0 - lambda_init)
    return o.reshape(B, T, DA_WIDTH).astype(dtype)


def rotate_every_two(x):
    x1 = x[..., ::2]
    x2 = x[..., 1::2]
    return jnp.stack((-x2, x1), axis=-1).reshape(x.shape)


def retention(q, k, v, gate):
    B, T, _ = q.shape
    dtype = q.dtype
    f32 = jnp.float32
    q = q.astype(f32).reshape(B, T, RET_HEADS, RET_DK)
    k = k.astype(f32).reshape(B, T, RET_HEADS, RET_DK) * (RET_DK ** -0.5)
    v = v.astype(f32).reshape(B, T, RET_HEADS, RET_DV)
    angle = 1.0 / (10000.0 ** jnp.linspace(0.0, 1.0, RET_DK // 2, dtype=f32))
    angle = jnp.repeat(angle, 2)
    ang = jnp.arange(T, dtype=f32)[:, None] * angle[None, :]
    sin = jnp.sin(ang)[:, None, :]
    cos = jnp.cos(ang)[:, None, :]
    q = q * cos + rotate_every_two(q) * sin
    k = k * cos + rotate_every_two(k) * sin

    C = RET_CHUNK
    N = T // C
    qc = q.reshape(B, N, C, RET_HEADS, RET_DK)
    kc = k.reshape(B, N, C, RET_HEADS, RET_DK)
    vc = v.reshape(B, N, C, RET_HEADS, RET_DV)
    log_g = jnp.log(1.0 - 2.0 ** (-5.0 - jnp.arange(RET_HEADS, dtype=f32)))
    idx = jnp.arange(C, dtype=f32)
    diff = idx[:, None] - idx[None, :]
    dmask = jnp.where(diff[None] >= 0,
                      jnp.exp(log_g[:, None, None] * jnp.maximum(diff, 0.0)[None]), 0.0)
    s = jnp.einsum('bnqhd,bnkhd->bnhqk', qc, kc) * dmask
    intra = jnp.einsum('bnhqk,bnkhe->bnqhe', s, vc)
    zeta = jnp.exp(log_g[:, None] * (C - 1.0 - idx)[None, :])
    kv = jnp.einsum('bnkhd,bnkhe,hk->nbhde', kc, vc, zeta)
    chunk_decay = jnp.exp(log_g * C)[None, :, None, None]

    def step(state, kv_n):
        return chunk_decay * state + kv_n, state

    _, prev = lax.scan(step, jnp.zeros((B, RET_HEADS, RET_DK, RET_DV), f32), kv)
    xi = jnp.exp(log_g[:, None] * (idx + 1.0)[None, :])
    inter = jnp.einsum('bnqhd,nbhde,hq->bnqhe', qc, prev, xi)
    y = (intra + inter).reshape(B, T, RET_HEADS, RET_DV)
    y = y * lax.rsqrt(jnp.mean(y * y, axis=-1, keepdims=True) + NORM_EPS)
    y = y.reshape(B, T, RET_WIDTH) * jax.nn.silu(gate.astype(f32))
    return y.astype(dtype)


def setup_inputs(seed: int = 0) -> dict:
    key = jax.random.key(seed)
    ks = jax.random.split(key, 32)
    L, D, F = DEPTH, D_MODEL, D_FF
    nrm = lambda k, shape, fan: jax.random.normal(k, shape, jnp.float32) * fan ** -0.5
    gain = lambda k, n: 1.0 + 0.02 * jax.random.normal(k, (L, n), jnp.float32)
    return {
        "x": jax.random.normal(ks[0], (BATCH, SEQ, D), jnp.float32),
        "ffn1_pre_g": gain(ks[1], D),
        "ffn1_w_gate": nrm(ks[2], (L, D, F), D),
        "ffn1_w_up": nrm(ks[3], (L, D, F), D),
        "ffn1_w_down": nrm(ks[4], (L, F, D), F),
        "ffn1_post_g": gain(ks[5], D),
        "mix_pre_g": gain(ks[6], D),
        "w_in": nrm(ks[7], (L, D, D_IN), D),
        "lambda_q1": 0.1 * jax.random.normal(ks[8], (L, DA_QK), jnp.float32),
        "lambda_k1": 0.1 * jax.random.normal(ks[9], (L, DA_QK), jnp.float32),
        "lambda_q2": 0.1 * jax.random.normal(ks[10], (L, DA_QK), jnp.float32),
        "lambda_k2": 0.1 * jax.random.normal(ks[11], (L, DA_QK), jnp.float32),
        "diff_subln_g": gain(ks[12], DA_V),
        "w_attn_proj": nrm(ks[13], (L, DA_WIDTH, D), DA_WIDTH),
        "w_ret_proj": nrm(ks[14], (L, RET_WIDTH, D), RET_WIDTH),
        "w_out": nrm(ks[15], (L, D, D), D),
        "mix_post_g": gain(ks[16], D),
        "ffn2_pre_g": gain(ks[17], D),
        "ffn2_w_gate": nrm(ks[18], (L, D, F), D),
        "ffn2_w_up": nrm(ks[19], (L, D, F), D),
        "ffn2_w_down": nrm(ks[20], (L, F, D), F),
        "ffn2_post_g": gain(ks[21], D),
    }


def reference(x, ffn1_pre_g, ffn1_w_gate, ffn1_w_up, ffn1_w_down, ffn1_post_g,
              mix_pre_g, w_in, lambda_q1, lambda_k1, lambda_q2, lambda_k2,
              diff_subln_g, w_attn_proj, w_ret_proj, w_out, mix_post_g,
              ffn2_pre_g, ffn2_w_gate, ffn2_w_up, ffn2_w_down, ffn2_post_g):
    cuts = list(np.cumsum(SPLIT_SIZES)[:-1])
    for l in range(DEPTH):
        lambda_init = 0.8 - 0.6 * math.exp(-0.3 * l)
        h = rmsnorm(x, ffn1_pre_g[l])
        x = x + 0.5 * rmsnorm(swiglu(h, ffn1_w_gate[l], ffn1_w_up[l], ffn1_w_down[l]), ffn1_post_g[l])
        h = rmsnorm(x, mix_pre_g[l])
        z = h @ w_in[l]
        dq, dk, dv, rq, rk, rv, rg, ga, gr = jnp.split(z, cuts, axis=-1)
        y_a = diff_attention(dq, dk, dv, lambda_q1[l], lambda_k1[l], lambda_q2[l],
                             lambda_k2[l], diff_subln_g[l], lambda_init) @ w_attn_proj[l]
        y_r = retention(rq, rk, rv, rg) @ w_ret_proj[l]
        merged = jax.nn.sigmoid(ga) * y_a + jax.nn.sigmoid(gr) * y_r
        x = x + rmsnorm(merged @ w_out[l], mix_post_g[l])
        h = rmsnorm(x, ffn2_pre_g[l])
        x = x + 0.5 * rmsnorm(swiglu(h, ffn2_w_gate[l], ffn2_w_up[l], ffn2_w_down[l]), ffn2_post_g[l])
    return x
```

```python
import math
from contextlib import ExitStack

import numpy as np
import concourse.bass as bass
import concourse.mybir as mybir
from concourse.bass_utils import run_bass_kernel_spmd

F32 = mybir.dt.float32
BF16 = mybir.dt.bfloat16
AF = mybir.ActivationFunctionType
ALU = mybir.AluOpType
AX = mybir.AxisListType

D = 1024
DFF = 2816
SEQ = 4096
NCORES = 8
TT = 512
NSUB = 4
LAMBDA_INIT = 0.8 - 0.6 * math.exp(-0.3 * 0)
NORM_EPS = 1e-6
SUBLN_EPS = 1e-5
RET_GAMMA = [1.0 - 2.0 ** (-5.0 - h) for h in range(4)]

PE, ACT, DVE, POOL, SP = "tensor", "scalar", "vector", "gpsimd", "sync"
ENGS = (PE, ACT, DVE, POOL, SP)


class Buf:
    __slots__ = ("name", "w", "r", "sem", "dcnt", "excl")

    def __init__(self, name, excl=False):
        self.name = name
        self.excl = excl
        self.w = None
        self.r = []
        self.sem = None
        self.dcnt = 0


class Op:
    __slots__ = ("eng", "fn", "waits", "need_inc", "is_dma", "sem", "val")

    def __init__(self, eng, fn, is_dma=False):
        self.eng = eng
        self.fn = fn
        self.waits = []
        self.need_inc = False
        self.is_dma = is_dma
        self.sem = None
        self.val = None


class Prog:
    def __init__(self, nc, stack, n_dma_sems=48):
        self.nc = nc
        self.ops = {e: [] for e in ENGS}
        self.esem = {e: stack.enter_context(nc.semaphore("es_" + e)) for e in ENGS}
        self.free_sems = [stack.enter_context(nc.semaphore("ds%d" % i)) for i in range(n_dma_sems)]

    def _dep(self, op, prev, same_ok):
        if prev is None or prev is op:
            return
        if (not prev.is_dma) and (not op.is_dma) and prev.eng == op.eng == PE:
            return
        if not prev.is_dma:
            prev.need_inc = True
        op.waits.append(prev)

    def _track(self, op, reads, writes):
        ex = [b for b in reads if b.excl]
        if ex:
            reads = [b for b in reads if not b.excl]
            writes = list(writes) + [b for b in ex if b not in writes]
        for b in reads:
            self._dep(op, b.w, same_ok=(op.eng == PE))
        for b in writes:
            self._dep(op, b.w, same_ok=True)
            for r in b.r:
                self._dep(op, r, same_ok=True)
        for b in reads:
            b.r.append(op)
        for b in writes:
            b.w = op
            b.r = []

    def op(self, eng, fn, reads=(), writes=()):
        o = Op(eng, fn)
        self._track(o, reads, writes)
        self.ops[eng].append(o)
        return o

    def dma(self, queue, out_ap, in_ap, sembuf, reads=(), writes=(), **kw):
        if sembuf.sem is None:
            sembuf.sem = {}
            sembuf.dcnt = {}
        if queue not in sembuf.sem:
            sembuf.sem[queue] = self.free_sems.pop()
            sembuf.dcnt[queue] = 0
        sembuf.dcnt[queue] += 1
        o = Op(queue, lambda eng: eng.dma_start(out=out_ap, in_=in_ap, **kw), is_dma=True)
        o.sem = sembuf.sem[queue]
        o.val = 16 * sembuf.dcnt[queue]
        self._track(o, reads, writes)
        self.ops[queue].append(o)
        return o

    def final_wait(self, eng, ops):
        o = Op(eng, None)
        for p in ops:
            self._dep(o, p, same_ok=False)
        self.ops[eng].append(o)

    def emit(self):
        nc = self.nc
        for e in ENGS:
            cnt = 0
            for o in self.ops[e]:
                if (not o.is_dma) and o.need_inc:
                    cnt += 1
                    o.val = cnt
                    o.sem = self.esem[e]
        with nc.Block() as block:
            for e in ENGS:
                def body(eng, e=e):
                    seen = {}
                    for o in self.ops[e]:
                        for w in o.waits:
                            k = id(w.sem)
                            if seen.get(k, 0) >= w.val:
                                continue
                            eng.wait_ge(w.sem, w.val)
                            seen[k] = w.val
                        if o.fn is None:
                            continue
                        ins = o.fn(eng)
                        if o.is_dma:
                            ins.then_inc(o.sem, 16)
                        elif o.need_inc:
                            ins.then_inc(o.sem, 1)
                getattr(block, e)(body)


C_ID, C_TRI, C_END = 0, 128, 256


def _const_tables(T):
    cst = np.zeros((128, C_END), np.float32)
    idx = np.arange(128)
    cst[:, C_ID:C_ID + 128] = np.eye(128, dtype=np.float32)
    cst[:, C_TRI:C_TRI + 128] = (idx[None, :] >= idx[:, None]).astype(np.float32)
    angle = (1.0 / (10000.0 ** np.linspace(0.0, 1.0, 64, dtype=np.float32))).astype(np.float32)
    angle = np.repeat(angle, 2)
    ang = (np.arange(T, dtype=np.float32)[:, None] * angle[None, :]).astype(np.float32)
    sn, cs = np.sin(ang).astype(np.float32), np.cos(ang).astype(np.float32)
    sgn = np.where(np.arange(128) % 2 == 0, -1.0, 1.0).astype(np.float32)
    ks = np.float32(128.0 ** -0.5)
    tl = (np.arange(T) % 128).astype(np.float64)
    rots = []
    for h in range(4):
        lg64 = np.log(np.float64(RET_GAMMA[h]))
        qa = np.exp(lg64 * (tl + 1.0)).astype(np.float32)[:, None]
        kb = (np.exp(-lg64 * (tl + 1.0)) * np.float64(ks)).astype(np.float32)[:, None]
        rots.append(np.concatenate([cs * qa, cs * kb, sn * sgn * qa, sn * sgn * kb], axis=1))
    rot = np.concatenate(rots, axis=0).astype(np.float32)
    return cst, rot


USE_DMA_CAST = True
NS = 8
NTMP = 8
NB = 32


def build(T, debug=False):
    NTILES = T // TT
    NKT = T // 128
    nc = bass.Bass("TRN2", target_bir_lowering=False)

    def din(name, shape):
        return nc.dram_tensor(name, shape, F32, kind="ExternalInput").ap()

    x_d = din("x", [T, D])
    out_d = nc.dram_tensor("out", [T, D], F32, kind="ExternalOutput").ap()
    Wd_ = {}
    for pre in ("ffn1", "ffn2"):
        Wd_[pre + "_w_gate"] = din(pre + "_w_gate", [D, DFF])
        Wd_[pre + "_w_up"] = din(pre + "_w_up", [D, DFF])
        Wd_[pre + "_w_down"] = din(pre + "_w_down", [DFF, D])
    Wd_["w_in"] = din("w_in", [D, 5632])
    Wd_["w_attn_proj"] = din("w_attn_proj", [512, D])
    Wd_["w_ret_proj"] = din("w_ret_proj", [512, D])
    Wd_["w_out"] = din("w_out", [D, D])
    gains = {n: din(n, [1, D]) for n in ("ffn1_pre_g", "ffn1_post_g", "mix_pre_g", "mix_post_g", "ffn2_pre_g", "ffn2_post_g")}
    lam_d = {n: din(n, [1, 64]) for n in ("lambda_q1", "lambda_k1", "lambda_q2", "lambda_k2")}
    subg_d = din("diff_subln_g", [1, 128])
    cst_d = din("cst", [128, C_END])
    rot_d = din("rot", [4 * T, 512])
    dbg_d = {}
    if debug:
        for n in ("dbg1", "dbg2"):
            dbg_d[n] = nc.dram_tensor(n, [T, D], F32, kind="ExternalOutput").ap()

    pieces = {}

    def addp(key, parts):
        pieces[key] = dict(parts=parts, idx=len(pieces))

    for pre in ("ffn1", "ffn2"):
        for i in range(11):
            addp((pre, "g", i), [(Wd_[pre + "_w_gate"], 0, 8, 256 * i, 256)])
            addp((pre, "u", i), [(Wd_[pre + "_w_up"], 0, 8, 256 * i, 256)])
            addp((pre, "d", i), [(Wd_[pre + "_w_down"], 2 * i, 2, 0, 1024)])
    win = Wd_["w_in"]
    for h in range(4):
        addp(("A", h), [(win, 0, 8, 128 * h, 128), (win, 0, 8, 512 + 128 * h, 128)])
        addp(("R", h), [(win, 0, 8, 1536 + 128 * h, 128), (win, 0, 8, 2048 + 128 * h, 128)])
        addp(("RV", h), [(win, 0, 8, 2560 + 128 * h, 128), (win, 0, 8, 3072 + 128 * h, 128)])
    for i in range(2):
        addp(("V", i), [(win, 0, 8, 1024 + 256 * i, 256)])
        addp(("AP", i), [(Wd_["w_attn_proj"], 0, 4, 512 * i, 512)])
        addp(("RP", i), [(Wd_["w_ret_proj"], 0, 4, 512 * i, 512)])
    for i in range(4):
        addp(("GA", i), [(win, 0, 8, 3584 + 256 * i, 256)])
        addp(("GR", i), [(win, 0, 8, 4608 + 256 * i, 256)])
        addp(("WO", i), [(Wd_["w_out"], 0, 8, 256 * i, 256)])
    NP = len(pieces)
    scr_d = nc.dram_tensor("wscr", [NP, 128, 2048], BF16).ap()

    with ExitStack() as st:
        P = Prog(nc, st, n_dma_sems=92)

        sbuf_bytes = [0]

        def sb(name, shape, dt):
            n = 1
            for d_ in shape[1:]:
                n *= d_
            sbuf_bytes[0] += n * (4 if dt == F32 else 2)
            return st.enter_context(nc.sbuf_tensor("sb_" + name, shape, dt))

        Kc = sb("Kc", [128, 4, T], BF16)
        bK = [[Buf("K%d_%d" % (j, h)) for h in range(4)] for j in range(NTILES)]
        Vc = sb("Vc", [128, NKT, 4, 130], BF16)
        bV = [Buf("V%d" % k) for k in range(NKT)]
        xs_ = [sb("x%d" % s, [128, D], F32) for s in range(NSUB)]
        bx = [Buf("x%d" % s) for s in range(NSUB)]
        Bp = sb("Bp", [128, NB, 512], BF16)
        bB = [Buf("B%d" % i) for i in range(NB)]
        slots = [sb("slot%d" % i, [128, 2048], BF16) for i in range(NS)]
        bslot = [Buf("slot%d" % i) for i in range(NS)]
        NSTG = 4
        stage = [sb("stg%d" % i, [128, 1024], F32) for i in range(NSTG)]
        bstage = [Buf("stg%d" % i) for i in range(NSTG)]
        gpost = sb("gpost", [128, D], F32)
        bgpost = Buf("gpost")
        tmpf = [sb("tf%d" % i, [128, 512], F32) for i in range(NTMP)]
        btmpf = [Buf("tf%d" % i) for i in range(NTMP)]
        cst = sb("cst", [128, C_END], F32)
        bcst = Buf("cst")
        cbf = sb("cbf", [128, 256], BF16)
        bcbf = Buf("cbf")
        rott = [sb("rot%d" % i, [128, 512], F32) for i in range(2)]
        brott = [Buf("rot%d" % i) for i in range(2)]
        gT = sb("gT", [128, 24], F32)
        bgT = Buf("gT")
        Sst = sb("Sst", [128, 512], F32)
        Sbf = sb("Sbf", [128, 512], BF16)
        bS = [Buf("S%d" % h) for h in range(4)]
        bSbf = [Buf("Sbf%d" % h) for h in range(4)]
        small = sb("small", [128, 64], F32)
        bsmall = [Buf("sm%d" % i) for i in range(64)]
        lamt = sb("lamt", [128, 4, 64], F32)
        blamt = Buf("lamt")
        lamv = sb("lamv", [128, 8], F32)
        blamv = Buf("lamv")
        mhalf = sb("mhalf", [128, 4], F32)
        bmhalf = Buf("mhalf")
        gsub = sb("gsub", [128, 128], F32)
        bgsub = Buf("gsub")
        yaE = sb("yaE", [128, 4, 128], F32)
        byaE = [Buf("yaE%d" % i) for i in range(4)]
        datok = sb("datok", [128, 4, 128], BF16)
        bdatok = [Buf("datok%d" % i) for i in range(4)]
        rsm = [sb("rsm%d" % i, [128, 1024], BF16) for i in range(2)]
        brsm = [{n: Buf("rsm%d_%s" % (i, n)) for n in ("qkrot", "qkT", "vtok", "vz", "sTm", "yr", "g1")} for i in range(2)]
        g1t = [sb("g1t%d" % i, [128, 128], F32) for i in range(2)]

        pb = [st.enter_context(nc.psum_tensor("pb%d" % i, [128, 512], F32)) for i in range(8)]
        bpb = [Buf("pb%d" % i, excl=True) for i in range(8)]
        pbh = [p.bitcast(BF16) for p in pb]
        bpbr = [{n: bpb[6 + i] for n in ("A", "B", "S", "KV", "I", "E")} for i in range(2)]

        ident = cbf[:, 0:128]
        tri = cbf[:, 128:256]

        state = dict(tf=0, sm=0, slot=0, stg=0, cast=0, inscr=set())

        def tf():
            i = state["tf"]
            state["tf"] = (i + 1) % NTMP
            return tmpf[i], btmpf[i]

        def sm():
            i = state["sm"]
            state["sm"] = (i + 1) % 64
            return small[:, i:i + 1], bsmall[i]

        def Bv(i):
            return Bp[:, i, :]

        def get_piece(key):
            pc = pieces[key]
            pi = pc["idx"]
            si = state["slot"]
            state["slot"] = (si + 1) % NS
            sl, bsl = slots[si], bslot[si]
            use_dc = USE_DMA_CAST and key[0] in ("ffn1", "ffn2")
            if pi not in state["inscr"] and use_dc:
                parts = pc["parts"]
                nrc = parts[0][2]
                ctot = sum(p[4] for p in parts)
                sv = sl[:, :].rearrange("p (k c) -> p k c", k=nrc)
                co = 0
                for (W, r0, _, c0, ncol) in parts:
                    src = W[r0 * 128:(r0 + nrc) * 128, c0:c0 + ncol].rearrange("(k p) c -> p k c", p=128)
                    P.dma(POOL, sv[:, :, co:co + ncol], src, bsl, writes=[bsl])
                    co += ncol
                bscr = Buf("scr%d" % pi)
                pc["bscr"] = bscr
                P.dma(SP, scr_d[pi], sl[:, :], bsl, reads=[bsl], writes=[bscr])
                state["inscr"].add(pi)
            elif pi not in state["inscr"]:
                parts = pc["parts"]
                nrc = parts[0][2]
                hr = nrc // 2
                ctot = sum(p[4] for p in parts)
                for hf in range(2):
                    g = state["stg"]
                    state["stg"] = (g + 1) % NSTG
                    ce = 1 + state["cast"]
                    state["cast"] = (state["cast"] + 1) % 2
                    sv = stage[g][:, :].rearrange("p (k c) -> p k c", k=hr)
                    co = 0
                    for (W, r0, _, c0, ncol) in parts:
                        rs = (r0 + hf * hr) * 128
                        src = W[rs:rs + hr * 128, c0:c0 + ncol].rearrange("(k p) c -> p k c", p=128)
                        P.dma(SP, sv[:, :, co:co + ncol], src, bstage[g], writes=[bstage[g]])
                        co += ncol
                    if ce == 0:
                        P.op(POOL, lambda e, sl=sl, hf=hf, g=g: e.tensor_copy(out=sl[:, hf * 1024:(hf + 1) * 1024], in_=stage[g][:, :]),
                             reads=[bstage[g]], writes=[bsl])
                    elif ce == 1:
                        P.op(ACT, lambda e, sl=sl, hf=hf, g=g: e.activation(out=sl[:, hf * 1024:(hf + 1) * 1024], in_=stage[g][:, :], func=AF.Copy),
                             reads=[bstage[g]], writes=[bsl])
                    else:
                        P.op(DVE, lambda e, sl=sl, hf=hf, g=g: e.tensor_copy(out=sl[:, hf * 1024:(hf + 1) * 1024], in_=stage[g][:, :]),
                             reads=[bstage[g]], writes=[bsl])
                bscr = Buf("scr%d" % pi)
                pc["bscr"] = bscr
                P.dma(POOL, scr_d[pi], sl[:, :], bsl, reads=[bsl], writes=[bscr])
                state["inscr"].add(pi)
            else:
                P.dma(SP, sl[:, :], scr_d[pi], bsl, reads=[pc["bscr"]], writes=[bsl])
            return sl, bsl

        def convert_rest():
            order = [("V", 0), ("V", 1), ("R", 0), ("RV", 0), ("R", 1), ("RV", 1), ("A", 0), ("A", 1), ("R", 2), ("RV", 2),
                     ("R", 3), ("RV", 3), ("A", 2), ("A", 3), ("AP", 0), ("RP", 0), ("GA", 0), ("GR", 0), ("GA", 1), ("GR", 1),
                     ("AP", 1), ("RP", 1), ("GA", 2), ("GR", 2), ("GA", 3), ("GR", 3)] + [("WO", i) for i in range(4)]
            for i in range(11):
                order += [("ffn2", "g", i), ("ffn2", "u", i)]
            order += [("ffn2", "d", i) for i in range(11)]
            convbufs = [Buf("conv%d" % i) for i in range(44)]
            ci = 0
            for key in order:
                pc = pieces[key]
                pi = pc["idx"]
                if pi in state["inscr"]:
                    continue
                cb = convbufs[ci % len(convbufs)]
                ci += 1
                parts = pc["parts"]
                nrc = parts[0][2]
                dv = scr_d[pi].rearrange("p (k c) -> p k c", k=nrc)
                bscr = Buf("scr%d" % pi)
                pc["bscr"] = bscr
                co = 0
                for (W, r0, _, c0, ncol) in parts:
                    src = W[r0 * 128:(r0 + nrc) * 128, c0:c0 + ncol].rearrange("(k p) c -> p k c", p=128)
                    P.dma(POOL, dv[:, :, co:co + ncol], src, cb, writes=[bscr, cb])
                    co += ncol
                state["inscr"].add(pi)
                yield

        P.dma(POOL, cst[:, :], cst_d, bcst, writes=[bcst])
        P.op(POOL, lambda e: e.tensor_copy(out=cbf[:, :], in_=cst[:, 0:256]), reads=[bcst], writes=[bcbf])
        for gi, n in enumerate(("ffn1_pre_g", "mix_pre_g", "ffn2_pre_g")):
            P.dma(POOL, gT[:, gi * 8:(gi + 1) * 8], gains[n].rearrange("o (k p) -> p (o k)", p=128), bgT, writes=[bgT],
                  allow_slow_non_contiguous=True)
        P.op(POOL, lambda e: e.tensor_scalar(out=gT[:, :], in0=gT[:, :], scalar1=float(math.sqrt(D)), scalar2=None, op0=ALU.mult),
             reads=[bgT], writes=[bgT])
        for i, n in enumerate(("lambda_q1", "lambda_k1", "lambda_q2", "lambda_k2")):
            P.dma(POOL, lamt[:, i, :], lam_d[n].partition_broadcast(128), blamt, writes=[blamt])
        P.op(DVE, lambda e: e.tensor_tensor(out=lamt[:, 0, :], in0=lamt[:, 0, :], in1=lamt[:, 1, :], op=ALU.mult), reads=[blamt], writes=[blamt])
        P.op(DVE, lambda e: e.tensor_tensor(out=lamt[:, 2, :], in0=lamt[:, 2, :], in1=lamt[:, 3, :], op=ALU.mult), reads=[blamt], writes=[blamt])
        P.op(DVE, lambda e: e.tensor_reduce(out=lamv[:, 2:3], in_=lamt[:, 0, :], axis=AX.X, op=ALU.add), reads=[blamt], writes=[blamv])
        P.op(DVE, lambda e: e.tensor_reduce(out=lamv[:, 3:4], in_=lamt[:, 2, :], axis=AX.X, op=ALU.add), reads=[blamt], writes=[blamv])
        P.op(ACT, lambda e: e.activation(out=lamv[:, 4:6], in_=lamv[:, 2:4], func=AF.Exp), reads=[blamv], writes=[blamv])
        P.op(DVE, lambda e: e.scalar_tensor_tensor(out=lamv[:, 0:1], in0=lamv[:, 5:6], scalar=float(-LAMBDA_INIT), in1=lamv[:, 4:5],
                                                   op0=ALU.add, op1=ALU.subtract), reads=[blamv], writes=[blamv])
        P.op(POOL, lambda e: e.memset(mhalf[:, :], -0.5), writes=[bmhalf])
        P.dma(POOL, gsub[:, :], subg_d.partition_broadcast(128), bgsub, writes=[bgsub])
        P.op(POOL, lambda e: e.tensor_scalar(out=gsub[:, :], in0=gsub[:, :], scalar1=float((1.0 - LAMBDA_INIT) * math.sqrt(128.0)),
                                             scalar2=None, op0=ALU.mult), reads=[bgsub], writes=[bgsub])
        P.op(POOL, lambda e: e.memset(Vc[:, :, :, :].rearrange("p a b c -> p (a b c)"), 1.0), writes=bV)
        P.op(POOL, lambda e: e.memset(Sst[:, :], 0.0), writes=bS)
        P.op(POOL, lambda e: e.memset(Sbf[:, :], 0.0), writes=bSbf)
        neglam = lamv[:, 0:1]

        def rsqrt_pool(dst, bdst, src, bsrc, n=1):
            P.op(POOL, lambda e: e.tensor_tensor(out=dst, in0=src, in1=mhalf[:, 0:n], op=ALU.pow),
                 reads=[bsrc, bmhalf], writes=[bdst])

        xnb = [sb("xnb%d" % i, [128, D], BF16) for i in range(NSUB)]
        bxnb = [Buf("xnb%d" % i) for i in range(NSUB)]
        XB = [xs_, stage]
        bXB = [bx, bstage]

        def pn_elem(s, X, bX):
            ss, bss = sm()
            se, bse = sm()
            r, br = sm()
            xb_ = xnb[s]
            P.op(ACT, lambda e: e.activation(out=xb_[:, :], in_=X[s][:, :], func=AF.Square, accum_out=ss),
                 reads=[bX[s]], writes=[bxnb[s], bss])
            P.op(DVE, lambda e: e.tensor_scalar(out=se, in0=ss, scalar1=float(NORM_EPS * D), scalar2=None, op0=ALU.add),
                 reads=[bss], writes=[bse])
            rsqrt_pool(r, br, se, bse)
            P.op(ACT, lambda e: e.activation(out=xb_[:, :], in_=X[s][:, :], func=AF.Copy, scale=r),
                 reads=[bX[s], br], writes=[bxnb[s]])

        def pn_pe(gi, s, bank):
            xb_ = xnb[s]
            for k in range(8):
                P.op(PE, lambda e, k=k: e.transpose(out=pbh[bank][:, k * 128:(k + 1) * 128], in_=xb_[:, k * 128:(k + 1) * 128], identity=ident),
                     reads=[bxnb[s], bcbf], writes=[bpb[bank]])
            P.op(DVE, lambda e: e.tensor_tensor(
                out=Bp[:, 0:8, s * 128:(s + 1) * 128],
                in0=pbh[bank][:, :].rearrange("p (k t) -> p k t", k=8),
                in1=gT[:, gi * 8:(gi + 1) * 8].unsqueeze(2).broadcast_to([128, 8, 128]), op=ALU.mult),
                reads=[bpb[bank], bgT], writes=bB[0:8])

        def load_gpost(name, coef):
            P.dma(POOL, gpost[:, :], gains[name].partition_broadcast(128), bgpost, writes=[bgpost])
            P.op(ACT, lambda e: e.activation(out=gpost[:, :], in_=gpost[:, :], func=AF.Copy, scale=float(coef * math.sqrt(D))),
                 reads=[bgpost], writes=[bgpost])

        def post_norm(s, bankA, bankB, fscale, X, bX):
            ssA, bssA = sm()
            ssB, bssB = sm()
            se, bse = sm()
            r, br = sm()
            for (bank, ss, bss) in ((bankA, ssA, bssA), (bankB, ssB, bssB)):
                junk, bjunk = tf()
                P.op(ACT, lambda e, bank=bank, ss=ss, junk=junk: e.activation(out=junk.bitcast(BF16)[:, 0:512], in_=pb[bank][:, :],
                                                                            func=AF.Square, accum_out=ss),
                     reads=[bpb[bank]], writes=[bjunk, bss])
            P.op(DVE, lambda e: e.tensor_scalar(out=se, in0=ssA, scalar1=ssB, scalar2=float(NORM_EPS * D / (fscale * fscale)),
                                                op0=ALU.add, op1=ALU.add), reads=[bssA, bssB], writes=[bse])
            rsqrt_pool(r, br, se, bse)
            for hf, bank in enumerate((bankA, bankB)):
                t, bt = tf()
                P.op(DVE, lambda e, bank=bank, t=t, hf=hf: e.scalar_tensor_tensor(out=t[:, :], in0=pb[bank][:, :], scalar=r,
                                                                                  in1=gpost[:, hf * 512:(hf + 1) * 512],
                                                                                  op0=ALU.mult, op1=ALU.mult),
                     reads=[bpb[bank], br, bgpost], writes=[bt])
                P.op(DVE, lambda e, t=t, hf=hf: e.tensor_tensor(out=X[s][:, hf * 512:(hf + 1) * 512], in0=X[s][:, hf * 512:(hf + 1) * 512],
                                                                in1=t[:, :], op=ALU.add),
                     reads=[bt, bX[s]], writes=[bX[s]])

        def ffn(pre, post_name, X, bX, early=None, early_pe=None, after_post=None, mid_b=None, tail=None):
            U0 = 8
            for fi in range(11):
                wg, bwg = get_piece((pre, "g", fi))
                wu, bwu = get_piece((pre, "u", fi))
                for c in range(2):
                    f = 2 * fi + c
                    gb_, ub_ = f % 2, 2 + f % 2
                    for (w, bw, bank) in ((wg, bwg, gb_), (wu, bwu, ub_)):
                        for k in range(8):
                            P.op(PE, lambda e, w=w, bank=bank, k=k, c=c: e.matmul(out=pb[bank][:, :], lhsT=w[:, k * 256 + c * 128:k * 256 + c * 128 + 128],
                                                                               rhs=Bv(k), start=(k == 0), stop=(k == 7)),
                                 reads=[bw, bB[k]], writes=[bpb[bank]])
                    t, bt = tf()
                    v, bv = tf()
                    P.op(ACT, lambda e, t=t, gb_=gb_: e.activation(out=t[:, :], in_=pb[gb_][:, :], func=AF.Tanh, scale=0.5),
                         reads=[bpb[gb_]], writes=[bt])
                    P.op(DVE, lambda e, t=t, v=v, gb_=gb_: e.scalar_tensor_tensor(out=v[:, :], in0=t[:, :], scalar=1.0, in1=pb[gb_][:, :],
                                                                                  op0=ALU.add, op1=ALU.mult),
                         reads=[bt, bpb[gb_]], writes=[bv])
                    P.op(DVE, lambda e, v=v, ub_=ub_, f=f: e.tensor_tensor(out=Bv(U0 + f), in0=v[:, :], in1=pb[ub_][:, :], op=ALU.mult),
                         reads=[bv, bpb[ub_]], writes=[bB[U0 + f]])
            load_gpost(post_name, 0.5)
            for ps_ in range(2):
                base = 4 if ps_ == 0 else 0
                for fi in range(11):
                    if ps_ == 0 and fi == 0 and early is not None:
                        early()
                    if ps_ == 0 and fi == 6 and early_pe is not None:
                        early_pe()
                    if ps_ == 1 and fi == 5 and mid_b is not None:
                        mid_b()
                    wd, bwd = get_piece((pre, "d", fi))
                    for c in range(2):
                        f = 2 * fi + c
                        for sl in range(2):
                            s = 2 * ps_ + sl
                            for hf in range(2):
                                bank = base + sl * 2 + hf
                                P.op(PE, lambda e, wd=wd, bank=bank, f=f, s=s, c=c, hf=hf: e.matmul(
                                    out=pb[bank][:, :], lhsT=Bp[:, U0 + f, s * 128:(s + 1) * 128],
                                    rhs=wd[:, c * 1024 + hf * 512:c * 1024 + hf * 512 + 512], start=(f == 0), stop=(f == 21)),
                                    reads=[bwd, bB[U0 + f]], writes=[bpb[bank]])
                for sl in range(2):
                    s = 2 * ps_ + sl
                    post_norm(s, base + sl * 2, base + sl * 2 + 1, 0.5, X, bX)
                    if after_post is not None:
                        after_post(s)
            if tail is not None:
                tail()

        QT0, DAT0, YRT0, MG0, PT0 = 16, 20, 24, 8, 28
        cnt = dict(pt=0, rot=0)


        PTS = [28, 29, 30, 31]

        def att_gen(j):
            pend_f2 = None
            pend_fin = None
            for q_ in (16, 18):
                P.op(POOL, lambda e, q_=q_: e.memset(Bp[64:128, q_, :], 0.0), writes=[bB[q_]])
                P.op(POOL, lambda e, q_=q_: e.memset(Bp[0:64, q_ + 1, :], 0.0), writes=[bB[q_ + 1]])
            for h in range(4):
                wa, bwa = get_piece(("A", h))
                qi = QT0 + (h % 2)
                for part in range(2):
                    for k in range(8):
                        P.op(PE, lambda e, part=part, k=k, wa=wa: e.matmul(out=pb[part][:, :], lhsT=wa[:, k * 256 + part * 128:k * 256 + part * 128 + 128],
                                                                           rhs=Bv(k), start=(k == 0), stop=(k == 7)),
                             reads=[bwa, bB[k]], writes=[bpb[part]])
                qz = (QT0 + 2 * (h % 2), QT0 + 2 * (h % 2) + 1)
                P.op(ACT, lambda e, qz=qz: e.activation(out=Bp[0:64, qz[0], :], in_=pb[0][0:64, :], func=AF.Copy), reads=[bpb[0]], writes=[bB[qz[0]]])
                P.op(DVE, lambda e, qz=qz: e.tensor_copy(out=Bp[64:128, qz[1], :], in_=pb[0][64:128, :]), reads=[bpb[0]], writes=[bB[qz[1]]])
                P.op(DVE, lambda e, h=h: e.tensor_copy(out=Kc[:, h, j * TT:(j + 1) * TT], in_=pb[1][:, :]), reads=[bpb[1]], writes=[bK[j][h]])
                yield
                if pend_f2 is not None:
                    pend_f2()
                    pend_f2 = None
                nkt = 4 * j + 4

                def acc(c, qs, lo, hi):
                    return pb[2 + 2 * c + qs // 2][:, (qs % 2) * 256 + lo:(qs % 2) * 256 + hi]

                info = {}

                def stageA(kt, c, info=info, h=h, qz=qz):
                    cdiag = kt - 4 * j
                    q0 = max(cdiag, 0)
                    ncols = 512 - q0 * 128
                    jk = kt // 4
                    qi = qz[c]
                    P.op(PE, lambda e: e.matmul(out=pb[c][:, 0:ncols], lhsT=Kc[:, h, kt * 128:(kt + 1) * 128],
                                                rhs=Bp[:, qi, q0 * 128:512], start=True, stop=True),
                         reads=[bK[jk][h], bB[qi]], writes=[bpb[c]])
                    pt = PTS[cnt["pt"] % 4]
                    cnt["pt"] += 1
                    P.op(ACT, lambda e: e.activation(out=Bp[:, pt, 0:ncols], in_=pb[c][:, 0:ncols], func=AF.Exp, scale=0.125),
                         reads=[bpb[c]], writes=[bB[pt]])
                    if cdiag >= 0:
                        P.op(POOL, lambda e: e.tensor_tensor(out=Bp[:, pt, 0:128], in0=Bp[:, pt, 0:128], in1=tri, op=ALU.mult),
                             reads=[bB[pt], bcbf], writes=[bB[pt]])
                    info[(kt, c)] = (pt, q0)

                def stageB(kt, c, info=info, h=h):
                    pt, q0 = info[(kt, c)]
                    for qs in range(q0, 4):
                        off = (qs - q0) * 128
                        P.op(PE, lambda e, qs=qs, off=off: e.matmul(
                            out=acc(c, qs, 0, 129), lhsT=Bp[:, pt, off:off + 128], rhs=Vc[:, kt, h, 0:129],
                            start=(kt == 0 and qs % 2 == 0), stop=(kt == 4 * j + qs), skip_group_check=True),
                            reads=[bB[pt], bV[kt]], writes=[bpb[2 + 2 * c + qs // 2]])

                stageA(0, 0)
                stageA(0, 1)
                for kt in range(nkt):
                    for c in range(2):
                        if kt + 1 < nkt:
                            stageA(kt + 1, c)
                        stageB(kt, c)
                    if kt == min(1, nkt - 1) and pend_fin is not None:
                        pend_fin()
                        pend_fin = None
                    yield
                for c in range(2):
                    for qs in range(4):
                        rl, brl = sm()
                        ab = bpb[2 + 2 * c + qs // 2]
                        P.op(DVE, lambda e, rl=rl, qs=qs, c=c: e.reciprocal(out=rl, in_=acc(c, qs, 128, 129)), reads=[ab], writes=[brl])
                        if c == 0:
                            P.op(DVE, lambda e, rl=rl, qs=qs, c=c: e.tensor_scalar(out=yaE[:, qs, :], in0=acc(c, qs, 0, 128), scalar1=rl, scalar2=None, op0=ALU.mult),
                                 reads=[ab, brl], writes=[byaE[qs]])
                        else:
                            cf, bcf = sm()
                            P.op(DVE, lambda e, rl=rl, cf=cf: e.tensor_tensor(out=cf, in0=rl, in1=neglam, op=ALU.mult), reads=[brl, blamv], writes=[bcf])
                            P.op(DVE, lambda e, cf=cf, qs=qs, c=c: e.scalar_tensor_tensor(out=yaE[:, qs, :], in0=acc(c, qs, 0, 128), scalar=cf, in1=yaE[:, qs, :],
                                                                                        op0=ALU.mult, op1=ALU.add),
                                 reads=[ab, bcf, byaE[qs]], writes=[byaE[qs]])

                def f2():
                    jk_, bjk_ = tf()
                    sc = [(sm(), sm(), sm()) for _ in range(4)]
                    for qs in range(4):
                        ss, bss = sc[qs][0]
                        P.op(DVE, lambda e, ss=ss, qs=qs: e.scalar_tensor_tensor(out=jk_[:, qs * 128:(qs + 1) * 128], in0=yaE[:, qs, :], scalar=1.0,
                                                                                in1=yaE[:, qs, :], op0=ALU.mult, op1=ALU.mult, accum_out=ss),
                             reads=[byaE[qs]], writes=[bjk_, bss])
                    for qs in range(4):
                        (ss, bss), (se, bse), _ = sc[qs]
                        P.op(DVE, lambda e, se=se, ss=ss: e.tensor_scalar(out=se, in0=ss, scalar1=float(SUBLN_EPS * 128), scalar2=None, op0=ALU.add),
                             reads=[bss], writes=[bse])
                    for qs in range(4):
                        _, (se, bse), (r, br) = sc[qs]
                        rsqrt_pool(r, br, se, bse)
                    for qs in range(4):
                        r, br = sc[qs][2]
                        P.op(DVE, lambda e, r=r, qs=qs: e.scalar_tensor_tensor(out=datok[:, qs, :], in0=yaE[:, qs, :], scalar=r, in1=gsub[:, :],
                                                                              op0=ALU.mult, op1=ALU.mult),
                             reads=[byaE[qs], br, bgsub], writes=[bdatok[qs]])

                def fin(h=h):
                    for qs in range(4):
                        P.op(PE, lambda e, qs=qs: e.transpose(out=pbh[1][:, qs * 128:(qs + 1) * 128], in_=datok[:, qs, :], identity=ident),
                             reads=[bdatok[qs], bcbf], writes=[bpb[1]])
                    P.op(ACT, lambda e: e.activation(out=Bv(DAT0 + h), in_=pbh[1][:, 0:512], func=AF.Copy), reads=[bpb[1]], writes=[bB[DAT0 + h]])
                pend_f2, pend_fin = f2, fin
                yield
            if pend_f2 is not None:
                pend_f2()
            if pend_fin is not None:
                pend_fin()

        def ret_gen(j, heads, p):
            pj = mb = 6 + p
            rs = rsm[p]
            brs = brsm[p]
            g1 = g1t[p]
            rt, brt = rott[p], brott[p]
            bA, bBk = bpbr[p]["A"], bpbr[p]["B"]
            bq = bpbr[p]
            qkrot, qkT = rs[:, 0:256], rs[:, 256:512]
            vtok, vz, sTm, yr = rs[:, 512:640], rs[:, 640:768], rs[:, 768:896], rs[:, 896:1024]
            for h in heads:
                wr, bwr = get_piece(("R", h))
                wrv, bwrv = get_piece(("RV", h))
                cd = float(RET_GAMMA[h] ** 128)
                hs = slice(h * 128, (h + 1) * 128)
                for s in range(NSUB):
                    tok = j * TT + s * 128
                    P.dma(POOL, rt[:, :], rot_d[h * T + tok:h * T + tok + 128, :], brt, writes=[brt])
                    for (w, bw, lo, bb) in ((wr, bwr, 0, bA), (wrv, bwrv, 256, bBk)):
                        for k in range(8):
                            P.op(PE, lambda e, w=w, lo=lo, k=k, s=s: e.matmul(
                                out=pb[pj][:, lo:lo + 256], lhsT=Bp[:, k, s * 128:(s + 1) * 128], rhs=w[:, k * 256:(k + 1) * 256],
                                start=(k == 0), stop=(k == 7)), reads=[bw, bB[k]], writes=[bb])
                    t1, bt1 = tf()
                    t2, bt2 = tf()
                    P.op(DVE, lambda e, t1=t1: e.tensor_tensor(out=t1[:, 0:256], in0=pb[pj][:, 0:256], in1=rt[:, 0:256], op=ALU.mult),
                         reads=[bA, brt], writes=[bt1])
                    P.op(DVE, lambda e, t2=t2: e.tensor_tensor(out=t2[:, 0:256:2], in0=pb[pj][:, 1:256:2], in1=rt[:, 256:512:2], op=ALU.mult),
                         reads=[bA, brt], writes=[bt2])
                    P.op(DVE, lambda e, t2=t2: e.tensor_tensor(out=t2[:, 1:256:2], in0=pb[pj][:, 0:256:2], in1=rt[:, 257:512:2], op=ALU.mult),
                         reads=[bA, brt], writes=[bt2])
                    P.op(DVE, lambda e, t1=t1, t2=t2: e.tensor_tensor(out=qkrot, in0=t1[:, 0:256], in1=t2[:, 0:256], op=ALU.add),
                         reads=[bt1, bt2], writes=[brs["qkrot"]])
                    P.op(DVE, lambda e: e.tensor_copy(out=vtok, in_=pb[pj][:, 256:384]), reads=[bBk], writes=[brs["vtok"]])
                    tg, btg = tf()
                    P.op(ACT, lambda e, tg=tg: e.activation(out=tg[:, 0:128], in_=pb[pj][:, 384:512], func=AF.Tanh, scale=0.5),
                         reads=[bBk], writes=[btg])
                    P.op(DVE, lambda e, tg=tg: e.scalar_tensor_tensor(out=g1[:, :], in0=tg[:, 0:128], scalar=1.0, in1=pb[pj][:, 384:512],
                                                                      op0=ALU.add, op1=ALU.mult),
                         reads=[btg, bBk], writes=[brs["g1"]])
                    yield
                    for i in range(2):
                        P.op(PE, lambda e, i=i: e.transpose(out=pbh[pj][:, i * 128:(i + 1) * 128], in_=rs[:, i * 128:(i + 1) * 128], identity=ident),
                             reads=[brs["qkrot"], bcbf], writes=[bA])
                    P.op(DVE, lambda e: e.tensor_copy(out=qkT, in_=pbh[pj][:, 0:256]), reads=[bA], writes=[brs["qkT"]])
                    yield
                    P.op(PE, lambda e: e.matmul(out=pb[mb][:, 256:384], lhsT=rs[:, 384:512], rhs=rs[:, 256:384], start=True, stop=True),
                         reads=[brs["qkT"]], writes=[bq["S"]])
                    P.op(DVE, lambda e: e.tensor_tensor(out=sTm, in0=pb[mb][:, 256:384], in1=cst[:, C_TRI:C_TRI + 128], op=ALU.mult),
                         reads=[bq["S"], bcst], writes=[brs["sTm"]])
                    yield
                    P.op(PE, lambda e: e.matmul(out=pb[mb][:, 0:128], lhsT=sTm, rhs=vtok, start=True, stop=False),
                         reads=[brs["sTm"], brs["vtok"]], writes=[bq["I"]])
                    P.op(PE, lambda e, hs=hs: e.matmul(out=pb[mb][:, 0:128], lhsT=rs[:, 256:384], rhs=Sbf[:, hs], start=False, stop=True),
                         reads=[brs["qkT"], bSbf[h]], writes=[bq["I"]])
                    P.op(PE, lambda e: e.matmul(out=pb[mb][:, 384:512], lhsT=rs[:, 128:256], rhs=vtok, start=True, stop=True),
                         reads=[brs["qkrot"], brs["vtok"]], writes=[bq["KV"]])
                    P.op(DVE, lambda e, hs=hs, cd=cd: e.scalar_tensor_tensor(out=Sst[:, hs], in0=Sst[:, hs], scalar=cd, in1=pb[mb][:, 384:512],
                                                                           op0=ALU.mult, op1=ALU.add),
                         reads=[bS[h], bq["KV"]], writes=[bS[h]])
                    P.op(POOL, lambda e, hs=hs, cd=cd: e.tensor_scalar(out=Sbf[:, hs], in0=Sst[:, hs], scalar1=cd, scalar2=None, op0=ALU.mult),
                         reads=[bS[h]], writes=[bSbf[h]])
                    jk2, bjk2 = tf()
                    ss, bss = sm()
                    se, bse = sm()
                    r2, br2 = sm()
                    cn = 0.5 * math.sqrt(128.0)
                    P.op(ACT, lambda e, jk2=jk2, ss=ss: e.activation(out=jk2[:, 0:128], in_=pb[mb][:, 0:128], func=AF.Square, scale=float(1.0 / cn), accum_out=ss),
                         reads=[bq["I"]], writes=[bjk2, bss])
                    P.op(DVE, lambda e, se=se, ss=ss: e.tensor_scalar(out=se, in0=ss, scalar1=float(NORM_EPS * 128 / (cn * cn)), scalar2=None, op0=ALU.add),
                         reads=[bss], writes=[bse])
                    rsqrt_pool(r2, br2, se, bse)
                    P.op(DVE, lambda e, r2=r2: e.scalar_tensor_tensor(out=yr, in0=pb[mb][:, 0:128], scalar=r2, in1=g1[:, :], op0=ALU.mult, op1=ALU.mult),
                         reads=[bq["I"], br2, brs["g1"]], writes=[brs["yr"]])
                    yield
                    yield
                    P.op(PE, lambda e: e.transpose(out=pbh[pj][:, 512:640], in_=yr, identity=ident), reads=[brs["yr"], bcbf], writes=[bBk])
                    P.op(DVE, lambda e, h=h, s=s: e.tensor_copy(out=Bp[:, YRT0 + h, s * 128:(s + 1) * 128], in_=pbh[pj][:, 512:640]),
                         reads=[bBk], writes=[bB[YRT0 + h]])
                    yield

        def mixer(j, X, bX, pre_tail=None):
            wv = [get_piece(("V", i)) for i in range(2)]

            def vproj(s):
                bank = s
                for i in range(2):
                    for k in range(8):
                        P.op(PE, lambda e, bank=bank, i=i, k=k, s=s: e.matmul(
                            out=pb[bank][:, i * 256:(i + 1) * 256], lhsT=Bp[:, k, s * 128:(s + 1) * 128],
                            rhs=wv[i][0][:, k * 256:(k + 1) * 256], start=(k == 0), stop=(k == 7)),
                            reads=[wv[i][1], bB[k]], writes=[bpb[bank]])
                kt = 4 * j + s
                P.op(ACT, lambda e, bank=bank, kt=kt: e.activation(out=Vc[:, kt, :, 0:128],
                                                                   in_=pb[bank][:, :].rearrange("p (h e) -> p h e", h=4), func=AF.Copy),
                     reads=[bpb[bank]], writes=[bV[kt]])

            gens = [ret_gen(j, (0, 2), 0), ret_gen(j, (1, 3), 1)]
            vproj(0)
            vproj(1)
            for g_ in gens:
                next(g_)
            if pre_tail is not None:
                pre_tail()
            vproj(2)
            vproj(3)

            n_att = 4 * (4 * j + 6)
            ratio = max(1, int(n_att / 48.0 + 0.5))
            ga_ = att_gen(j)
            att_live = True
            while gens or att_live:
                for _ in range(ratio if gens else 1):
                    if att_live:
                        try:
                            next(ga_)
                        except StopIteration:
                            att_live = False
                for g_ in list(gens):
                    try:
                        next(g_)
                    except StopIteration:
                        gens.remove(g_)
                if conv_gen[0] is not None:
                    next(conv_gen[0], None)

            load_gpost("mix_post_g", 1.0)
            gate_w = {}
            tq = {}

            def stage1(g):
                if g % 2 == 0:
                    gate_w["ga"] = get_piece(("GA", g // 2))
                    gate_w["gr"] = get_piece(("GR", g // 2))
                par = (g % 2) * 4
                cc = g % 2
                ts_ = []
                for (w, bank) in ((gate_w["ga"], par + 2), (gate_w["gr"], par + 3)):
                    for k in range(8):
                        P.op(PE, lambda e, w=w, bank=bank, k=k, cc=cc: e.matmul(
                            out=pb[bank][:, :], lhsT=w[0][:, k * 256 + cc * 128:k * 256 + cc * 128 + 128], rhs=Bv(k),
                            start=(k == 0), stop=(k == 7)), reads=[w[1], bB[k]], writes=[bpb[bank]])
                    t, bt = tf()
                    P.op(ACT, lambda e, t=t, bank=bank: e.activation(out=t[:, :], in_=pb[bank][:, :], func=AF.Tanh, scale=0.5),
                         reads=[bpb[bank]], writes=[bt])
                    ts_.append((t, bt))
                tq[g] = ts_

            proj_w = {}

            def stage2(g):
                if g % 4 == 0:
                    proj_w["ap"] = get_piece(("AP", g // 4))
                    proj_w["rp"] = get_piece(("RP", g // 4))
                par = (g % 2) * 4
                cg = g % 4
                for (w, bank, src0) in ((proj_w["ap"], par + 0, DAT0), (proj_w["rp"], par + 1, YRT0)):
                    for c4 in range(4):
                        P.op(PE, lambda e, w=w, bank=bank, src0=src0, c4=c4, cg=cg: e.matmul(
                            out=pb[bank][:, :], lhsT=w[0][:, c4 * 512 + cg * 128:c4 * 512 + cg * 128 + 128], rhs=Bv(src0 + c4),
                            start=(c4 == 0), stop=(c4 == 3)), reads=[w[1], bB[src0 + c4]], writes=[bpb[bank]])
                ts_ = tq.pop(g)
                for (t, bt), ybank in zip(ts_, (par + 0, par + 1)):
                    P.op(DVE, lambda e, t=t, ybank=ybank: e.scalar_tensor_tensor(out=t[:, :], in0=t[:, :], scalar=1.0, in1=pb[ybank][:, :],
                                                                              op0=ALU.add, op1=ALU.mult),
                         reads=[bt, bpb[ybank]], writes=[bt])
                P.op(DVE, lambda e, ts_=ts_, g=g: e.tensor_tensor(out=Bv(MG0 + g), in0=ts_[0][0][:, :], in1=ts_[1][0][:, :], op=ALU.add),
                     reads=[ts_[0][1], ts_[1][1]], writes=[bB[MG0 + g]])

            stage1(0)
            for g in range(8):
                if g + 1 < 8:
                    stage1(g + 1)
                stage2(g)
            wo = [get_piece(("WO", i)) for i in range(4)]
            for s in range(NSUB):
                bA, bB2 = (s % 2) * 2, (s % 2) * 2 + 1
                for i in range(4):
                    bank = bA if i < 2 else bB2
                    col = (i % 2) * 256
                    for k in range(8):
                        P.op(PE, lambda e, bank=bank, col=col, k=k, s=s, i=i: e.matmul(
                            out=pb[bank][:, col:col + 256], lhsT=Bp[:, MG0 + k, s * 128:(s + 1) * 128], rhs=wo[i][0][:, k * 256:(k + 1) * 256],
                            start=(k == 0), stop=(k == 7)), reads=[wo[i][1], bB[MG0 + k]], writes=[bpb[bank]])
                if s >= 1:
                    pn_pe(2, s - 1, 4 + (s - 1) % 2)
                post_norm(s, bA, bB2, 0.5, X, bX)
                pn_elem(s, X, bX)
            pn_pe(2, NSUB - 1, 4 + (NSUB - 1) % 2)

        stores = []
        conv_gen = [None]

        def load_x(j, X, bX):
            for s in range(NSUB):
                r0 = (j * NSUB + s) * 128
                P.dma(POOL, X[s][:, :], x_d[r0:r0 + 128, :], bX[s], writes=[bX[s]])

        load_x(0, XB[0], bXB[0])
        for s in range(NSUB):
            pn_elem(s, XB[0], bXB[0])
            pn_pe(0, s, 6 + s % 2)
        for j in range(NTILES):
            X, bX = XB[j % 2], bXB[j % 2]
            Xn, bXn = XB[(j + 1) % 2], bXB[(j + 1) % 2]
            last = (j == NTILES - 1)

            def f1_after(s, X=X, bX=bX):
                pn_elem(s, X, bX)

            def f1_mid():
                pn_pe(1, 0, 4)
                pn_pe(1, 1, 6)

            def f1_tail():
                pn_pe(1, 2, 5)
                pn_pe(1, 3, 7)

            def f1_early():
                conv_gen[0] = convert_rest()
                for _ in range(14):
                    next(conv_gen[0], None)

            ffn("ffn1", "ffn1_post_g", X, bX, early=f1_early if j == 0 else None, after_post=f1_after, mid_b=f1_mid)
            if debug:
                for s in range(NSUB):
                    r0 = (j * NSUB + s) * 128
                    stores.append(P.dma(POOL, dbg_d["dbg1"][r0:r0 + 128, :], X[s][:, :], bX[s], reads=[bX[s]]))
            mixer(j, X, bX, pre_tail=f1_tail)
            if conv_gen[0] is not None:
                for _ in conv_gen[0]:
                    pass
                conv_gen[0] = None
            if debug:
                for s in range(NSUB):
                    r0 = (j * NSUB + s) * 128
                    stores.append(P.dma(POOL, dbg_d["dbg2"][r0:r0 + 128, :], X[s][:, :], bX[s], reads=[bX[s]]))
            overlap = (not last) and j >= 1
            if overlap:
                load_x(j + 1, Xn, bXn)

            def f2_early(Xn=Xn, bXn=bXn):
                for s in range(NSUB):
                    pn_elem(s, Xn, bXn)

            def f2_early_pe():
                for s in range(NSUB):
                    pn_pe(0, s, s)

            def f2_after(s, j=j, X=X, bX=bX):
                r0 = (j * NSUB + s) * 128
                stores.append(P.dma(POOL, out_d[r0:r0 + 128, :], X[s][:, :], bX[s], reads=[bX[s]]))

            ffn("ffn2", "ffn2_post_g", X, bX, early=f2_early if overlap else None, early_pe=f2_early_pe if overlap else None,
                after_post=f2_after)
            if (not last) and not overlap:
                load_x(j + 1, Xn, bXn)
                for s in range(NSUB):
                    pn_elem(s, Xn, bXn)
                    pn_pe(0, s, 6 + s % 2)
        if debug:
            print("SBUF bytes/partition", sbuf_bytes[0])
        P.final_wait(POOL, stores)
        P.emit()
    return nc


W2D = ("ffn1_w_gate", "ffn1_w_up", "ffn1_w_down", "w_in", "w_attn_proj", "w_ret_proj", "w_out",
       "ffn2_w_gate", "ffn2_w_up", "ffn2_w_down")
VECS = ("ffn1_pre_g", "ffn1_post_g", "mix_pre_g", "mix_post_g", "ffn2_pre_g", "ffn2_post_g",
        "lambda_q1", "lambda_k1", "lambda_q2", "lambda_k2", "diff_subln_g")


def make_in_maps(inputs, T, ncores, debug=False):
    cst, rot = _const_tables(T)
    shared = {"cst": cst, "rot": rot}
    for n in W2D:
        shared[n] = np.ascontiguousarray(np.asarray(inputs[n], dtype=np.float32)[0])
    for n in VECS:
        shared[n] = np.ascontiguousarray(np.asarray(inputs[n], dtype=np.float32).reshape(1, -1))
    x = np.asarray(inputs["x"], dtype=np.float32)
    maps = []
    for c in range(ncores):
        m = dict(shared)
        m["x"] = np.ascontiguousarray(x[c, :T])
        maps.append(m)
    return maps


def kernel(**inputs):
    nc = build(SEQ)
    in_maps = make_in_maps(inputs, SEQ, NCORES)
    res = run_bass_kernel_spmd(nc, in_maps, core_ids=list(range(NCORES)))
    out = np.stack([np.asarray(res.results[c]["out"]) for c in range(NCORES)], axis=0)
    return out.astype(np.float32, copy=False)
```

```python
import math
from contextlib import ExitStack

import numpy as np
import concourse.bass as bass
import concourse.mybir as mybir
from concourse.bass_utils import run_bass_kernel_spmd

F32 = mybir.dt.float32
BF16 = mybir.dt.bfloat16
AF = mybir.ActivationFunctionType
ALU = mybir.AluOpType
AX = mybir.AxisListType

D = 1024
DFF = 2816
SEQ = 4096
NCORES = 8
TT = 512
NSUB = 4
LAMBDA_INIT = 0.8 - 0.6 * math.exp(-0.3 * 0)
NORM_EPS = 1e-6
SUBLN_EPS = 1e-5
RET_GAMMA = [1.0 - 2.0 ** (-5.0 - h) for h in range(4)]

PE, ACT, DVE, POOL, SP = "tensor", "scalar", "vector", "gpsimd", "sync"
ENGS = (PE, ACT, DVE, POOL, SP)


class Buf:
    __slots__ = ("name", "w", "r", "sem", "dcnt", "excl")

    def __init__(self, name, excl=False):
        self.name = name
        self.excl = excl
        self.w = None
        self.r = []
        self.sem = None
        self.dcnt = 0


class Op:
    __slots__ = ("eng", "fn", "waits", "need_inc", "is_dma", "sem", "val")

    def __init__(self, eng, fn, is_dma=False):
        self.eng = eng
        self.fn = fn
        self.waits = []
        self.need_inc = False
        self.is_dma = is_dma
        self.sem = None
        self.val = None


class Prog:
    def __init__(self, nc, stack, n_dma_sems=48):
        self.nc = nc
        self.ops = {e: [] for e in ENGS}
        self.esem = {e: stack.enter_context(nc.semaphore("es_" + e)) for e in ENGS}
        self.free_sems = [stack.enter_context(nc.semaphore("ds%d" % i)) for i in range(n_dma_sems)]

    def _dep(self, op, prev, same_ok):
        if prev is None or prev is op:
            return
        if (not prev.is_dma) and (not op.is_dma) and prev.eng == op.eng == PE:
            return
        if not prev.is_dma:
            prev.need_inc = True
        op.waits.append(prev)

    def _track(self, op, reads, writes):
        ex = [b for b in reads if b.excl]
        if ex:
            reads = [b for b in reads if not b.excl]
            writes = list(writes) + [b for b in ex if b not in writes]
        for b in reads:
            self._dep(op, b.w, same_ok=(op.eng == PE))
        for b in writes:
            self._dep(op, b.w, same_ok=True)
            for r in b.r:
                self._dep(op, r, same_ok=True)
        for b in reads:
            b.r.append(op)
        for b in writes:
            b.w = op
            b.r = []

    def op(self, eng, fn, reads=(), writes=()):
        o = Op(eng, fn)
        self._track(o, reads, writes)
        self.ops[eng].append(o)
        return o

    def dma(self, queue, out_ap, in_ap, sembuf, reads=(), writes=(), **kw):
        if sembuf.sem is None:
            sembuf.sem = {}
            sembuf.dcnt = {}
        if queue not in sembuf.sem:
            sembuf.sem[queue] = self.free_sems.pop()
            sembuf.dcnt[queue] = 0
        sembuf.dcnt[queue] += 1
        o = Op(queue, lambda eng: eng.dma_start(out=out_ap, in_=in_ap, **kw), is_dma=True)
        o.sem = sembuf.sem[queue]
        o.val = 16 * sembuf.dcnt[queue]
        self._track(o, reads, writes)
        self.ops[queue].append(o)
        return o

    def final_wait(self, eng, ops):
        o = Op(eng, None)
        for p in ops:
            self._dep(o, p, same_ok=False)
        self.ops[eng].append(o)

    def emit(self):
        nc = self.nc
        for e in ENGS:
            cnt = 0
            for o in self.ops[e]:
                if (not o.is_dma) and o.need_inc:
                    cnt += 1
                    o.val = cnt
                    o.sem = self.esem[e]
        with nc.Block() as block:
            for e in ENGS:
                def body(eng, e=e):
                    seen = {}
                    for o in self.ops[e]:
                        for w in o.waits:
                            k = id(w.sem)
                            if seen.get(k, 0) >= w.val:
                                continue
                            eng.wait_ge(w.sem, w.val)
                            seen[k] = w.val
                        if o.fn is None:
                            continue
                        ins = o.fn(eng)
                        if o.is_dma:
                            ins.then_inc(o.sem, 16)
                        elif o.need_inc:
                            ins.then_inc(o.sem, 1)
                getattr(block, e)(body)


C_ID, C_TRI, C_END = 0, 128, 256


def _const_tables(T):
    cst = np.zeros((128, C_END), np.float32)
    idx = np.arange(128)
    cst[:, C_ID:C_ID + 128] = np.eye(128, dtype=np.float32)
    cst[:, C_TRI:C_TRI + 128] = (idx[None, :] >= idx[:, None]).astype(np.float32)
    angle = (1.0 / (10000.0 ** np.linspace(0.0, 1.0, 64, dtype=np.float32))).astype(np.float32)
    angle = np.repeat(angle, 2)
    ang = (np.arange(T, dtype=np.float32)[:, None] * angle[None, :]).astype(np.float32)
    sn, cs = np.sin(ang).astype(np.float32), np.cos(ang).astype(np.float32)
    sgn = np.where(np.arange(128) % 2 == 0, -1.0, 1.0).astype(np.float32)
    ks = np.float32(128.0 ** -0.5)
    tl = (np.arange(T) % 128).astype(np.float64)
    rots = []
    for h in range(4):
        lg64 = np.log(np.float64(RET_GAMMA[h]))
        qa = np.exp(lg64 * (tl + 1.0)).astype(np.float32)[:, None]
        kb = (np.exp(-lg64 * (tl + 1.0)) * np.float64(ks)).astype(np.float32)[:, None]
        rots.append(np.concatenate([cs * qa, cs * kb, sn * sgn * qa, sn * sgn * kb], axis=1))
    rot = np.concatenate(rots, axis=0).astype(np.float32)
    return cst, rot


USE_DMA_CAST = True
NS = 8
NTMP = 8
NB = 32


def build(T, debug=False):
    NTILES = T // TT
    NKT = T // 128
    nc = bass.Bass("TRN2", target_bir_lowering=False)

    def din(name, shape):
        return nc.dram_tensor(name, shape, F32, kind="ExternalInput").ap()

    x_d = din("x", [T, D])
    out_d = nc.dram_tensor("out", [T, D], F32, kind="ExternalOutput").ap()
    Wd_ = {}
    for pre in ("ffn1", "ffn2"):
        Wd_[pre + "_w_gate"] = din(pre + "_w_gate", [D, DFF])
        Wd_[pre + "_w_up"] = din(pre + "_w_up", [D, DFF])
        Wd_[pre + "_w_down"] = din(pre + "_w_down", [DFF, D])
    Wd_["w_in"] = din("w_in", [D, 5632])
    Wd_["w_attn_proj"] = din("w_attn_proj", [512, D])
    Wd_["w_ret_proj"] = din("w_ret_proj", [512, D])
    Wd_["w_out"] = din("w_out", [D, D])
    gains = {n: din(n, [1, D]) for n in ("ffn1_pre_g", "ffn1_post_g", "mix_pre_g", "mix_post_g", "ffn2_pre_g", "ffn2_post_g")}
    lam_d = {n: din(n, [1, 64]) for n in ("lambda_q1", "lambda_k1", "lambda_q2", "lambda_k2")}
    subg_d = din("diff_subln_g", [1, 128])
    cst_d = din("cst", [128, C_END])
    rot_d = din("rot", [4 * T, 512])
    dbg_d = {}
    if debug:
        for n in ("dbg1", "dbg2"):
            dbg_d[n] = nc.dram_tensor(n, [T, D], F32, kind="ExternalOutput").ap()

    pieces = {}

    def addp(key, parts):
        pieces[key] = dict(parts=parts, idx=len(pieces))

    for pre in ("ffn1", "ffn2"):
        for i in range(11):
            addp((pre, "g", i), [(Wd_[pre + "_w_gate"], 0, 8, 256 * i, 256)])
            addp((pre, "u", i), [(Wd_[pre + "_w_up"], 0, 8, 256 * i, 256)])
            addp((pre, "d", i), [(Wd_[pre + "_w_down"], 2 * i, 2, 0, 1024)])
    win = Wd_["w_in"]
    for h in range(4):
        addp(("A", h), [(win, 0, 8, 128 * h, 128), (win, 0, 8, 512 + 128 * h, 128)])
        addp(("R", h), [(win, 0, 8, 1536 + 128 * h, 128), (win, 0, 8, 2048 + 128 * h, 128)])
        addp(("RV", h), [(win, 0, 8, 2560 + 128 * h, 128), (win, 0, 8, 3072 + 128 * h, 128)])
    for i in range(2):
        addp(("V", i), [(win, 0, 8, 1024 + 256 * i, 256)])
        addp(("AP", i), [(Wd_["w_attn_proj"], 0, 4, 512 * i, 512)])
        addp(("RP", i), [(Wd_["w_ret_proj"], 0, 4, 512 * i, 512)])
    for i in range(4):
        addp(("GA", i), [(win, 0, 8, 3584 + 256 * i, 256)])
        addp(("GR", i), [(win, 0, 8, 4608 + 256 * i, 256)])
        addp(("WO", i), [(Wd_["w_out"], 0, 8, 256 * i, 256)])
    NP = len(pieces)
    scr_d = nc.dram_tensor("wscr", [NP, 128, 2048], BF16).ap()

    with ExitStack() as st:
        P = Prog(nc, st, n_dma_sems=92)

        sbuf_bytes = [0]

        def sb(name, shape, dt):
            n = 1
            for d_ in shape[1:]:
                n *= d_
            sbuf_bytes[0] += n * (4 if dt == F32 else 2)
            return st.enter_context(nc.sbuf_tensor("sb_" + name, shape, dt))

        Kc = sb("Kc", [128, 4, T], BF16)
        bK = [[Buf("K%d_%d" % (j, h)) for h in range(4)] for j in range(NTILES)]
        Vc = sb("Vc", [128, NKT, 4, 130], BF16)
        bV = [Buf("V%d" % k) for k in range(NKT)]
        xs_ = [sb("x%d" % s, [128, D], F32) for s in range(NSUB)]
        bx = [Buf("x%d" % s) for s in range(NSUB)]
        Bp = sb("Bp", [128, NB, 512], BF16)
        bB = [Buf("B%d" % i) for i in range(NB)]
        slots = [sb("slot%d" % i, [128, 2048], BF16) for i in range(NS)]
        bslot = [Buf("slot%d" % i) for i in range(NS)]
        NSTG = 4
        stage = [sb("stg%d" % i, [128, 1024], F32) for i in range(NSTG)]
        bstage = [Buf("stg%d" % i) for i in range(NSTG)]
        gpost = sb("gpost", [128, D], F32)
        bgpost = Buf("gpost")
        tmpf = [sb("tf%d" % i, [128, 512], F32) for i in range(NTMP)]
        btmpf = [Buf("tf%d" % i) for i in range(NTMP)]
        cst = sb("cst", [128, C_END], F32)
        bcst = Buf("cst")
        cbf = sb("cbf", [128, 256], BF16)
        bcbf = Buf("cbf")
        rott = [sb("rot%d" % i, [128, 512], F32) for i in range(2)]
        brott = [Buf("rot%d" % i) for i in range(2)]
        gT = sb("gT", [128, 24], F32)
        bgT = Buf("gT")
        Sst = sb("Sst", [128, 512], F32)
        Sbf = sb("Sbf", [128, 512], BF16)
        bS = [Buf("S%d" % h) for h in range(4)]
        bSbf = [Buf("Sbf%d" % h) for h in range(4)]
        small = sb("small", [128, 64], F32)
        bsmall = [Buf("sm%d" % i) for i in range(64)]
        lamt = sb("lamt", [128, 4, 64], F32)
        blamt = Buf("lamt")
        lamv = sb("lamv", [128, 8], F32)
        blamv = Buf("lamv")
        mhalf = sb("mhalf", [128, 4], F32)
        bmhalf = Buf("mhalf")
        gsub = sb("gsub", [128, 128], F32)
        bgsub = Buf("gsub")
        yaE = sb("yaE", [128, 4, 128], F32)
        byaE = [Buf("yaE%d" % i) for i in range(4)]
        datok = sb("datok", [128, 4, 128], BF16)
        bdatok = [Buf("datok%d" % i) for i in range(4)]
        rsm = [sb("rsm%d" % i, [128, 1024], BF16) for i in range(2)]
        brsm = [{n: Buf("rsm%d_%s" % (i, n)) for n in ("qkrot", "qkT", "vtok", "vz", "sTm", "yr", "g1")} for i in range(2)]
        g1t = [sb("g1t%d" % i, [128, 128], F32) for i in range(2)]

        pb = [st.enter_context(nc.psum_tensor("pb%d" % i, [128, 512], F32)) for i in range(8)]
        bpb = [Buf("pb%d" % i, excl=True) for i in range(8)]
        pbh = [p.bitcast(BF16) for p in pb]
        bpbr = [{n: bpb[6 + i] for n in ("A", "B", "S", "KV", "I", "E")} for i in range(2)]

        ident = cbf[:, 0:128]
        tri = cbf[:, 128:256]

        state = dict(tf=0, sm=0, slot=0, stg=0, cast=0, inscr=set())

        def tf():
            i = state["tf"]
            state["tf"] = (i + 1) % NTMP
            return tmpf[i], btmpf[i]

        def sm():
            i = state["sm"]
            state["sm"] = (i + 1) % 64
            return small[:, i:i + 1], bsmall[i]

        def Bv(i):
            return Bp[:, i, :]

        def get_piece(key):
            pc = pieces[key]
            pi = pc["idx"]
            si = state["slot"]
            state["slot"] = (si + 1) % NS
            sl, bsl = slots[si], bslot[si]
            use_dc = USE_DMA_CAST and key[0] in ("ffn1", "ffn2")
            if pi not in state["inscr"] and use_dc:
                parts = pc["parts"]
                nrc = parts[0][2]
                ctot = sum(p[4] for p in parts)
                sv = sl[:, :].rearrange("p (k c) -> p k c", k=nrc)
                co = 0
                for (W, r0, _, c0, ncol) in parts:
                    src = W[r0 * 128:(r0 + nrc) * 128, c0:c0 + ncol].rearrange("(k p) c -> p k c", p=128)
                    P.dma(POOL, sv[:, :, co:co + ncol], src, bsl, writes=[bsl])
                    co += ncol
                bscr = Buf("scr%d" % pi)
                pc["bscr"] = bscr
                P.dma(SP, scr_d[pi], sl[:, :], bsl, reads=[bsl], writes=[bscr])
                state["inscr"].add(pi)
            elif pi not in state["inscr"]:
                parts = pc["parts"]
                nrc = parts[0][2]
                hr = nrc // 2
                ctot = sum(p[4] for p in parts)
                for hf in range(2):
                    g = state["stg"]
                    state["stg"] = (g + 1) % NSTG
                    ce = 1 + state["cast"]
                    state["cast"] = (state["cast"] + 1) % 2
                    sv = stage[g][:, :].rearrange("p (k c) -> p k c", k=hr)
                    co = 0
                    for (W, r0, _, c0, ncol) in parts:
                        rs = (r0 + hf * hr) * 128
                        src = W[rs:rs + hr * 128, c0:c0 + ncol].rearrange("(k p) c -> p k c", p=128)
                        P.dma(SP, sv[:, :, co:co + ncol], src, bstage[g], writes=[bstage[g]])
                        co += ncol
                    if ce == 0:
                        P.op(POOL, lambda e, sl=sl, hf=hf, g=g: e.tensor_copy(out=sl[:, hf * 1024:(hf + 1) * 1024], in_=stage[g][:, :]),
                             reads=[bstage[g]], writes=[bsl])
                    elif ce == 1:
                        P.op(ACT, lambda e, sl=sl, hf=hf, g=g: e.activation(out=sl[:, hf * 1024:(hf + 1) * 1024], in_=stage[g][:, :], func=AF.Copy),
                             reads=[bstage[g]], writes=[bsl])
                    else:
                        P.op(DVE, lambda e, sl=sl, hf=hf, g=g: e.tensor_copy(out=sl[:, hf * 1024:(hf + 1) * 1024], in_=stage[g][:, :]),
                             reads=[bstage[g]], writes=[bsl])
                bscr = Buf("scr%d" % pi)
                pc["bscr"] = bscr
                P.dma(POOL, scr_d[pi], sl[:, :], bsl, reads=[bsl], writes=[bscr])
                state["inscr"].add(pi)
            else:
                P.dma(SP, sl[:, :], scr_d[pi], bsl, reads=[pc["bscr"]], writes=[bsl])
            return sl, bsl

        def convert_rest():
            order = [("V", 0), ("V", 1), ("R", 0), ("RV", 0), ("R", 1), ("RV", 1), ("A", 0), ("A", 1), ("R", 2), ("RV", 2),
                     ("R", 3), ("RV", 3), ("A", 2), ("A", 3), ("AP", 0), ("RP", 0), ("GA", 0), ("GR", 0), ("GA", 1), ("GR", 1),
                     ("AP", 1), ("RP", 1), ("GA", 2), ("GR", 2), ("GA", 3), ("GR", 3)] + [("WO", i) for i in range(4)]
            for i in range(11):
                order += [("ffn2", "g", i), ("ffn2", "u", i)]
            order += [("ffn2", "d", i) for i in range(11)]
            convbufs = [Buf("conv%d" % i) for i in range(44)]
            ci = 0
            for key in order:
                pc = pieces[key]
                pi = pc["idx"]
                if pi in state["inscr"]:
                    continue
                cb = convbufs[ci % len(convbufs)]
                ci += 1
                parts = pc["parts"]
                nrc = parts[0][2]
                dv = scr_d[pi].rearrange("p (k c) -> p k c", k=nrc)
                bscr = Buf("scr%d" % pi)
                pc["bscr"] = bscr
                co = 0
                for (W, r0, _, c0, ncol) in parts:
                    src = W[r0 * 128:(r0 + nrc) * 128, c0:c0 + ncol].rearrange("(k p) c -> p k c", p=128)
                    P.dma(POOL, dv[:, :, co:co + ncol], src, cb, writes=[bscr, cb])
                    co += ncol
                state["inscr"].add(pi)
                yield

        P.dma(POOL, cst[:, :], cst_d, bcst, writes=[bcst])
        P.op(POOL, lambda e: e.tensor_copy(out=cbf[:, :], in_=cst[:, 0:256]), reads=[bcst], writes=[bcbf])
        for gi, n in enumerate(("ffn1_pre_g", "mix_pre_g", "ffn2_pre_g")):
            P.dma(POOL, gT[:, gi * 8:(gi + 1) * 8], gains[n].rearrange("o (k p) -> p (o k)", p=128), bgT, writes=[bgT],
                  allow_slow_non_contiguous=True)
        P.op(POOL, lambda e: e.tensor_scalar(out=gT[:, :], in0=gT[:, :], scalar1=float(math.sqrt(D)), scalar2=None, op0=ALU.mult),
             reads=[bgT], writes=[bgT])
        for i, n in enumerate(("lambda_q1", "lambda_k1", "lambda_q2", "lambda_k2")):
            P.dma(POOL, lamt[:, i, :], lam_d[n].partition_broadcast(128), blamt, writes=[blamt])
        P.op(DVE, lambda e: e.tensor_tensor(out=lamt[:, 0, :], in0=lamt[:, 0, :], in1=lamt[:, 1, :], op=ALU.mult), reads=[blamt], writes=[blamt])
        P.op(DVE, lambda e: e.tensor_tensor(out=lamt[:, 2, :], in0=lamt[:, 2, :], in1=lamt[:, 3, :], op=ALU.mult), reads=[blamt], writes=[blamt])
        P.op(DVE, lambda e: e.tensor_reduce(out=lamv[:, 2:3], in_=lamt[:, 0, :], axis=AX.X, op=ALU.add), reads=[blamt], writes=[blamv])
        P.op(DVE, lambda e: e.tensor_reduce(out=lamv[:, 3:4], in_=lamt[:, 2, :], axis=AX.X, op=ALU.add), reads=[blamt], writes=[blamv])
        P.op(ACT, lambda e: e.activation(out=lamv[:, 4:6], in_=lamv[:, 2:4], func=AF.Exp), reads=[blamv], writes=[blamv])
        P.op(DVE, lambda e: e.scalar_tensor_tensor(out=lamv[:, 0:1], in0=lamv[:, 5:6], scalar=float(-LAMBDA_INIT), in1=lamv[:, 4:5],
                                                   op0=ALU.add, op1=ALU.subtract), reads=[blamv], writes=[blamv])
        P.op(POOL, lambda e: e.memset(mhalf[:, :], -0.5), writes=[bmhalf])
        P.dma(POOL, gsub[:, :], subg_d.partition_broadcast(128), bgsub, writes=[bgsub])
        P.op(POOL, lambda e: e.tensor_scalar(out=gsub[:, :], in0=gsub[:, :], scalar1=float((1.0 - LAMBDA_INIT) * math.sqrt(128.0)),
                                             scalar2=None, op0=ALU.mult), reads=[bgsub], writes=[bgsub])
        P.op(POOL, lambda e: e.memset(Vc[:, :, :, :].rearrange("p a b c -> p (a b c)"), 1.0), writes=bV)
        P.op(POOL, lambda e: e.memset(Sst[:, :], 0.0), writes=bS)
        P.op(POOL, lambda e: e.memset(Sbf[:, :], 0.0), writes=bSbf)
        neglam = lamv[:, 0:1]

        def rsqrt_pool(dst, bdst, src, bsrc, n=1):
            P.op(POOL, lambda e: e.tensor_tensor(out=dst, in0=src, in1=mhalf[:, 0:n], op=ALU.pow),
                 reads=[bsrc, bmhalf], writes=[bdst])

        xnb = [sb("xnb%d" % i, [128, D], BF16) for i in range(NSUB)]
        bxnb = [Buf("xnb%d" % i) for i in range(NSUB)]
        XB = [xs_, stage]
        bXB = [bx, bstage]

        def pn_elem(s, X, bX):
            ss, bss = sm()
            se, bse = sm()
            r, br = sm()
            xb_ = xnb[s]
            P.op(ACT, lambda e: e.activation(out=xb_[:, :], in_=X[s][:, :], func=AF.Square, accum_out=ss),
                 reads=[bX[s]], writes=[bxnb[s], bss])
            P.op(DVE, lambda e: e.tensor_scalar(out=se, in0=ss, scalar1=float(NORM_EPS * D), scalar2=None, op0=ALU.add),
                 reads=[bss], writes=[bse])
            rsqrt_pool(r, br, se, bse)
            P.op(ACT, lambda e: e.activation(out=xb_[:, :], in_=X[s][:, :], func=AF.Copy, scale=r),
                 reads=[bX[s], br], writes=[bxnb[s]])

        def pn_pe(gi, s, bank):
            xb_ = xnb[s]
            for k in range(8):
                P.op(PE, lambda e, k=k: e.transpose(out=pbh[bank][:, k * 128:(k + 1) * 128], in_=xb_[:, k * 128:(k + 1) * 128], identity=ident),
                     reads=[bxnb[s], bcbf], writes=[bpb[bank]])
            P.op(DVE, lambda e: e.tensor_tensor(
                out=Bp[:, 0:8, s * 128:(s + 1) * 128],
                in0=pbh[bank][:, :].rearrange("p (k t) -> p k t", k=8),
                in1=gT[:, gi * 8:(gi + 1) * 8].unsqueeze(2).broadcast_to([128, 8, 128]), op=ALU.mult),
                reads=[bpb[bank], bgT], writes=bB[0:8])

        def load_gpost(name, coef):
            P.dma(POOL, gpost[:, :], gains[name].partition_broadcast(128), bgpost, writes=[bgpost])
            P.op(ACT, lambda e: e.activation(out=gpost[:, :], in_=gpost[:, :], func=AF.Copy, scale=float(coef * math.sqrt(D))),
                 reads=[bgpost], writes=[bgpost])

        def post_norm(s, bankA, bankB, fscale, X, bX):
            ssA, bssA = sm()
            ssB, bssB = sm()
            se, bse = sm()
            r, br = sm()
            for (bank, ss, bss) in ((bankA, ssA, bssA), (bankB, ssB, bssB)):
                junk, bjunk = tf()
                P.op(ACT, lambda e, bank=bank, ss=ss, junk=junk: e.activation(out=junk.bitcast(BF16)[:, 0:512], in_=pb[bank][:, :],
                                                                            func=AF.Square, accum_out=ss),
                     reads=[bpb[bank]], writes=[bjunk, bss])
            P.op(DVE, lambda e: e.tensor_scalar(out=se, in0=ssA, scalar1=ssB, scalar2=float(NORM_EPS * D / (fscale * fscale)),
                                                op0=ALU.add, op1=ALU.add), reads=[bssA, bssB], writes=[bse])
            rsqrt_pool(r, br, se, bse)
            for hf, bank in enumerate((bankA, bankB)):
                t, bt = tf()
                P.op(DVE, lambda e, bank=bank, t=t, hf=hf: e.scalar_tensor_tensor(out=t[:, :], in0=pb[bank][:, :], scalar=r,
                                                                                  in1=gpost[:, hf * 512:(hf + 1) * 512],
                                                                                  op0=ALU.mult, op1=ALU.mult),
                     reads=[bpb[bank], br, bgpost], writes=[bt])
                P.op(DVE, lambda e, t=t, hf=hf: e.tensor_tensor(out=X[s][:, hf * 512:(hf + 1) * 512], in0=X[s][:, hf * 512:(hf + 1) * 512],
                                                                in1=t[:, :], op=ALU.add),
                     reads=[bt, bX[s]], writes=[bX[s]])

        def ffn(pre, post_name, X, bX, early=None, early_pe=None, after_post=None, mid_b=None, tail=None):
            U0 = 8
            for fi in range(11):
                wg, bwg = get_piece((pre, "g", fi))
                wu, bwu = get_piece((pre, "u", fi))
                for c in range(2):
                    f = 2 * fi + c
                    gb_, ub_ = f % 2, 2 + f % 2
                    for (w, bw, bank) in ((wg, bwg, gb_), (wu, bwu, ub_)):
                        for k in range(8):
                            P.op(PE, lambda e, w=w, bank=bank, k=k, c=c: e.matmul(out=pb[bank][:, :], lhsT=w[:, k * 256 + c * 128:k * 256 + c * 128 + 128],
                                                                               rhs=Bv(k), start=(k == 0), stop=(k == 7)),
                                 reads=[bw, bB[k]], writes=[bpb[bank]])
                    t, bt = tf()
                    v, bv = tf()
                    P.op(ACT, lambda e, t=t, gb_=gb_: e.activation(out=t[:, :], in_=pb[gb_][:, :], func=AF.Tanh, scale=0.5),
                         reads=[bpb[gb_]], writes=[bt])
                    P.op(DVE, lambda e, t=t, v=v, gb_=gb_: e.scalar_tensor_tensor(out=v[:, :], in0=t[:, :], scalar=1.0, in1=pb[gb_][:, :],
                                                                                  op0=ALU.add, op1=ALU.mult),
                         reads=[bt, bpb[gb_]], writes=[bv])
                    P.op(DVE, lambda e, v=v, ub_=ub_, f=f: e.tensor_tensor(out=Bv(U0 + f), in0=v[:, :], in1=pb[ub_][:, :], op=ALU.mult),
                         reads=[bv, bpb[ub_]], writes=[bB[U0 + f]])
            load_gpost(post_name, 0.5)
            for ps_ in range(2):
                base = 4 if ps_ == 0 else 0
                for fi in range(11):
                    if ps_ == 0 and fi == 0 and early is not None:
                        early()
                    if ps_ == 0 and fi == 6 and early_pe is not None:
                        early_pe()
                    if ps_ == 1 and fi == 5 and mid_b is not None:
                        mid_b()
                    wd, bwd = get_piece((pre, "d", fi))
                    for c in range(2):
                        f = 2 * fi + c
                        for sl in range(2):
                            s = 2 * ps_ + sl
                            for hf in range(2):
                                bank = base + sl * 2 + hf
                                P.op(PE, lambda e, wd=wd, bank=bank, f=f, s=s, c=c, hf=hf: e.matmul(
                                    out=pb[bank][:, :], lhsT=Bp[:, U0 + f, s * 128:(s + 1) * 128],
                                    rhs=wd[:, c * 1024 + hf * 512:c * 1024 + hf * 512 + 512], start=(f == 0), stop=(f == 21)),
                                    reads=[bwd, bB[U0 + f]], writes=[bpb[bank]])
                for sl in range(2):
                    s = 2 * ps_ + sl
                    post_norm(s, base + sl * 2, base + sl * 2 + 1, 0.5, X, bX)
                    if after_post is not None:
                        after_post(s)
            if tail is not None:
                tail()

        QT0, DAT0, YRT0, MG0, PT0 = 16, 20, 24, 8, 28
        cnt = dict(pt=0, rot=0)


        PTS = [28, 29, 30, 31]

        def att_gen(j):
            pend_f2 = None
            pend_fin = None
            for q_ in (16, 18):
                P.op(POOL, lambda e, q_=q_: e.memset(Bp[64:128, q_, :], 0.0), writes=[bB[q_]])
                P.op(POOL, lambda e, q_=q_: e.memset(Bp[0:64, q_ + 1, :], 0.0), writes=[bB[q_ + 1]])
            for h in range(4):
                wa, bwa = get_piece(("A", h))
                qi = QT0 + (h % 2)
                for part in range(2):
                    for k in range(8):
                        P.op(PE, lambda e, part=part, k=k, wa=wa: e.matmul(out=pb[part][:, :], lhsT=wa[:, k * 256 + part * 128:k * 256 + part * 128 + 128],
                                                                           rhs=Bv(k), start=(k == 0), stop=(k == 7)),
                             reads=[bwa, bB[k]], writes=[bpb[part]])
                qz = (QT0 + 2 * (h % 2), QT0 + 2 * (h % 2) + 1)
                P.op(ACT, lambda e, qz=qz: e.activation(out=Bp[0:64, qz[0], :], in_=pb[0][0:64, :], func=AF.Copy), reads=[bpb[0]], writes=[bB[qz[0]]])
                P.op(DVE, lambda e, qz=qz: e.tensor_copy(out=Bp[64:128, qz[1], :], in_=pb[0][64:128, :]), reads=[bpb[0]], writes=[bB[qz[1]]])
                P.op(DVE, lambda e, h=h: e.tensor_copy(out=Kc[:, h, j * TT:(j + 1) * TT], in_=pb[1][:, :]), reads=[bpb[1]], writes=[bK[j][h]])
                yield
                if pend_f2 is not None:
                    pend_f2()
                    pend_f2 = None
                nkt = 4 * j + 4

                def acc(c, qs, lo, hi):
                    return pb[2 + 2 * c + qs // 2][:, (qs % 2) * 256 + lo:(qs % 2) * 256 + hi]

                info = {}

                def stageA(kt, c, info=info, h=h, qz=qz):
                    cdiag = kt - 4 * j
                    q0 = max(cdiag, 0)
                    ncols = 512 - q0 * 128
                    jk = kt // 4
                    qi = qz[c]
                    P.op(PE, lambda e: e.matmul(out=pb[c][:, 0:ncols], lhsT=Kc[:, h, kt * 128:(kt + 1) * 128],
                                                rhs=Bp[:, qi, q0 * 128:512], start=True, stop=True),
                         reads=[bK[jk][h], bB[qi]], writes=[bpb[c]])
                    pt = PTS[cnt["pt"] % 4]
                    cnt["pt"] += 1
                    P.op(ACT, lambda e: e.activation(out=Bp[:, pt, 0:ncols], in_=pb[c][:, 0:ncols], func=AF.Exp, scale=0.125),
                         reads=[bpb[c]], writes=[bB[pt]])
                    if cdiag >= 0:
                        P.op(POOL, lambda e: e.tensor_tensor(out=Bp[:, pt, 0:128], in0=Bp[:, pt, 0:128], in1=tri, op=ALU.mult),
                             reads=[bB[pt], bcbf], writes=[bB[pt]])
                    info[(kt, c)] = (pt, q0)

                def stageB(kt, c, info=info, h=h):
                    pt, q0 = info[(kt, c)]
                    for qs in range(q0, 4):
                        off = (qs - q0) * 128
                        P.op(PE, lambda e, qs=qs, off=off: e.matmul(
                            out=acc(c, qs, 0, 129), lhsT=Bp[:, pt, off:off + 128], rhs=Vc[:, kt, h, 0:129],
                            start=(kt == 0 and qs % 2 == 0), stop=(kt == 4 * j + qs), skip_group_check=True),
                            reads=[bB[pt], bV[kt]], writes=[bpb[2 + 2 * c + qs // 2]])

                stageA(0, 0)
                stageA(0, 1)
                for kt in range(nkt):
                    for c in range(2):
                        if kt + 1 < nkt:
                            stageA(kt + 1, c)
                        stageB(kt, c)
                    if kt == min(1, nkt - 1) and pend_fin is not None:
                        pend_fin()
                        pend_fin = None
                    yield
                for c in range(2):
                    for qs in range(4):
                        rl, brl = sm()
                        ab = bpb[2 + 2 * c + qs // 2]
                        P.op(DVE, lambda e, rl=rl, qs=qs, c=c: e.reciprocal(out=rl, in_=acc(c, qs, 128, 129)), reads=[ab], writes=[brl])
                        if c == 0:
                            P.op(DVE, lambda e, rl=rl, qs=qs, c=c: e.tensor_scalar(out=yaE[:, qs, :], in0=acc(c, qs, 0, 128), scalar1=rl, scalar2=None, op0=ALU.mult),
                                 reads=[ab, brl], writes=[byaE[qs]])
                        else:
                            cf, bcf = sm()
                            P.op(DVE, lambda e, rl=rl, cf=cf: e.tensor_tensor(out=cf, in0=rl, in1=neglam, op=ALU.mult), reads=[brl, blamv], writes=[bcf])
                            P.op(DVE, lambda e, cf=cf, qs=qs, c=c: e.scalar_tensor_tensor(out=yaE[:, qs, :], in0=acc(c, qs, 0, 128), scalar=cf, in1=yaE[:, qs, :],
                                                                                        op0=ALU.mult, op1=ALU.add),
                                 reads=[ab, bcf, byaE[qs]], writes=[byaE[qs]])

                def f2():
                    jk_, bjk_ = tf()
                    sc = [(sm(), sm(), sm()) for _ in range(4)]
                    for qs in range(4):
                        ss, bss = sc[qs][0]
                        P.op(DVE, lambda e, ss=ss, qs=qs: e.scalar_tensor_tensor(out=jk_[:, qs * 128:(qs + 1) * 128], in0=yaE[:, qs, :], scalar=1.0,
                                                                                in1=yaE[:, qs, :], op0=ALU.mult, op1=ALU.mult, accum_out=ss),
                             reads=[byaE[qs]], writes=[bjk_, bss])
                    for qs in range(4):
                        (ss, bss), (se, bse), _ = sc[qs]
                        P.op(DVE, lambda e, se=se, ss=ss: e.tensor_scalar(out=se, in0=ss, scalar1=float(SUBLN_EPS * 128), scalar2=None, op0=ALU.add),
                             reads=[bss], writes=[bse])
                    for qs in range(4):
                        _, (se, bse), (r, br) = sc[qs]
                        rsqrt_pool(r, br, se, bse)
                    for qs in range(4):
                        r, br = sc[qs][2]
                        P.op(DVE, lambda e, r=r, qs=qs: e.scalar_tensor_tensor(out=datok[:, qs, :], in0=yaE[:, qs, :], scalar=r, in1=gsub[:, :],
                                                                              op0=ALU.mult, op1=ALU.mult),
                             reads=[byaE[qs], br, bgsub], writes=[bdatok[qs]])

                def fin(h=h):
                    for qs in range(4):
                        P.op(PE, lambda e, qs=qs: e.transpose(out=pbh[1][:, qs * 128:(qs + 1) * 128], in_=datok[:, qs, :], identity=ident),
                             reads=[bdatok[qs], bcbf], writes=[bpb[1]])
                    P.op(ACT, lambda e: e.activation(out=Bv(DAT0 + h), in_=pbh[1][:, 0:512], func=AF.Copy), reads=[bpb[1]], writes=[bB[DAT0 + h]])
                pend_f2, pend_fin = f2, fin
                yield
            if pend_f2 is not None:
                pend_f2()
            if pend_fin is not None:
                pend_fin()

        def ret_gen(j, heads, p):
            pj = mb = 6 + p
            rs = rsm[p]
            brs = brsm[p]
            g1 = g1t[p]
            rt, brt = rott[p], brott[p]
            bA, bBk = bpbr[p]["A"], bpbr[p]["B"]
            bq = bpbr[p]
            qkrot, qkT = rs[:, 0:256], rs[:, 256:512]
            vtok, vz, sTm, yr = rs[:, 512:640], rs[:, 640:768], rs[:, 768:896], rs[:, 896:1024]
            for h in heads:
                wr, bwr = get_piece(("R", h))
                wrv, bwrv = get_piece(("RV", h))
                cd = float(RET_GAMMA[h] ** 128)
                hs = slice(h * 128, (h + 1) * 128)
                for s in range(NSUB):
                    tok = j * TT + s * 128
                    P.dma(POOL, rt[:, :], rot_d[h * T + tok:h * T + tok + 128, :], brt, writes=[brt])
                    for (w, bw, lo, bb) in ((wr, bwr, 0, bA), (wrv, bwrv, 256, bBk)):
                        for k in range(8):
                            P.op(PE, lambda e, w=w, lo=lo, k=k, s=s: e.matmul(
                                out=pb[pj][:, lo:lo + 256], lhsT=Bp[:, k, s * 128:(s + 1) * 128], rhs=w[:, k * 256:(k + 1) * 256],
                                start=(k == 0), stop=(k == 7)), reads=[bw, bB[k]], writes=[bb])
                    t1, bt1 = tf()
                    t2, bt2 = tf()
                    P.op(DVE, lambda e, t1=t1: e.tensor_tensor(out=t1[:, 0:256], in0=pb[pj][:, 0:256], in1=rt[:, 0:256], op=ALU.mult),
                         reads=[bA, brt], writes=[bt1])
                    P.op(DVE, lambda e, t2=t2: e.tensor_tensor(out=t2[:, 0:256:2], in0=pb[pj][:, 1:256:2], in1=rt[:, 256:512:2], op=ALU.mult),
                         reads=[bA, brt], writes=[bt2])
                    P.op(DVE, lambda e, t2=t2: e.tensor_tensor(out=t2[:, 1:256:2], in0=pb[pj][:, 0:256:2], in1=rt[:, 257:512:2], op=ALU.mult),
                         reads=[bA, brt], writes=[bt2])
                    P.op(DVE, lambda e, t1=t1, t2=t2: e.tensor_tensor(out=qkrot, in0=t1[:, 0:256], in1=t2[:, 0:256], op=ALU.add),
                         reads=[bt1, bt2], writes=[brs["qkrot"]])
                    P.op(DVE, lambda e: e.tensor_copy(out=vtok, in_=pb[pj][:, 256:384]), reads=[bBk], writes=[brs["vtok"]])
                    tg, btg = tf()
                    P.op(ACT, lambda e, tg=tg: e.activation(out=tg[:, 0:128], in_=pb[pj][:, 384:512], func=AF.Tanh, scale=0.5),
                         reads=[bBk], writes=[btg])
                    P.op(DVE, lambda e, tg=tg: e.scalar_tensor_tensor(out=g1[:, :], in0=tg[:, 0:128], scalar=1.0, in1=pb[pj][:, 384:512],
                                                                      op0=ALU.add, op1=ALU.mult),
                         reads=[btg, bBk], writes=[brs["g1"]])
                    yield
                    for i in range(2):
                        P.op(PE, lambda e, i=i: e.transpose(out=pbh[pj][:, i * 128:(i + 1) * 128], in_=rs[:, i * 128:(i + 1) * 128], identity=ident),
                             reads=[brs["qkrot"], bcbf], writes=[bA])
                    P.op(DVE, lambda e: e.tensor_copy(out=qkT, in_=pbh[pj][:, 0:256]), reads=[bA], writes=[brs["qkT"]])
                    yield
                    P.op(PE, lambda e: e.matmul(out=pb[mb][:, 256:384], lhsT=rs[:, 384:512], rhs=rs[:, 256:384], start=True, stop=True),
                         reads=[brs["qkT"]], writes=[bq["S"]])
                    P.op(DVE, lambda e: e.tensor_tensor(out=sTm, in0=pb[mb][:, 256:384], in1=cst[:, C_TRI:C_TRI + 128], op=ALU.mult),
                         reads=[bq["S"], bcst], writes=[brs["sTm"]])
                    yield
                    P.op(PE, lambda e: e.matmul(out=pb[mb][:, 0:128], lhsT=sTm, rhs=vtok, start=True, stop=False),
                         reads=[brs["sTm"], brs["vtok"]], writes=[bq["I"]])
                    P.op(PE, lambda e, hs=hs: e.matmul(out=pb[mb][:, 0:128], lhsT=rs[:, 256:384], rhs=Sbf[:, hs], start=False, stop=True),
                         reads=[brs["qkT"], bSbf[h]], writes=[bq["I"]])
                    P.op(PE, lambda e: e.matmul(out=pb[mb][:, 384:512], lhsT=rs[:, 128:256], rhs=vtok, start=True, stop=True),
                         reads=[brs["qkrot"], brs["vtok"]], writes=[bq["KV"]])
                    P.op(DVE, lambda e, hs=hs, cd=cd: e.scalar_tensor_tensor(out=Sst[:, hs], in0=Sst[:, hs], scalar=cd, in1=pb[mb][:, 384:512],
                                                                           op0=ALU.mult, op1=ALU.add),
                         reads=[bS[h], bq["KV"]], writes=[bS[h]])
                    P.op(POOL, lambda e, hs=hs, cd=cd: e.tensor_scalar(out=Sbf[:, hs], in0=Sst[:, hs], scalar1=cd, scalar2=None, op0=ALU.mult),
                         reads=[bS[h]], writes=[bSbf[h]])
                    jk2, bjk2 = tf()
                    ss, bss = sm()
                    se, bse = sm()
                    r2, br2 = sm()
                    cn = 0.5 * math.sqrt(128.0)
                    P.op(ACT, lambda e, jk2=jk2, ss=ss: e.activation(out=jk2[:, 0:128], in_=pb[mb][:, 0:128], func=AF.Square, scale=float(1.0 / cn), accum_out=ss),
                         reads=[bq["I"]], writes=[bjk2, bss])
                    P.op(DVE, lambda e, se=se, ss=ss: e.tensor_scalar(out=se, in0=ss, scalar1=float(NORM_EPS * 128 / (cn * cn)), scalar2=None, op0=ALU.add),
                         reads=[bss], writes=[bse])
                    rsqrt_pool(r2, br2, se, bse)
                    P.op(DVE, lambda e, r2=r2: e.scalar_tensor_tensor(out=yr, in0=pb[mb][:, 0:128], scalar=r2, in1=g1[:, :], op0=ALU.mult, op1=ALU.mult),
                         reads=[bq["I"], br2, brs["g1"]], writes=[brs["yr"]])
                    yield
                    yield
                    P.op(PE, lambda e: e.transpose(out=pbh[pj][:, 512:640], in_=yr, identity=ident), reads=[brs["yr"], bcbf], writes=[bBk])
                    P.op(DVE, lambda e, h=h, s=s: e.tensor_copy(out=Bp[:, YRT0 + h, s * 128:(s + 1) * 128], in_=pbh[pj][:, 512:640]),
                         reads=[bBk], writes=[bB[YRT0 + h]])
                    yield

        def mixer(j, X, bX, pre_tail=None):
            wv = [get_piece(("V", i)) for i in range(2)]

            def vproj(s):
                bank = s
                for i in range(2):
                    for k in range(8):
                        P.op(PE, lambda e, bank=bank, i=i, k=k, s=s: e.matmul(
                            out=pb[bank][:, i * 256:(i + 1) * 256], lhsT=Bp[:, k, s * 128:(s + 1) * 128],
                            rhs=wv[i][0][:, k * 256:(k + 1) * 256], start=(k == 0), stop=(k == 7)),
                            reads=[wv[i][1], bB[k]], writes=[bpb[bank]])
                kt = 4 * j + s
                P.op(ACT, lambda e, bank=bank, kt=kt: e.activation(out=Vc[:, kt, :, 0:128],
                                                                   in_=pb[bank][:, :].rearrange("p (h e) -> p h e", h=4), func=AF.Copy),
                     reads=[bpb[bank]], writes=[bV[kt]])

            gens = [ret_gen(j, (0, 2), 0), ret_gen(j, (1, 3), 1)]
            vproj(0)
            vproj(1)
            for g_ in gens:
                next(g_)
            if pre_tail is not None:
                pre_tail()
            vproj(2)
            vproj(3)

            n_att = 4 * (4 * j + 6)
            ratio = max(1, int(n_att / 48.0 + 0.5))
            ga_ = att_gen(j)
            att_live = True
            while gens or att_live:
                for _ in range(ratio if gens else 1):
                    if att_live:
                        try:
                            next(ga_)
                        except StopIteration:
                            att_live = False
                for g_ in list(gens):
                    try:
                        next(g_)
                    except StopIteration:
                        gens.remove(g_)
                if conv_gen[0] is not None:
                    next(conv_gen[0], None)

            load_gpost("mix_post_g", 1.0)
            gate_w = {}
            tq = {}

            def stage1(g):
                if g % 2 == 0:
                    gate_w["ga"] = get_piece(("GA", g // 2))
                    gate_w["gr"] = get_piece(("GR", g // 2))
                par = (g % 2) * 4
                cc = g % 2
                ts_ = []
                for (w, bank) in ((gate_w["ga"], par + 2), (gate_w["gr"], par + 3)):
                    for k in range(8):
                        P.op(PE, lambda e, w=w, bank=bank, k=k, cc=cc: e.matmul(
                            out=pb[bank][:, :], lhsT=w[0][:, k * 256 + cc * 128:k * 256 + cc * 128 + 128], rhs=Bv(k),
                            start=(k == 0), stop=(k == 7)), reads=[w[1], bB[k]], writes=[bpb[bank]])
                    t, bt = tf()
                    P.op(ACT, lambda e, t=t, bank=bank: e.activation(out=t[:, :], in_=pb[bank][:, :], func=AF.Tanh, scale=0.5),
                         reads=[bpb[bank]], writes=[bt])
                    ts_.append((t, bt))
                tq[g] = ts_

            proj_w = {}

            def stage2(g):
                if g % 4 == 0:
                    proj_w["ap"] = get_piece(("AP", g // 4))
                    proj_w["rp"] = get_piece(("RP", g // 4))
                par = (g % 2) * 4
                cg = g % 4
                for (w, bank, src0) in ((proj_w["ap"], par + 0, DAT0), (proj_w["rp"], par + 1, YRT0)):
                    for c4 in range(4):
                        P.op(PE, lambda e, w=w, bank=bank, src0=src0, c4=c4, cg=cg: e.matmul(
                            out=pb[bank][:, :], lhsT=w[0][:, c4 * 512 + cg * 128:c4 * 512 + cg * 128 + 128], rhs=Bv(src0 + c4),
                            start=(c4 == 0), stop=(c4 == 3)), reads=[w[1], bB[src0 + c4]], writes=[bpb[bank]])
                ts_ = tq.pop(g)
                for (t, bt), ybank in zip(ts_, (par + 0, par + 1)):
                    P.op(DVE, lambda e, t=t, ybank=ybank: e.scalar_tensor_tensor(out=t[:, :], in0=t[:, :], scalar=1.0, in1=pb[ybank][:, :],
                                                                              op0=ALU.add, op1=ALU.mult),
                         reads=[bt, bpb[ybank]], writes=[bt])
                P.op(DVE, lambda e, ts_=ts_, g=g: e.tensor_tensor(out=Bv(MG0 + g), in0=ts_[0][0][:, :], in1=ts_[1][0][:, :], op=ALU.add),
                     reads=[ts_[0][1], ts_[1][1]], writes=[bB[MG0 + g]])

            stage1(0)
            for g in range(8):
                if g + 1 < 8:
                    stage1(g + 1)
                stage2(g)
            wo = [get_piece(("WO", i)) for i in range(4)]
            for s in range(NSUB):
                bA, bB2 = (s % 2) * 2, (s % 2) * 2 + 1
                for i in range(4):
                    bank = bA if i < 2 else bB2
                    col = (i % 2) * 256
                    for k in range(8):
                        P.op(PE, lambda e, bank=bank, col=col, k=k, s=s, i=i: e.matmul(
                            out=pb[bank][:, col:col + 256], lhsT=Bp[:, MG0 + k, s * 128:(s + 1) * 128], rhs=wo[i][0][:, k * 256:(k + 1) * 256],
                            start=(k == 0), stop=(k == 7)), reads=[wo[i][1], bB[MG0 + k]], writes=[bpb[bank]])
                if s >= 2:
                    pn_pe(2, s - 2, 4 + (s - 2) % 2)
                post_norm(s, bA, bB2, 0.5, X, bX)
                pn_elem(s, X, bX)
            pn_pe(2, NSUB - 2, 4 + (NSUB - 2) % 2)
            pn_pe(2, NSUB - 1, 4 + (NSUB - 1) % 2)

        stores = []
        conv_gen = [None]

        def load_x(j, X, bX):
            for s in range(NSUB):
                r0 = (j * NSUB + s) * 128
                P.dma(POOL, X[s][:, :], x_d[r0:r0 + 128, :], bX[s], writes=[bX[s]])

        load_x(0, XB[0], bXB[0])
        for s in range(NSUB):
            pn_elem(s, XB[0], bXB[0])
            pn_pe(0, s, 6 + s % 2)
        for j in range(NTILES):
            X, bX = XB[j % 2], bXB[j % 2]
            Xn, bXn = XB[(j + 1) % 2], bXB[(j + 1) % 2]
            last = (j == NTILES - 1)

            def f1_after(s, X=X, bX=bX):
                pn_elem(s, X, bX)

            def f1_mid():
                pn_pe(1, 0, 4)
                pn_pe(1, 1, 6)

            def f1_tail():
                pn_pe(1, 2, 5)
                pn_pe(1, 3, 7)

            def f1_early():
                conv_gen[0] = convert_rest()
                for _ in range(14):
                    next(conv_gen[0], None)

            ffn("ffn1", "ffn1_post_g", X, bX, early=f1_early if j == 0 else None, after_post=f1_after, mid_b=f1_mid)
            if debug:
                for s in range(NSUB):
                    r0 = (j * NSUB + s) * 128
                    stores.append(P.dma(POOL, dbg_d["dbg1"][r0:r0 + 128, :], X[s][:, :], bX[s], reads=[bX[s]]))
            mixer(j, X, bX, pre_tail=f1_tail)
            if conv_gen[0] is not None:
                for _ in conv_gen[0]:
                    pass
                conv_gen[0] = None
            if debug:
                for s in range(NSUB):
                    r0 = (j * NSUB + s) * 128
                    stores.append(P.dma(POOL, dbg_d["dbg2"][r0:r0 + 128, :], X[s][:, :], bX[s], reads=[bX[s]]))
            overlap = (not last) and j >= 1
            if overlap:
                load_x(j + 1, Xn, bXn)

            def f2_early(Xn=Xn, bXn=bXn):
                for s in range(NSUB):
                    pn_elem(s, Xn, bXn)

            def f2_early_pe():
                for s in range(NSUB):
                    pn_pe(0, s, s)

            def f2_after(s, j=j, X=X, bX=bX):
                r0 = (j * NSUB + s) * 128
                stores.append(P.dma(POOL, out_d[r0:r0 + 128, :], X[s][:, :], bX[s], reads=[bX[s]]))

            ffn("ffn2", "ffn2_post_g", X, bX, early=f2_early if overlap else None, early_pe=f2_early_pe if overlap else None,
                after_post=f2_after)
            if (not last) and not overlap:
                load_x(j + 1, Xn, bXn)
                for s in range(NSUB):
                    pn_elem(s, Xn, bXn)
                    pn_pe(0, s, 6 + s % 2)
        if debug:
            print("SBUF bytes/partition", sbuf_bytes[0])
        P.final_wait(POOL, stores)
        P.emit()
    return nc


W2D = ("ffn1_w_gate", "ffn1_w_up", "ffn1_w_down", "w_in", "w_attn_proj", "w_ret_proj", "w_out",
       "ffn2_w_gate", "ffn2_w_up", "ffn2_w_down")
VECS = ("ffn1_pre_g", "ffn1_post_g", "mix_pre_g", "mix_post_g", "ffn2_pre_g", "ffn2_post_g",
        "lambda_q1", "lambda_k1", "lambda_q2", "lambda_k2", "diff_subln_g")


def make_in_maps(inputs, T, ncores, debug=False):
    cst, rot = _const_tables(T)
    shared = {"cst": cst, "rot": rot}
    for n in W2D:
        shared[n] = np.ascontiguousarray(np.asarray(inputs[n], dtype=np.float32)[0])
    for n in VECS:
        shared[n] = np.ascontiguousarray(np.asarray(inputs[n], dtype=np.float32).reshape(1, -1))
    x = np.asarray(inputs["x"], dtype=np.float32)
    maps = []
    for c in range(ncores):
        m = dict(shared)
        m["x"] = np.ascontiguousarray(x[c, :T])
        maps.append(m)
    return maps


def kernel(**inputs):
    nc = build(SEQ)
    in_maps = make_in_maps(inputs, SEQ, NCORES)
    res = run_bass_kernel_spmd(nc, in_maps, core_ids=list(range(NCORES)))
    out = np.stack([np.asarray(res.results[c]["out"]) for c in range(NCORES)], axis=0)
    return out.astype(np.float32, copy=False)
```

```python
import math
from contextlib import ExitStack

import numpy as np
import concourse.bass as bass
import concourse.mybir as mybir
from concourse.bass_utils import run_bass_kernel_spmd

F32 = mybir.dt.float32
BF16 = mybir.dt.bfloat16
AF = mybir.ActivationFunctionType
ALU = mybir.AluOpType
AX = mybir.AxisListType

D = 1024
DFF = 2816
SEQ = 4096
NCORES = 8
TT = 512
NSUB = 4
LAMBDA_INIT = 0.8 - 0.6 * math.exp(-0.3 * 0)
NORM_EPS = 1e-6
SUBLN_EPS = 1e-5
RET_GAMMA = [1.0 - 2.0 ** (-5.0 - h) for h in range(4)]

PE, ACT, DVE, POOL, SP = "tensor", "scalar", "vector", "gpsimd", "sync"
ENGS = (PE, ACT, DVE, POOL, SP)


class Buf:
    __slots__ = ("name", "w", "r", "sem", "dcnt", "excl")

    def __init__(self, name, excl=False):
        self.name = name
        self.excl = excl
        self.w = None
        self.r = []
        self.sem = None
        self.dcnt = 0


class Op:
    __slots__ = ("eng", "fn", "waits", "need_inc", "is_dma", "sem", "val")

    def __init__(self, eng, fn, is_dma=False):
        self.eng = eng
        self.fn = fn
        self.waits = []
        self.need_inc = False
        self.is_dma = is_dma
        self.sem = None
        self.val = None


class Prog:
    def __init__(self, nc, stack, n_dma_sems=48):
        self.nc = nc
        self.ops = {e: [] for e in ENGS}
        self.esem = {e: stack.enter_context(nc.semaphore("es_" + e)) for e in ENGS}
        self.free_sems = [stack.enter_context(nc.semaphore("ds%d" % i)) for i in range(n_dma_sems)]

    def _dep(self, op, prev, same_ok):
        if prev is None or prev is op:
            return
        if (not prev.is_dma) and (not op.is_dma) and prev.eng == op.eng == PE:
            return
        if not prev.is_dma:
            prev.need_inc = True
        op.waits.append(prev)

    def _track(self, op, reads, writes):
        ex = [b for b in reads if b.excl]
        if ex:
            reads = [b for b in reads if not b.excl]
            writes = list(writes) + [b for b in ex if b not in writes]
        for b in reads:
            self._dep(op, b.w, same_ok=(op.eng == PE))
        for b in writes:
            self._dep(op, b.w, same_ok=True)
            for r in b.r:
                self._dep(op, r, same_ok=True)
        for b in reads:
            b.r.append(op)
        for b in writes:
            b.w = op
            b.r = []

    def op(self, eng, fn, reads=(), writes=()):
        o = Op(eng, fn)
        self._track(o, reads, writes)
        self.ops[eng].append(o)
        return o

    def dma(self, queue, out_ap, in_ap, sembuf, reads=(), writes=(), **kw):
        if sembuf.sem is None:
            sembuf.sem = {}
            sembuf.dcnt = {}
        if queue not in sembuf.sem:
            sembuf.sem[queue] = self.free_sems.pop()
            sembuf.dcnt[queue] = 0
        sembuf.dcnt[queue] += 1
        o = Op(queue, lambda eng: eng.dma_start(out=out_ap, in_=in_ap, **kw), is_dma=True)
        o.sem = sembuf.sem[queue]
        o.val = 16 * sembuf.dcnt[queue]
        self._track(o, reads, writes)
        self.ops[queue].append(o)
        return o

    def final_wait(self, eng, ops):
        o = Op(eng, None)
        for p in ops:
            self._dep(o, p, same_ok=False)
        self.ops[eng].append(o)

    def emit(self):
        nc = self.nc
        for e in ENGS:
            cnt = 0
            for o in self.ops[e]:
                if (not o.is_dma) and o.need_inc:
                    cnt += 1
                    o.val = cnt
                    o.sem = self.esem[e]
        with nc.Block() as block:
            for e in ENGS:
                def body(eng, e=e):
                    seen = {}
                    for o in self.ops[e]:
                        for w in o.waits:
                            k = id(w.sem)
                            if seen.get(k, 0) >= w.val:
                                continue
                            eng.wait_ge(w.sem, w.val)
                            seen[k] = w.val
                        if o.fn is None:
                            continue
                        ins = o.fn(eng)
                        if o.is_dma:
                            ins.then_inc(o.sem, 16)
                        elif o.need_inc:
                            ins.then_inc(o.sem, 1)
                getattr(block, e)(body)


C_ID, C_TRI, C_END = 0, 128, 256


def _const_tables(T):
    cst = np.zeros((128, C_END), np.float32)
    idx = np.arange(128)
    cst[:, C_ID:C_ID + 128] = np.eye(128, dtype=np.float32)
    cst[:, C_TRI:C_TRI + 128] = (idx[None, :] >= idx[:, None]).astype(np.float32)
    angle = (1.0 / (10000.0 ** np.linspace(0.0, 1.0, 64, dtype=np.float32))).astype(np.float32)
    angle = np.repeat(angle, 2)
    ang = (np.arange(T, dtype=np.float32)[:, None] * angle[None, :]).astype(np.float32)
    sn, cs = np.sin(ang).astype(np.float32), np.cos(ang).astype(np.float32)
    sgn = np.where(np.arange(128) % 2 == 0, -1.0, 1.0).astype(np.float32)
    ks = np.float32(128.0 ** -0.5)
    tl = (np.arange(T) % 128).astype(np.float64)
    rots = []
    for h in range(4):
        lg64 = np.log(np.float64(RET_GAMMA[h]))
        qa = np.exp(lg64 * (tl + 1.0)).astype(np.float32)[:, None]
        kb = (np.exp(-lg64 * (tl + 1.0)) * np.float64(ks)).astype(np.float32)[:, None]
        rots.append(np.concatenate([cs * qa, cs * kb, sn * sgn * qa, sn * sgn * kb], axis=1))
    rot = np.concatenate(rots, axis=0).astype(np.float32)
    return cst, rot


USE_DMA_CAST = True
NS = 8
NTMP = 8
NB = 32


def build(T, debug=False):
    NTILES = T // TT
    NKT = T // 128
    nc = bass.Bass("TRN2", target_bir_lowering=False)

    def din(name, shape):
        return nc.dram_tensor(name, shape, F32, kind="ExternalInput").ap()

    x_d = din("x", [T, D])
    out_d = nc.dram_tensor("out", [T, D], F32, kind="ExternalOutput").ap()
    Wd_ = {}
    for pre in ("ffn1", "ffn2"):
        Wd_[pre + "_w_gate"] = din(pre + "_w_gate", [D, DFF])
        Wd_[pre + "_w_up"] = din(pre + "_w_up", [D, DFF])
        Wd_[pre + "_w_down"] = din(pre + "_w_down", [DFF, D])
    Wd_["w_in"] = din("w_in", [D, 5632])
    Wd_["w_attn_proj"] = din("w_attn_proj", [512, D])
    Wd_["w_ret_proj"] = din("w_ret_proj", [512, D])
    Wd_["w_out"] = din("w_out", [D, D])
    gains = {n: din(n, [1, D]) for n in ("ffn1_pre_g", "ffn1_post_g", "mix_pre_g", "mix_post_g", "ffn2_pre_g", "ffn2_post_g")}
    lam_d = {n: din(n, [1, 64]) for n in ("lambda_q1", "lambda_k1", "lambda_q2", "lambda_k2")}
    subg_d = din("diff_subln_g", [1, 128])
    cst_d = din("cst", [128, C_END])
    rot_d = din("rot", [4 * T, 512])
    dbg_d = {}
    if debug:
        for n in ("dbg1", "dbg2"):
            dbg_d[n] = nc.dram_tensor(n, [T, D], F32, kind="ExternalOutput").ap()

    pieces = {}

    def addp(key, parts):
        pieces[key] = dict(parts=parts, idx=len(pieces))

    for pre in ("ffn1", "ffn2"):
        for i in range(11):
            addp((pre, "g", i), [(Wd_[pre + "_w_gate"], 0, 8, 256 * i, 256)])
            addp((pre, "u", i), [(Wd_[pre + "_w_up"], 0, 8, 256 * i, 256)])
            addp((pre, "d", i), [(Wd_[pre + "_w_down"], 2 * i, 2, 0, 1024)])
    win = Wd_["w_in"]
    for h in range(4):
        addp(("A", h), [(win, 0, 8, 128 * h, 128), (win, 0, 8, 512 + 128 * h, 128)])
        addp(("R", h), [(win, 0, 8, 1536 + 128 * h, 128), (win, 0, 8, 2048 + 128 * h, 128)])
        addp(("RV", h), [(win, 0, 8, 2560 + 128 * h, 128), (win, 0, 8, 3072 + 128 * h, 128)])
    for i in range(2):
        addp(("V", i), [(win, 0, 8, 1024 + 256 * i, 256)])
        addp(("AP", i), [(Wd_["w_attn_proj"], 0, 4, 512 * i, 512)])
        addp(("RP", i), [(Wd_["w_ret_proj"], 0, 4, 512 * i, 512)])
    for i in range(4):
        addp(("GA", i), [(win, 0, 8, 3584 + 256 * i, 256)])
        addp(("GR", i), [(win, 0, 8, 4608 + 256 * i, 256)])
        addp(("WO", i), [(Wd_["w_out"], 0, 8, 256 * i, 256)])
    NP = len(pieces)
    scr_d = nc.dram_tensor("wscr", [NP, 128, 2048], BF16).ap()

    with ExitStack() as st:
        P = Prog(nc, st, n_dma_sems=92)

        sbuf_bytes = [0]

        def sb(name, shape, dt):
            n = 1
            for d_ in shape[1:]:
                n *= d_
            sbuf_bytes[0] += n * (4 if dt == F32 else 2)
            return st.enter_context(nc.sbuf_tensor("sb_" + name, shape, dt))

        Kc = sb("Kc", [128, 4, T], BF16)
        bK = [[Buf("K%d_%d" % (j, h)) for h in range(4)] for j in range(NTILES)]
        Vc = sb("Vc", [128, NKT, 4, 130], BF16)
        bV = [Buf("V%d" % k) for k in range(NKT)]
        xs_ = [sb("x%d" % s, [128, D], F32) for s in range(NSUB)]
        bx = [Buf("x%d" % s) for s in range(NSUB)]
        Bp = sb("Bp", [128, NB, 512], BF16)
        bB = [Buf("B%d" % i) for i in range(NB)]
        slots = [sb("slot%d" % i, [128, 2048], BF16) for i in range(NS)]
        bslot = [Buf("slot%d" % i) for i in range(NS)]
        NSTG = 4
        stage = [sb("stg%d" % i, [128, 1024], F32) for i in range(NSTG)]
        bstage = [Buf("stg%d" % i) for i in range(NSTG)]
        gpost = sb("gpost", [128, D], F32)
        bgpost = Buf("gpost")
        tmpf = [sb("tf%d" % i, [128, 512], F32) for i in range(NTMP)]
        btmpf = [Buf("tf%d" % i) for i in range(NTMP)]
        cst = sb("cst", [128, C_END], F32)
        bcst = Buf("cst")
        cbf = sb("cbf", [128, 256], BF16)
        bcbf = Buf("cbf")
        rott = [sb("rot%d" % i, [128, 512], F32) for i in range(2)]
        brott = [Buf("rot%d" % i) for i in range(2)]
        gT = sb("gT", [128, 24], F32)
        bgT = Buf("gT")
        Sst = sb("Sst", [128, 512], F32)
        Sbf = sb("Sbf", [128, 512], BF16)
        bS = [Buf("S%d" % h) for h in range(4)]
        bSbf = [Buf("Sbf%d" % h) for h in range(4)]
        small = sb("small", [128, 64], F32)
        bsmall = [Buf("sm%d" % i) for i in range(64)]
        lamt = sb("lamt", [128, 4, 64], F32)
        blamt = Buf("lamt")
        lamv = sb("lamv", [128, 8], F32)
        blamv = Buf("lamv")
        mhalf = sb("mhalf", [128, 4], F32)
        bmhalf = Buf("mhalf")
        gsub = sb("gsub", [128, 128], F32)
        bgsub = Buf("gsub")
        yaE = sb("yaE", [128, 4, 128], F32)
        byaE = [Buf("yaE%d" % i) for i in range(4)]
        datok = sb("datok", [128, 4, 128], BF16)
        bdatok = [Buf("datok%d" % i) for i in range(4)]
        rsm = [sb("rsm%d" % i, [128, 1024], BF16) for i in range(2)]
        brsm = [{n: Buf("rsm%d_%s" % (i, n)) for n in ("qkrot", "qkT", "vtok", "vz", "sTm", "yr", "g1")} for i in range(2)]
        g1t = [sb("g1t%d" % i, [128, 128], F32) for i in range(2)]

        pb = [st.enter_context(nc.psum_tensor("pb%d" % i, [128, 512], F32)) for i in range(8)]
        bpb = [Buf("pb%d" % i, excl=True) for i in range(8)]
        pbh = [p.bitcast(BF16) for p in pb]
        bpbr = [{n: bpb[6 + i] for n in ("A", "B", "S", "KV", "I", "E")} for i in range(2)]

        ident = cbf[:, 0:128]
        tri = cbf[:, 128:256]

        state = dict(tf=0, sm=0, slot=0, stg=0, cast=0, inscr=set())

        def tf():
            i = state["tf"]
            state["tf"] = (i + 1) % NTMP
            return tmpf[i], btmpf[i]

        def sm():
            i = state["sm"]
            state["sm"] = (i + 1) % 64
            return small[:, i:i + 1], bsmall[i]

        def Bv(i):
            return Bp[:, i, :]

        def get_piece(key):
            pc = pieces[key]
            pi = pc["idx"]
            si = state["slot"]
            state["slot"] = (si + 1) % NS
            sl, bsl = slots[si], bslot[si]
            use_dc = USE_DMA_CAST and key[0] in ("ffn1", "ffn2")
            if pi not in state["inscr"] and use_dc:
                parts = pc["parts"]
                nrc = parts[0][2]
                ctot = sum(p[4] for p in parts)
                sv = sl[:, :].rearrange("p (k c) -> p k c", k=nrc)
                co = 0
                for (W, r0, _, c0, ncol) in parts:
                    src = W[r0 * 128:(r0 + nrc) * 128, c0:c0 + ncol].rearrange("(k p) c -> p k c", p=128)
                    P.dma(POOL, sv[:, :, co:co + ncol], src, bsl, writes=[bsl])
                    co += ncol
                bscr = Buf("scr%d" % pi)
                pc["bscr"] = bscr
                P.dma(SP, scr_d[pi], sl[:, :], bsl, reads=[bsl], writes=[bscr])
                state["inscr"].add(pi)
            elif pi not in state["inscr"]:
                parts = pc["parts"]
                nrc = parts[0][2]
                hr = nrc // 2
                ctot = sum(p[4] for p in parts)
                for hf in range(2):
                    g = state["stg"]
                    state["stg"] = (g + 1) % NSTG
                    ce = 1 + state["cast"]
                    state["cast"] = (state["cast"] + 1) % 2
                    sv = stage[g][:, :].rearrange("p (k c) -> p k c", k=hr)
                    co = 0
                    for (W, r0, _, c0, ncol) in parts:
                        rs = (r0 + hf * hr) * 128
                        src = W[rs:rs + hr * 128, c0:c0 + ncol].rearrange("(k p) c -> p k c", p=128)
                        P.dma(SP, sv[:, :, co:co + ncol], src, bstage[g], writes=[bstage[g]])
                        co += ncol
                    if ce == 0:
                        P.op(POOL, lambda e, sl=sl, hf=hf, g=g: e.tensor_copy(out=sl[:, hf * 1024:(hf + 1) * 1024], in_=stage[g][:, :]),
                             reads=[bstage[g]], writes=[bsl])
                    elif ce == 1:
                        P.op(ACT, lambda e, sl=sl, hf=hf, g=g: e.activation(out=sl[:, hf * 1024:(hf + 1) * 1024], in_=stage[g][:, :], func=AF.Copy),
                             reads=[bstage[g]], writes=[bsl])
                    else:
                        P.op(DVE, lambda e, sl=sl, hf=hf, g=g: e.tensor_copy(out=sl[:, hf * 1024:(hf + 1) * 1024], in_=stage[g][:, :]),
                             reads=[bstage[g]], writes=[bsl])
                bscr = Buf("scr%d" % pi)
                pc["bscr"] = bscr
                P.dma(POOL, scr_d[pi], sl[:, :], bsl, reads=[bsl], writes=[bscr])
                state["inscr"].add(pi)
            else:
                P.dma(SP, sl[:, :], scr_d[pi], bsl, reads=[pc["bscr"]], writes=[bsl])
            return sl, bsl

        def convert_rest():
            order = [("V", 0), ("V", 1), ("R", 0), ("RV", 0), ("R", 1), ("RV", 1), ("A", 0), ("A", 1), ("R", 2), ("RV", 2),
                     ("R", 3), ("RV", 3), ("A", 2), ("A", 3), ("AP", 0), ("RP", 0), ("GA", 0), ("GR", 0), ("GA", 1), ("GR", 1),
                     ("AP", 1), ("RP", 1), ("GA", 2), ("GR", 2), ("GA", 3), ("GR", 3)] + [("WO", i) for i in range(4)]
            for i in range(11):
                order += [("ffn2", "g", i), ("ffn2", "u", i)]
            order += [("ffn2", "d", i) for i in range(11)]
            convbufs = [Buf("conv%d" % i) for i in range(44)]
            ci = 0
            for key in order:
                pc = pieces[key]
                pi = pc["idx"]
                if pi in state["inscr"]:
                    continue
                cb = convbufs[ci % len(convbufs)]
                ci += 1
                parts = pc["parts"]
                nrc = parts[0][2]
                dv = scr_d[pi].rearrange("p (k c) -> p k c", k=nrc)
                bscr = Buf("scr%d" % pi)
                pc["bscr"] = bscr
                co = 0
                for (W, r0, _, c0, ncol) in parts:
                    src = W[r0 * 128:(r0 + nrc) * 128, c0:c0 + ncol].rearrange("(k p) c -> p k c", p=128)
                    P.dma(POOL, dv[:, :, co:co + ncol], src, cb, writes=[bscr, cb])
                    co += ncol
                state["inscr"].add(pi)
                yield

        P.dma(POOL, cst[:, :], cst_d, bcst, writes=[bcst])
        P.op(POOL, lambda e: e.tensor_copy(out=cbf[:, :], in_=cst[:, 0:256]), reads=[bcst], writes=[bcbf])
        for gi, n in enumerate(("ffn1_pre_g", "mix_pre_g", "ffn2_pre_g")):
            P.dma(POOL, gT[:, gi * 8:(gi + 1) * 8], gains[n].rearrange("o (k p) -> p (o k)", p=128), bgT, writes=[bgT],
                  allow_slow_non_contiguous=True)
        P.op(POOL, lambda e: e.tensor_scalar(out=gT[:, :], in0=gT[:, :], scalar1=float(math.sqrt(D)), scalar2=None, op0=ALU.mult),
             reads=[bgT], writes=[bgT])
        for i, n in enumerate(("lambda_q1", "lambda_k1", "lambda_q2", "lambda_k2")):
            P.dma(POOL, lamt[:, i, :], lam_d[n].partition_broadcast(128), blamt, writes=[blamt])
        P.op(DVE, lambda e: e.tensor_tensor(out=lamt[:, 0, :], in0=lamt[:, 0, :], in1=lamt[:, 1, :], op=ALU.mult), reads=[blamt], writes=[blamt])
        P.op(DVE, lambda e: e.tensor_tensor(out=lamt[:, 2, :], in0=lamt[:, 2, :], in1=lamt[:, 3, :], op=ALU.mult), reads=[blamt], writes=[blamt])
        P.op(DVE, lambda e: e.tensor_reduce(out=lamv[:, 2:3], in_=lamt[:, 0, :], axis=AX.X, op=ALU.add), reads=[blamt], writes=[blamv])
        P.op(DVE, lambda e: e.tensor_reduce(out=lamv[:, 3:4], in_=lamt[:, 2, :], axis=AX.X, op=ALU.add), reads=[blamt], writes=[blamv])
        P.op(ACT, lambda e: e.activation(out=lamv[:, 4:6], in_=lamv[:, 2:4], func=AF.Exp), reads=[blamv], writes=[blamv])
        P.op(DVE, lambda e: e.scalar_tensor_tensor(out=lamv[:, 0:1], in0=lamv[:, 5:6], scalar=float(-LAMBDA_INIT), in1=lamv[:, 4:5],
                                                   op0=ALU.add, op1=ALU.subtract), reads=[blamv], writes=[blamv])
        P.op(POOL, lambda e: e.memset(mhalf[:, :], -0.5), writes=[bmhalf])
        P.dma(POOL, gsub[:, :], subg_d.partition_broadcast(128), bgsub, writes=[bgsub])
        P.op(POOL, lambda e: e.tensor_scalar(out=gsub[:, :], in0=gsub[:, :], scalar1=float((1.0 - LAMBDA_INIT) * math.sqrt(128.0)),
                                             scalar2=None, op0=ALU.mult), reads=[bgsub], writes=[bgsub])
        P.op(POOL, lambda e: e.memset(Vc[:, :, :, :].rearrange("p a b c -> p (a b c)"), 1.0), writes=bV)
        P.op(POOL, lambda e: e.memset(Sst[:, :], 0.0), writes=bS)
        P.op(POOL, lambda e: e.memset(Sbf[:, :], 0.0), writes=bSbf)
        neglam = lamv[:, 0:1]

        def rsqrt_pool(dst, bdst, src, bsrc, n=1):
            P.op(POOL, lambda e: e.tensor_tensor(out=dst, in0=src, in1=mhalf[:, 0:n], op=ALU.pow),
                 reads=[bsrc, bmhalf], writes=[bdst])

        xnb = [sb("xnb%d" % i, [128, D], BF16) for i in range(NSUB)]
        bxnb = [Buf("xnb%d" % i) for i in range(NSUB)]
        XB = [xs_, stage]
        bXB = [bx, bstage]

        def pn_elem(s, X, bX):
            ss, bss = sm()
            se, bse = sm()
            r, br = sm()
            xb_ = xnb[s]
            P.op(ACT, lambda e: e.activation(out=xb_[:, :], in_=X[s][:, :], func=AF.Square, accum_out=ss),
                 reads=[bX[s]], writes=[bxnb[s], bss])
            P.op(DVE, lambda e: e.tensor_scalar(out=se, in0=ss, scalar1=float(NORM_EPS * D), scalar2=None, op0=ALU.add),
                 reads=[bss], writes=[bse])
            rsqrt_pool(r, br, se, bse)
            P.op(ACT, lambda e: e.activation(out=xb_[:, :], in_=X[s][:, :], func=AF.Copy, scale=r),
                 reads=[bX[s], br], writes=[bxnb[s]])

        def pn_pe(gi, s, bank):
            xb_ = xnb[s]
            for k in range(8):
                P.op(PE, lambda e, k=k: e.transpose(out=pbh[bank][:, k * 128:(k + 1) * 128], in_=xb_[:, k * 128:(k + 1) * 128], identity=ident),
                     reads=[bxnb[s], bcbf], writes=[bpb[bank]])
            P.op(DVE, lambda e: e.tensor_tensor(
                out=Bp[:, 0:8, s * 128:(s + 1) * 128],
                in0=pbh[bank][:, :].rearrange("p (k t) -> p k t", k=8),
                in1=gT[:, gi * 8:(gi + 1) * 8].unsqueeze(2).broadcast_to([128, 8, 128]), op=ALU.mult),
                reads=[bpb[bank], bgT], writes=bB[0:8])

        def load_gpost(name, coef):
            P.dma(POOL, gpost[:, :], gains[name].partition_broadcast(128), bgpost, writes=[bgpost])
            P.op(ACT, lambda e: e.activation(out=gpost[:, :], in_=gpost[:, :], func=AF.Copy, scale=float(coef * math.sqrt(D))),
                 reads=[bgpost], writes=[bgpost])

        def post_norm(s, bankA, bankB, fscale, X, bX):
            ssA, bssA = sm()
            ssB, bssB = sm()
            se, bse = sm()
            r, br = sm()
            for (bank, ss, bss) in ((bankA, ssA, bssA), (bankB, ssB, bssB)):
                junk, bjunk = tf()
                P.op(ACT, lambda e, bank=bank, ss=ss, junk=junk: e.activation(out=junk.bitcast(BF16)[:, 0:512], in_=pb[bank][:, :],
                                                                            func=AF.Square, accum_out=ss),
                     reads=[bpb[bank]], writes=[bjunk, bss])
            P.op(DVE, lambda e: e.tensor_scalar(out=se, in0=ssA, scalar1=ssB, scalar2=float(NORM_EPS * D / (fscale * fscale)),
                                                op0=ALU.add, op1=ALU.add), reads=[bssA, bssB], writes=[bse])
            rsqrt_pool(r, br, se, bse)
            for hf, bank in enumerate((bankA, bankB)):
                t, bt = tf()
                P.op(DVE, lambda e, bank=bank, t=t, hf=hf: e.scalar_tensor_tensor(out=t[:, :], in0=pb[bank][:, :], scalar=r,
                                                                                  in1=gpost[:, hf * 512:(hf + 1) * 512],
                                                                                  op0=ALU.mult, op1=ALU.mult),
                     reads=[bpb[bank], br, bgpost], writes=[bt])
                P.op(DVE, lambda e, t=t, hf=hf: e.tensor_tensor(out=X[s][:, hf * 512:(hf + 1) * 512], in0=X[s][:, hf * 512:(hf + 1) * 512],
                                                                in1=t[:, :], op=ALU.add),
                     reads=[bt, bX[s]], writes=[bX[s]])

        def ffn(pre, post_name, X, bX, early=None, early_pe=None, after_post=None, mid_b=None, tail=None):
            U0 = 8
            for fi in range(11):
                wg, bwg = get_piece((pre, "g", fi))
                wu, bwu = get_piece((pre, "u", fi))
                for c in range(2):
                    f = 2 * fi + c
                    gb_, ub_ = f % 2, 2 + f % 2
                    for (w, bw, bank) in ((wg, bwg, gb_), (wu, bwu, ub_)):
                        for k in range(8):
                            P.op(PE, lambda e, w=w, bank=bank, k=k, c=c: e.matmul(out=pb[bank][:, :], lhsT=w[:, k * 256 + c * 128:k * 256 + c * 128 + 128],
                                                                               rhs=Bv(k), start=(k == 0), stop=(k == 7)),
                                 reads=[bw, bB[k]], writes=[bpb[bank]])
                    t, bt = tf()
                    v, bv = tf()
                    P.op(ACT, lambda e, t=t, gb_=gb_: e.activation(out=t[:, :], in_=pb[gb_][:, :], func=AF.Tanh, scale=0.5),
                         reads=[bpb[gb_]], writes=[bt])
                    P.op(DVE, lambda e, t=t, v=v, gb_=gb_: e.scalar_tensor_tensor(out=v[:, :], in0=t[:, :], scalar=1.0, in1=pb[gb_][:, :],
                                                                                  op0=ALU.add, op1=ALU.mult),
                         reads=[bt, bpb[gb_]], writes=[bv])
                    P.op(DVE, lambda e, v=v, ub_=ub_, f=f: e.tensor_tensor(out=Bv(U0 + f), in0=v[:, :], in1=pb[ub_][:, :], op=ALU.mult),
                         reads=[bv, bpb[ub_]], writes=[bB[U0 + f]])
            load_gpost(post_name, 0.5)
            for ps_ in range(2):
                base = 4 if ps_ == 0 else 0
                for fi in range(11):
                    if ps_ == 0 and fi == 0 and early is not None:
                        early()
                    if ps_ == 0 and fi == 6 and early_pe is not None:
                        early_pe()
                    if ps_ == 1 and fi == 5 and mid_b is not None:
                        mid_b()
                    wd, bwd = get_piece((pre, "d", fi))
                    for c in range(2):
                        f = 2 * fi + c
                        for sl in range(2):
                            s = 2 * ps_ + sl
                            for hf in range(2):
                                bank = base + sl * 2 + hf
                                P.op(PE, lambda e, wd=wd, bank=bank, f=f, s=s, c=c, hf=hf: e.matmul(
                                    out=pb[bank][:, :], lhsT=Bp[:, U0 + f, s * 128:(s + 1) * 128],
                                    rhs=wd[:, c * 1024 + hf * 512:c * 1024 + hf * 512 + 512], start=(f == 0), stop=(f == 21)),
                                    reads=[bwd, bB[U0 + f]], writes=[bpb[bank]])
                for sl in range(2):
                    s = 2 * ps_ + sl
                    post_norm(s, base + sl * 2, base + sl * 2 + 1, 0.5, X, bX)
                if after_post is not None:
                    for sl in range(2):
                        after_post(2 * ps_ + sl)
            if tail is not None:
                tail()

        QT0, DAT0, YRT0, MG0, PT0 = 16, 20, 24, 8, 28
        cnt = dict(pt=0, rot=0)


        PTS = [28, 29, 30, 31]

        def att_gen(j):
            pend_f2 = None
            pend_fin = None
            for q_ in (16, 18):
                P.op(POOL, lambda e, q_=q_: e.memset(Bp[64:128, q_, :], 0.0), writes=[bB[q_]])
                P.op(POOL, lambda e, q_=q_: e.memset(Bp[0:64, q_ + 1, :], 0.0), writes=[bB[q_ + 1]])
            for h in range(4):
                wa, bwa = get_piece(("A", h))
                qi = QT0 + (h % 2)
                for part in range(2):
                    for k in range(8):
                        P.op(PE, lambda e, part=part, k=k, wa=wa: e.matmul(out=pb[part][:, :], lhsT=wa[:, k * 256 + part * 128:k * 256 + part * 128 + 128],
                                                                           rhs=Bv(k), start=(k == 0), stop=(k == 7)),
                             reads=[bwa, bB[k]], writes=[bpb[part]])
                qz = (QT0 + 2 * (h % 2), QT0 + 2 * (h % 2) + 1)
                P.op(ACT, lambda e, qz=qz: e.activation(out=Bp[0:64, qz[0], :], in_=pb[0][0:64, :], func=AF.Copy), reads=[bpb[0]], writes=[bB[qz[0]]])
                P.op(DVE, lambda e, qz=qz: e.tensor_copy(out=Bp[64:128, qz[1], :], in_=pb[0][64:128, :]), reads=[bpb[0]], writes=[bB[qz[1]]])
                P.op(DVE, lambda e, h=h: e.tensor_copy(out=Kc[:, h, j * TT:(j + 1) * TT], in_=pb[1][:, :]), reads=[bpb[1]], writes=[bK[j][h]])
                yield
                if pend_f2 is not None:
                    pend_f2()
                    pend_f2 = None
                nkt = 4 * j + 4

                def acc(c, qs, lo, hi):
                    return pb[2 + 2 * c + qs // 2][:, (qs % 2) * 256 + lo:(qs % 2) * 256 + hi]

                info = {}

                def stageA(kt, c, info=info, h=h, qz=qz):
                    cdiag = kt - 4 * j
                    q0 = max(cdiag, 0)
                    ncols = 512 - q0 * 128
                    jk = kt // 4
                    qi = qz[c]
                    P.op(PE, lambda e: e.matmul(out=pb[c][:, 0:ncols], lhsT=Kc[:, h, kt * 128:(kt + 1) * 128],
                                                rhs=Bp[:, qi, q0 * 128:512], start=True, stop=True),
                         reads=[bK[jk][h], bB[qi]], writes=[bpb[c]])
                    pt = PTS[cnt["pt"] % 4]
                    cnt["pt"] += 1
                    P.op(ACT, lambda e: e.activation(out=Bp[:, pt, 0:ncols], in_=pb[c][:, 0:ncols], func=AF.Exp, scale=0.125),
                         reads=[bpb[c]], writes=[bB[pt]])
                    if cdiag >= 0:
                        P.op(POOL, lambda e: e.tensor_tensor(out=Bp[:, pt, 0:128], in0=Bp[:, pt, 0:128], in1=tri, op=ALU.mult),
                             reads=[bB[pt], bcbf], writes=[bB[pt]])
                    info[(kt, c)] = (pt, q0)

                def stageB(kt, c, info=info, h=h):
                    pt, q0 = info[(kt, c)]
                    for qs in range(q0, 4):
                        off = (qs - q0) * 128
                        P.op(PE, lambda e, qs=qs, off=off: e.matmul(
                            out=acc(c, qs, 0, 129), lhsT=Bp[:, pt, off:off + 128], rhs=Vc[:, kt, h, 0:129],
                            start=(kt == 0 and qs % 2 == 0), stop=(kt == 4 * j + qs), skip_group_check=True),
                            reads=[bB[pt], bV[kt]], writes=[bpb[2 + 2 * c + qs // 2]])

                stageA(0, 0)
                stageA(0, 1)
                for kt in range(nkt):
                    for c in range(2):
                        if kt + 1 < nkt:
                            stageA(kt + 1, c)
                        stageB(kt, c)
                    if kt == min(1, nkt - 1) and pend_fin is not None:
                        pend_fin()
                        pend_fin = None
                    yield
                for c in range(2):
                    for qs in range(4):
                        rl, brl = sm()
                        ab = bpb[2 + 2 * c + qs // 2]
                        P.op(DVE, lambda e, rl=rl, qs=qs, c=c: e.reciprocal(out=rl, in_=acc(c, qs, 128, 129)), reads=[ab], writes=[brl])
                        if c == 0:
                            P.op(DVE, lambda e, rl=rl, qs=qs, c=c: e.tensor_scalar(out=yaE[:, qs, :], in0=acc(c, qs, 0, 128), scalar1=rl, scalar2=None, op0=ALU.mult),
                                 reads=[ab, brl], writes=[byaE[qs]])
                        else:
                            cf, bcf = sm()
                            P.op(DVE, lambda e, rl=rl, cf=cf: e.tensor_tensor(out=cf, in0=rl, in1=neglam, op=ALU.mult), reads=[brl, blamv], writes=[bcf])
                            P.op(DVE, lambda e, cf=cf, qs=qs, c=c: e.scalar_tensor_tensor(out=yaE[:, qs, :], in0=acc(c, qs, 0, 128), scalar=cf, in1=yaE[:, qs, :],
                                                                                        op0=ALU.mult, op1=ALU.add),
                                 reads=[ab, bcf, byaE[qs]], writes=[byaE[qs]])

                def f2():
                    jk_, bjk_ = tf()
                    sc = [(sm(), sm(), sm()) for _ in range(4)]
                    for qs in range(4):
                        ss, bss = sc[qs][0]
                        P.op(DVE, lambda e, ss=ss, qs=qs: e.scalar_tensor_tensor(out=jk_[:, qs * 128:(qs + 1) * 128], in0=yaE[:, qs, :], scalar=1.0,
                                                                                in1=yaE[:, qs, :], op0=ALU.mult, op1=ALU.mult, accum_out=ss),
                             reads=[byaE[qs]], writes=[bjk_, bss])
                    for qs in range(4):
                        (ss, bss), (se, bse), _ = sc[qs]
                        P.op(DVE, lambda e, se=se, ss=ss: e.tensor_scalar(out=se, in0=ss, scalar1=float(SUBLN_EPS * 128), scalar2=None, op0=ALU.add),
                             reads=[bss], writes=[bse])
                    for qs in range(4):
                        _, (se, bse), (r, br) = sc[qs]
                        rsqrt_pool(r, br, se, bse)
                    for qs in range(4):
                        r, br = sc[qs][2]
                        P.op(DVE, lambda e, r=r, qs=qs: e.scalar_tensor_tensor(out=datok[:, qs, :], in0=yaE[:, qs, :], scalar=r, in1=gsub[:, :],
                                                                              op0=ALU.mult, op1=ALU.mult),
                             reads=[byaE[qs], br, bgsub], writes=[bdatok[qs]])

                def fin(h=h):
                    for qs in range(4):
                        P.op(PE, lambda e, qs=qs: e.transpose(out=pbh[1][:, qs * 128:(qs + 1) * 128], in_=datok[:, qs, :], identity=ident),
                             reads=[bdatok[qs], bcbf], writes=[bpb[1]])
                    P.op(ACT, lambda e: e.activation(out=Bv(DAT0 + h), in_=pbh[1][:, 0:512], func=AF.Copy), reads=[bpb[1]], writes=[bB[DAT0 + h]])
                pend_f2, pend_fin = f2, fin
                yield
            if pend_f2 is not None:
                pend_f2()
            if pend_fin is not None:
                pend_fin()

        def ret_gen(j, heads, p):
            pj = mb = 6 + p
            rs = rsm[p]
            brs = brsm[p]
            g1 = g1t[p]
            rt, brt = rott[p], brott[p]
            bA, bBk = bpbr[p]["A"], bpbr[p]["B"]
            bq = bpbr[p]
            qkrot, qkT = rs[:, 0:256], rs[:, 256:512]
            vtok, vz, sTm, yr = rs[:, 512:640], rs[:, 640:768], rs[:, 768:896], rs[:, 896:1024]
            for h in heads:
                wr, bwr = get_piece(("R", h))
                wrv, bwrv = get_piece(("RV", h))
                cd = float(RET_GAMMA[h] ** 128)
                hs = slice(h * 128, (h + 1) * 128)
                for s in range(NSUB):
                    tok = j * TT + s * 128
                    P.dma(POOL, rt[:, :], rot_d[h * T + tok:h * T + tok + 128, :], brt, writes=[brt])
                    for (w, bw, lo, bb) in ((wr, bwr, 0, bA), (wrv, bwrv, 256, bBk)):
                        for k in range(8):
                            P.op(PE, lambda e, w=w, lo=lo, k=k, s=s: e.matmul(
                                out=pb[pj][:, lo:lo + 256], lhsT=Bp[:, k, s * 128:(s + 1) * 128], rhs=w[:, k * 256:(k + 1) * 256],
                                start=(k == 0), stop=(k == 7)), reads=[bw, bB[k]], writes=[bb])
                    t1, bt1 = tf()
                    t2, bt2 = tf()
                    P.op(DVE, lambda e, t1=t1: e.tensor_tensor(out=t1[:, 0:256], in0=pb[pj][:, 0:256], in1=rt[:, 0:256], op=ALU.mult),
                         reads=[bA, brt], writes=[bt1])
                    P.op(DVE, lambda e, t2=t2: e.tensor_tensor(out=t2[:, 0:256:2], in0=pb[pj][:, 1:256:2], in1=rt[:, 256:512:2], op=ALU.mult),
                         reads=[bA, brt], writes=[bt2])
                    P.op(DVE, lambda e, t2=t2: e.tensor_tensor(out=t2[:, 1:256:2], in0=pb[pj][:, 0:256:2], in1=rt[:, 257:512:2], op=ALU.mult),
                         reads=[bA, brt], writes=[bt2])
                    P.op(DVE, lambda e, t1=t1, t2=t2: e.tensor_tensor(out=qkrot, in0=t1[:, 0:256], in1=t2[:, 0:256], op=ALU.add),
                         reads=[bt1, bt2], writes=[brs["qkrot"]])
                    P.op(DVE, lambda e: e.tensor_copy(out=vtok, in_=pb[pj][:, 256:384]), reads=[bBk], writes=[brs["vtok"]])
                    tg, btg = tf()
                    P.op(ACT, lambda e, tg=tg: e.activation(out=tg[:, 0:128], in_=pb[pj][:, 384:512], func=AF.Tanh, scale=0.5),
                         reads=[bBk], writes=[btg])
                    P.op(DVE, lambda e, tg=tg: e.scalar_tensor_tensor(out=g1[:, :], in0=tg[:, 0:128], scalar=1.0, in1=pb[pj][:, 384:512],
                                                                      op0=ALU.add, op1=ALU.mult),
                         reads=[btg, bBk], writes=[brs["g1"]])
                    yield
                    for i in range(2):
                        P.op(PE, lambda e, i=i: e.transpose(out=pbh[pj][:, i * 128:(i + 1) * 128], in_=rs[:, i * 128:(i + 1) * 128], identity=ident),
                             reads=[brs["qkrot"], bcbf], writes=[bA])
                    P.op(DVE, lambda e: e.tensor_copy(out=qkT, in_=pbh[pj][:, 0:256]), reads=[bA], writes=[brs["qkT"]])
                    yield
                    P.op(PE, lambda e: e.matmul(out=pb[mb][:, 256:384], lhsT=rs[:, 384:512], rhs=rs[:, 256:384], start=True, stop=True),
                         reads=[brs["qkT"]], writes=[bq["S"]])
                    P.op(DVE, lambda e: e.tensor_tensor(out=sTm, in0=pb[mb][:, 256:384], in1=cst[:, C_TRI:C_TRI + 128], op=ALU.mult),
                         reads=[bq["S"], bcst], writes=[brs["sTm"]])
                    yield
                    P.op(PE, lambda e: e.matmul(out=pb[mb][:, 0:128], lhsT=sTm, rhs=vtok, start=True, stop=False),
                         reads=[brs["sTm"], brs["vtok"]], writes=[bq["I"]])
                    P.op(PE, lambda e, hs=hs: e.matmul(out=pb[mb][:, 0:128], lhsT=rs[:, 256:384], rhs=Sbf[:, hs], start=False, stop=True),
                         reads=[brs["qkT"], bSbf[h]], writes=[bq["I"]])
                    P.op(PE, lambda e: e.matmul(out=pb[mb][:, 384:512], lhsT=rs[:, 128:256], rhs=vtok, start=True, stop=True),
                         reads=[brs["qkrot"], brs["vtok"]], writes=[bq["KV"]])
                    P.op(DVE, lambda e, hs=hs, cd=cd: e.scalar_tensor_tensor(out=Sst[:, hs], in0=Sst[:, hs], scalar=cd, in1=pb[mb][:, 384:512],
                                                                           op0=ALU.mult, op1=ALU.add),
                         reads=[bS[h], bq["KV"]], writes=[bS[h]])
                    P.op(POOL, lambda e, hs=hs, cd=cd: e.tensor_scalar(out=Sbf[:, hs], in0=Sst[:, hs], scalar1=cd, scalar2=None, op0=ALU.mult),
                         reads=[bS[h]], writes=[bSbf[h]])
                    jk2, bjk2 = tf()
                    ss, bss = sm()
                    se, bse = sm()
                    r2, br2 = sm()
                    cn = 0.5 * math.sqrt(128.0)
                    P.op(ACT, lambda e, jk2=jk2, ss=ss: e.activation(out=jk2[:, 0:128], in_=pb[mb][:, 0:128], func=AF.Square, scale=float(1.0 / cn), accum_out=ss),
                         reads=[bq["I"]], writes=[bjk2, bss])
                    P.op(DVE, lambda e, se=se, ss=ss: e.tensor_scalar(out=se, in0=ss, scalar1=float(NORM_EPS * 128 / (cn * cn)), scalar2=None, op0=ALU.add),
                         reads=[bss], writes=[bse])
                    rsqrt_pool(r2, br2, se, bse)
                    P.op(DVE, lambda e, r2=r2: e.scalar_tensor_tensor(out=yr, in0=pb[mb][:, 0:128], scalar=r2, in1=g1[:, :], op0=ALU.mult, op1=ALU.mult),
                         reads=[bq["I"], br2, brs["g1"]], writes=[brs["yr"]])
                    yield
                    yield
                    P.op(PE, lambda e: e.transpose(out=pbh[pj][:, 512:640], in_=yr, identity=ident), reads=[brs["yr"], bcbf], writes=[bBk])
                    P.op(DVE, lambda e, h=h, s=s: e.tensor_copy(out=Bp[:, YRT0 + h, s * 128:(s + 1) * 128], in_=pbh[pj][:, 512:640]),
                         reads=[bBk], writes=[bB[YRT0 + h]])
                    yield

        def mixer(j, X, bX, pre_tail=None):
            wv = [get_piece(("V", i)) for i in range(2)]

            def vproj(s):
                bank = s
                for i in range(2):
                    for k in range(8):
                        P.op(PE, lambda e, bank=bank, i=i, k=k, s=s: e.matmul(
                            out=pb[bank][:, i * 256:(i + 1) * 256], lhsT=Bp[:, k, s * 128:(s + 1) * 128],
                            rhs=wv[i][0][:, k * 256:(k + 1) * 256], start=(k == 0), stop=(k == 7)),
                            reads=[wv[i][1], bB[k]], writes=[bpb[bank]])
                kt = 4 * j + s
                P.op(ACT, lambda e, bank=bank, kt=kt: e.activation(out=Vc[:, kt, :, 0:128],
                                                                   in_=pb[bank][:, :].rearrange("p (h e) -> p h e", h=4), func=AF.Copy),
                     reads=[bpb[bank]], writes=[bV[kt]])

            gens = [ret_gen(j, (0, 2), 0), ret_gen(j, (1, 3), 1)]
            vproj(0)
            vproj(1)
            for g_ in gens:
                next(g_)
            if pre_tail is not None:
                pre_tail()
            vproj(2)
            vproj(3)

            n_att = 4 * (4 * j + 6)
            ratio = max(1, int(n_att / 48.0 + 0.5))
            ga_ = att_gen(j)
            att_live = True
            while gens or att_live:
                for _ in range(ratio if gens else 1):
                    if att_live:
                        try:
                            next(ga_)
                        except StopIteration:
                            att_live = False
                for g_ in list(gens):
                    try:
                        next(g_)
                    except StopIteration:
                        gens.remove(g_)
                if conv_gen[0] is not None:
                    next(conv_gen[0], None)

            load_gpost("mix_post_g", 1.0)
            gate_w = {}
            tq = {}

            def stage1(g):
                if g % 2 == 0:
                    gate_w["ga"] = get_piece(("GA", g // 2))
                    gate_w["gr"] = get_piece(("GR", g // 2))
                par = (g % 2) * 4
                cc = g % 2
                ts_ = []
                for (w, bank) in ((gate_w["ga"], par + 2), (gate_w["gr"], par + 3)):
                    for k in range(8):
                        P.op(PE, lambda e, w=w, bank=bank, k=k, cc=cc: e.matmul(
                            out=pb[bank][:, :], lhsT=w[0][:, k * 256 + cc * 128:k * 256 + cc * 128 + 128], rhs=Bv(k),
                            start=(k == 0), stop=(k == 7)), reads=[w[1], bB[k]], writes=[bpb[bank]])
                    t, bt = tf()
                    P.op(ACT, lambda e, t=t, bank=bank: e.activation(out=t[:, :], in_=pb[bank][:, :], func=AF.Tanh, scale=0.5),
                         reads=[bpb[bank]], writes=[bt])
                    ts_.append((t, bt))
                tq[g] = ts_

            proj_w = {}

            def stage2(g):
                if g % 4 == 0:
                    proj_w["ap"] = get_piece(("AP", g // 4))
                    proj_w["rp"] = get_piece(("RP", g // 4))
                par = (g % 2) * 4
                cg = g % 4
                for (w, bank, src0) in ((proj_w["ap"], par + 0, DAT0), (proj_w["rp"], par + 1, YRT0)):
                    for c4 in range(4):
                        P.op(PE, lambda e, w=w, bank=bank, src0=src0, c4=c4, cg=cg: e.matmul(
                            out=pb[bank][:, :], lhsT=w[0][:, c4 * 512 + cg * 128:c4 * 512 + cg * 128 + 128], rhs=Bv(src0 + c4),
                            start=(c4 == 0), stop=(c4 == 3)), reads=[w[1], bB[src0 + c4]], writes=[bpb[bank]])
                ts_ = tq.pop(g)
                for (t, bt), ybank in zip(ts_, (par + 0, par + 1)):
                    P.op(DVE, lambda e, t=t, ybank=ybank: e.scalar_tensor_tensor(out=t[:, :], in0=t[:, :], scalar=1.0, in1=pb[ybank][:, :],
                                                                              op0=ALU.add, op1=ALU.mult),
                         reads=[bt, bpb[ybank]], writes=[bt])
                P.op(DVE, lambda e, ts_=ts_, g=g: e.tensor_tensor(out=Bv(MG0 + g), in0=ts_[0][0][:, :], in1=ts_[1][0][:, :], op=ALU.add),
                     reads=[ts_[0][1], ts_[1][1]], writes=[bB[MG0 + g]])

            stage1(0)
            for g in range(8):
                if g + 1 < 8:
                    stage1(g + 1)
                stage2(g)
            wo = [get_piece(("WO", i)) for i in range(4)]
            for s in range(NSUB):
                bA, bB2 = (s % 2) * 2, (s % 2) * 2 + 1
                for i in range(4):
                    bank = bA if i < 2 else bB2
                    col = (i % 2) * 256
                    for k in range(8):
                        P.op(PE, lambda e, bank=bank, col=col, k=k, s=s, i=i: e.matmul(
                            out=pb[bank][:, col:col + 256], lhsT=Bp[:, MG0 + k, s * 128:(s + 1) * 128], rhs=wo[i][0][:, k * 256:(k + 1) * 256],
                            start=(k == 0), stop=(k == 7)), reads=[wo[i][1], bB[MG0 + k]], writes=[bpb[bank]])
                if s >= 2:
                    pn_pe(2, s - 2, 4 + (s - 2) % 2)
                post_norm(s, bA, bB2, 0.5, X, bX)
                pn_elem(s, X, bX)
            pn_pe(2, NSUB - 2, 4 + (NSUB - 2) % 2)
            pn_pe(2, NSUB - 1, 4 + (NSUB - 1) % 2)

        stores = []
        conv_gen = [None]

        def load_x(j, X, bX):
            for s in range(NSUB):
                r0 = (j * NSUB + s) * 128
                P.dma(POOL, X[s][:, :], x_d[r0:r0 + 128, :], bX[s], writes=[bX[s]])

        load_x(0, XB[0], bXB[0])
        for s in range(NSUB):
            pn_elem(s, XB[0], bXB[0])
            pn_pe(0, s, 6 + s % 2)
        for j in range(NTILES):
            X, bX = XB[j % 2], bXB[j % 2]
            Xn, bXn = XB[(j + 1) % 2], bXB[(j + 1) % 2]
            last = (j == NTILES - 1)

            def f1_after(s, X=X, bX=bX):
                pn_elem(s, X, bX)

            def f1_mid():
                pn_pe(1, 0, 4)
                pn_pe(1, 1, 6)

            def f1_tail():
                pn_pe(1, 2, 5)
                pn_pe(1, 3, 7)

            def f1_early():
                conv_gen[0] = convert_rest()
                for _ in range(14):
                    next(conv_gen[0], None)

            ffn("ffn1", "ffn1_post_g", X, bX, early=f1_early if j == 0 else None, after_post=f1_after, mid_b=f1_mid)
            if debug:
                for s in range(NSUB):
                    r0 = (j * NSUB + s) * 128
                    stores.append(P.dma(POOL, dbg_d["dbg1"][r0:r0 + 128, :], X[s][:, :], bX[s], reads=[bX[s]]))
            mixer(j, X, bX, pre_tail=f1_tail)
            if conv_gen[0] is not None:
                for _ in conv_gen[0]:
                    pass
                conv_gen[0] = None
            if debug:
                for s in range(NSUB):
                    r0 = (j * NSUB + s) * 128
                    stores.append(P.dma(POOL, dbg_d["dbg2"][r0:r0 + 128, :], X[s][:, :], bX[s], reads=[bX[s]]))
            overlap = (not last) and j >= 1
            if overlap:
                load_x(j + 1, Xn, bXn)

            def f2_early(Xn=Xn, bXn=bXn):
                for s in range(NSUB):
                    pn_elem(s, Xn, bXn)

            def f2_early_pe():
                for s in range(NSUB):
                    pn_pe(0, s, s)

            def f2_after(s, j=j, X=X, bX=bX):
                r0 = (j * NSUB + s) * 128
                stores.append(P.dma(POOL, out_d[r0:r0 + 128, :], X[s][:, :], bX[s], reads=[bX[s]]))

            ffn("ffn2", "ffn2_post_g", X, bX, early=f2_early if overlap else None, early_pe=f2_early_pe if overlap else None,
                after_post=f2_after)
            if (not last) and not overlap:
                load_x(j + 1, Xn, bXn)
                for s in range(NSUB):
                    pn_elem(s, Xn, bXn)
                    pn_pe(0, s, 6 + s % 2)
        if debug:
            print("SBUF bytes/partition", sbuf_bytes[0])
        P.final_wait(POOL, stores)
        P.emit()
    return nc


W2D = ("ffn1_w_gate", "ffn1_w_up", "ffn1_w_down", "w_in", "w_attn_proj", "w_ret_proj", "w_out",
       "ffn2_w_gate", "ffn2_w_up", "ffn2_w_down")
VECS = ("ffn1_pre_g", "ffn1_post_g", "mix_pre_g", "mix_post_g", "ffn2_pre_g", "ffn2_post_g",
        "lambda_q1", "lambda_k1", "lambda_q2", "lambda_k2", "diff_subln_g")


def make_in_maps(inputs, T, ncores, debug=False):
    cst, rot = _const_tables(T)
    shared = {"cst": cst, "rot": rot}
    for n in W2D:
        shared[n] = np.ascontiguousarray(np.asarray(inputs[n], dtype=np.float32)[0])
    for n in VECS:
        shared[n] = np.ascontiguousarray(np.asarray(inputs[n], dtype=np.float32).reshape(1, -1))
    x = np.asarray(inputs["x"], dtype=np.float32)
    maps = []
    for c in range(ncores):
        m = dict(shared)
        m["x"] = np.ascontiguousarray(x[c, :T])
        maps.append(m)
    return maps


def kernel(**inputs):
    nc = build(SEQ)
    in_maps = make_in_maps(inputs, SEQ, NCORES)
    res = run_bass_kernel_spmd(nc, in_maps, core_ids=list(range(NCORES)))
    out = np.stack([np.asarray(res.results[c]["out"]) for c in range(NCORES)], axis=0)
    return out.astype(np.float32, copy=False)
```

```python
import math
from contextlib import ExitStack

import numpy as np
import concourse.bass as bass
import concourse.mybir as mybir
from concourse.bass_utils import run_bass_kernel_spmd

F32 = mybir.dt.float32
BF16 = mybir.dt.bfloat16
AF = mybir.ActivationFunctionType
ALU = mybir.AluOpType
AX = mybir.AxisListType

D = 1024
DFF = 2816
SEQ = 4096
NCORES = 8
TT = 512
NSUB = 4
LAMBDA_INIT = 0.8 - 0.6 * math.exp(-0.3 * 0)
NORM_EPS = 1e-6
SUBLN_EPS = 1e-5
RET_GAMMA = [1.0 - 2.0 ** (-5.0 - h) for h in range(4)]

PE, ACT, DVE, POOL, SP = "tensor", "scalar", "vector", "gpsimd", "sync"
ENGS = (PE, ACT, DVE, POOL, SP)


class Buf:
    __slots__ = ("name", "w", "r", "sem", "dcnt", "excl")

    def __init__(self, name, excl=False):
        self.name = name
        self.excl = excl
        self.w = None
        self.r = []
        self.sem = None
        self.dcnt = 0


class Op:
    __slots__ = ("eng", "fn", "waits", "need_inc", "is_dma", "sem", "val")

    def __init__(self, eng, fn, is_dma=False):
        self.eng = eng
        self.fn = fn
        self.waits = []
        self.need_inc = False
        self.is_dma = is_dma
        self.sem = None
        self.val = None


class Prog:
    def __init__(self, nc, stack, n_dma_sems=48):
        self.nc = nc
        self.ops = {e: [] for e in ENGS}
        self.esem = {e: stack.enter_context(nc.semaphore("es_" + e)) for e in ENGS}
        self.free_sems = [stack.enter_context(nc.semaphore("ds%d" % i)) for i in range(n_dma_sems)]

    def _dep(self, op, prev, same_ok):
        if prev is None or prev is op:
            return
        if (not prev.is_dma) and (not op.is_dma) and prev.eng == op.eng == PE:
            return
        if not prev.is_dma:
            prev.need_inc = True
        op.waits.append(prev)

    def _track(self, op, reads, writes):
        ex = [b for b in reads if b.excl]
        if ex:
            reads = [b for b in reads if not b.excl]
            writes = list(writes) + [b for b in ex if b not in writes]
        for b in reads:
            self._dep(op, b.w, same_ok=(op.eng == PE))
        for b in writes:
            self._dep(op, b.w, same_ok=True)
            for r in b.r:
                self._dep(op, r, same_ok=True)
        for b in reads:
            b.r.append(op)
        for b in writes:
            b.w = op
            b.r = []

    def op(self, eng, fn, reads=(), writes=()):
        o = Op(eng, fn)
        self._track(o, reads, writes)
        self.ops[eng].append(o)
        return o

    def dma(self, queue, out_ap, in_ap, sembuf, reads=(), writes=(), **kw):
        if sembuf.sem is None:
            sembuf.sem = {}
            sembuf.dcnt = {}
        if queue not in sembuf.sem:
            sembuf.sem[queue] = self.free_sems.pop()
            sembuf.dcnt[queue] = 0
        sembuf.dcnt[queue] += 1
        o = Op(queue, lambda eng: eng.dma_start(out=out_ap, in_=in_ap, **kw), is_dma=True)
        o.sem = sembuf.sem[queue]
        o.val = 16 * sembuf.dcnt[queue]
        self._track(o, reads, writes)
        self.ops[queue].append(o)
        return o

    def final_wait(self, eng, ops):
        o = Op(eng, None)
        for p in ops:
            self._dep(o, p, same_ok=False)
        self.ops[eng].append(o)

    def emit(self):
        nc = self.nc
        for e in ENGS:
            cnt = 0
            for o in self.ops[e]:
                if (not o.is_dma) and o.need_inc:
                    cnt += 1
                    o.val = cnt
                    o.sem = self.esem[e]
        with nc.Block() as block:
            for e in ENGS:
                def body(eng, e=e):
                    seen = {}
                    for o in self.ops[e]:
                        for w in o.waits:
                            k = id(w.sem)
                            if seen.get(k, 0) >= w.val:
                                continue
                            eng.wait_ge(w.sem, w.val)
                            seen[k] = w.val
                        if o.fn is None:
                            continue
                        ins = o.fn(eng)
                        if o.is_dma:
                            ins.then_inc(o.sem, 16)
                        elif o.need_inc:
                            ins.then_inc(o.sem, 1)
                getattr(block, e)(body)


C_ID, C_TRI, C_END = 0, 128, 256


def _const_tables(T):
    cst = np.zeros((128, C_END), np.float32)
    idx = np.arange(128)
    cst[:, C_ID:C_ID + 128] = np.eye(128, dtype=np.float32)
    cst[:, C_TRI:C_TRI + 128] = (idx[None, :] >= idx[:, None]).astype(np.float32)
    angle = (1.0 / (10000.0 ** np.linspace(0.0, 1.0, 64, dtype=np.float32))).astype(np.float32)
    angle = np.repeat(angle, 2)
    ang = (np.arange(T, dtype=np.float32)[:, None] * angle[None, :]).astype(np.float32)
    sn, cs = np.sin(ang).astype(np.float32), np.cos(ang).astype(np.float32)
    sgn = np.where(np.arange(128) % 2 == 0, -1.0, 1.0).astype(np.float32)
    ks = np.float32(128.0 ** -0.5)
    tl = (np.arange(T) % 128).astype(np.float64)
    rots = []
    for h in range(4):
        lg64 = np.log(np.float64(RET_GAMMA[h]))
        qa = np.exp(lg64 * (tl + 1.0)).astype(np.float32)[:, None]
        kb = (np.exp(-lg64 * (tl + 1.0)) * np.float64(ks)).astype(np.float32)[:, None]
        rots.append(np.concatenate([cs * qa, cs * kb, sn * sgn * qa, sn * sgn * kb], axis=1))
    rot = np.concatenate(rots, axis=0).astype(np.float32)
    return cst, rot


USE_DMA_CAST = True
NS = 8
NTMP = 8
NB = 32


def build(T, debug=False):
    NTILES = T // TT
    NKT = T // 128
    nc = bass.Bass("TRN2", target_bir_lowering=False)

    def din(name, shape):
        return nc.dram_tensor(name, shape, F32, kind="ExternalInput").ap()

    x_d = din("x", [T, D])
    out_d = nc.dram_tensor("out", [T, D], F32, kind="ExternalOutput").ap()
    Wd_ = {}
    for pre in ("ffn1", "ffn2"):
        Wd_[pre + "_w_gate"] = din(pre + "_w_gate", [D, DFF])
        Wd_[pre + "_w_up"] = din(pre + "_w_up", [D, DFF])
        Wd_[pre + "_w_down"] = din(pre + "_w_down", [DFF, D])
    Wd_["w_in"] = din("w_in", [D, 5632])
    Wd_["w_attn_proj"] = din("w_attn_proj", [512, D])
    Wd_["w_ret_proj"] = din("w_ret_proj", [512, D])
    Wd_["w_out"] = din("w_out", [D, D])
    gains = {n: din(n, [1, D]) for n in ("ffn1_pre_g", "ffn1_post_g", "mix_pre_g", "mix_post_g", "ffn2_pre_g", "ffn2_post_g")}
    lam_d = {n: din(n, [1, 64]) for n in ("lambda_q1", "lambda_k1", "lambda_q2", "lambda_k2")}
    subg_d = din("diff_subln_g", [1, 128])
    cst_d = din("cst", [128, C_END])
    rot_d = din("rot", [4 * T, 512])
    dbg_d = {}
    if debug:
        for n in ("dbg1", "dbg2"):
            dbg_d[n] = nc.dram_tensor(n, [T, D], F32, kind="ExternalOutput").ap()

    pieces = {}

    def addp(key, parts):
        pieces[key] = dict(parts=parts, idx=len(pieces))

    for pre in ("ffn1", "ffn2"):
        for i in range(11):
            addp((pre, "g", i), [(Wd_[pre + "_w_gate"], 0, 8, 256 * i, 256)])
            addp((pre, "u", i), [(Wd_[pre + "_w_up"], 0, 8, 256 * i, 256)])
            addp((pre, "d", i), [(Wd_[pre + "_w_down"], 2 * i, 2, 0, 1024)])
    win = Wd_["w_in"]
    for h in range(4):
        addp(("A", h), [(win, 0, 8, 128 * h, 128), (win, 0, 8, 512 + 128 * h, 128)])
        addp(("R", h), [(win, 0, 8, 1536 + 128 * h, 128), (win, 0, 8, 2048 + 128 * h, 128)])
        addp(("RV", h), [(win, 0, 8, 2560 + 128 * h, 128), (win, 0, 8, 3072 + 128 * h, 128)])
    for i in range(2):
        addp(("V", i), [(win, 0, 8, 1024 + 256 * i, 256)])
        addp(("AP", i), [(Wd_["w_attn_proj"], 0, 4, 512 * i, 512)])
        addp(("RP", i), [(Wd_["w_ret_proj"], 0, 4, 512 * i, 512)])
    for i in range(4):
        addp(("GA", i), [(win, 0, 8, 3584 + 256 * i, 256)])
        addp(("GR", i), [(win, 0, 8, 4608 + 256 * i, 256)])
        addp(("WO", i), [(Wd_["w_out"], 0, 8, 256 * i, 256)])
    NP = len(pieces)
    scr_d = nc.dram_tensor("wscr", [NP, 128, 2048], BF16).ap()

    with ExitStack() as st:
        P = Prog(nc, st, n_dma_sems=92)

        sbuf_bytes = [0]

        def sb(name, shape, dt):
            n = 1
            for d_ in shape[1:]:
                n *= d_
            sbuf_bytes[0] += n * (4 if dt == F32 else 2)
            return st.enter_context(nc.sbuf_tensor("sb_" + name, shape, dt))

        Kc = sb("Kc", [128, 4, T], BF16)
        bK = [[Buf("K%d_%d" % (j, h)) for h in range(4)] for j in range(NTILES)]
        Vc = sb("Vc", [128, NKT, 4, 130], BF16)
        bV = [Buf("V%d" % k) for k in range(NKT)]
        xs_ = [sb("x%d" % s, [128, D], F32) for s in range(NSUB)]
        bx = [Buf("x%d" % s) for s in range(NSUB)]
        Bp = sb("Bp", [128, NB, 512], BF16)
        bB = [Buf("B%d" % i) for i in range(NB)]
        slots = [sb("slot%d" % i, [128, 2048], BF16) for i in range(NS)]
        bslot = [Buf("slot%d" % i) for i in range(NS)]
        NSTG = 4
        stage = [sb("stg%d" % i, [128, 1024], F32) for i in range(NSTG)]
        bstage = [Buf("stg%d" % i) for i in range(NSTG)]
        gpost = sb("gpost", [128, D], F32)
        bgpost = Buf("gpost")
        tmpf = [sb("tf%d" % i, [128, 512], F32) for i in range(NTMP)]
        btmpf = [Buf("tf%d" % i) for i in range(NTMP)]
        cst = sb("cst", [128, C_END], F32)
        bcst = Buf("cst")
        cbf = sb("cbf", [128, 256], BF16)
        bcbf = Buf("cbf")
        rott = [sb("rot%d" % i, [128, 512], F32) for i in range(2)]
        brott = [Buf("rot%d" % i) for i in range(2)]
        gT = sb("gT", [128, 24], F32)
        bgT = Buf("gT")
        Sst = sb("Sst", [128, 512], F32)
        Sbf = sb("Sbf", [128, 512], BF16)
        bS = [Buf("S%d" % h) for h in range(4)]
        bSbf = [Buf("Sbf%d" % h) for h in range(4)]
        small = sb("small", [128, 64], F32)
        bsmall = [Buf("sm%d" % i) for i in range(64)]
        lamt = sb("lamt", [128, 4, 64], F32)
        blamt = Buf("lamt")
        lamv = sb("lamv", [128, 8], F32)
        blamv = Buf("lamv")
        mhalf = sb("mhalf", [128, 4], F32)
        bmhalf = Buf("mhalf")
        gsub = sb("gsub", [128, 128], F32)
        bgsub = Buf("gsub")
        yaE = sb("yaE", [128, 4, 128], F32)
        byaE = [Buf("yaE%d" % i) for i in range(4)]
        datok = sb("datok", [128, 4, 128], BF16)
        bdatok = [Buf("datok%d" % i) for i in range(4)]
        rsm = [sb("rsm%d" % i, [128, 1024], BF16) for i in range(2)]
        brsm = [{n: Buf("rsm%d_%s" % (i, n)) for n in ("qkrot", "qkT", "vtok", "vz", "sTm", "yr", "g1")} for i in range(2)]
        g1t = [sb("g1t%d" % i, [128, 128], F32) for i in range(2)]

        pb = [st.enter_context(nc.psum_tensor("pb%d" % i, [128, 512], F32)) for i in range(8)]
        bpb = [Buf("pb%d" % i, excl=True) for i in range(8)]
        pbh = [p.bitcast(BF16) for p in pb]
        bpbr = [{n: bpb[6 + i] for n in ("A", "B", "S", "KV", "I", "E")} for i in range(2)]

        ident = cbf[:, 0:128]
        tri = cbf[:, 128:256]

        state = dict(tf=0, sm=0, slot=0, stg=0, cast=0, inscr=set())

        def tf():
            i = state["tf"]
            state["tf"] = (i + 1) % NTMP
            return tmpf[i], btmpf[i]

        def sm():
            i = state["sm"]
            state["sm"] = (i + 1) % 64
            return small[:, i:i + 1], bsmall[i]

        def Bv(i):
            return Bp[:, i, :]

        def get_piece(key):
            pc = pieces[key]
            pi = pc["idx"]
            si = state["slot"]
            state["slot"] = (si + 1) % NS
            sl, bsl = slots[si], bslot[si]
            use_dc = USE_DMA_CAST and key[0] in ("ffn1", "ffn2")
            if pi not in state["inscr"] and use_dc:
                parts = pc["parts"]
                nrc = parts[0][2]
                ctot = sum(p[4] for p in parts)
                sv = sl[:, :].rearrange("p (k c) -> p k c", k=nrc)
                co = 0
                for (W, r0, _, c0, ncol) in parts:
                    src = W[r0 * 128:(r0 + nrc) * 128, c0:c0 + ncol].rearrange("(k p) c -> p k c", p=128)
                    P.dma(POOL, sv[:, :, co:co + ncol], src, bsl, writes=[bsl])
                    co += ncol
                bscr = Buf("scr%d" % pi)
                pc["bscr"] = bscr
                P.dma(SP, scr_d[pi], sl[:, :], bsl, reads=[bsl], writes=[bscr])
                state["inscr"].add(pi)
            elif pi not in state["inscr"]:
                parts = pc["parts"]
                nrc = parts[0][2]
                hr = nrc // 2
                ctot = sum(p[4] for p in parts)
                for hf in range(2):
                    g = state["stg"]
                    state["stg"] = (g + 1) % NSTG
                    ce = 1 + state["cast"]
                    state["cast"] = (state["cast"] + 1) % 2
                    sv = stage[g][:, :].rearrange("p (k c) -> p k c", k=hr)
                    co = 0
                    for (W, r0, _, c0, ncol) in parts:
                        rs = (r0 + hf * hr) * 128
                        src = W[rs:rs + hr * 128, c0:c0 + ncol].rearrange("(k p) c -> p k c", p=128)
                        P.dma(SP, sv[:, :, co:co + ncol], src, bstage[g], writes=[bstage[g]])
                        co += ncol
                    if ce == 0:
                        P.op(POOL, lambda e, sl=sl, hf=hf, g=g: e.tensor_copy(out=sl[:, hf * 1024:(hf + 1) * 1024], in_=stage[g][:, :]),
                             reads=[bstage[g]], writes=[bsl])
                    elif ce == 1:
                        P.op(ACT, lambda e, sl=sl, hf=hf, g=g: e.activation(out=sl[:, hf * 1024:(hf + 1) * 1024], in_=stage[g][:, :], func=AF.Copy),
                             reads=[bstage[g]], writes=[bsl])
                    else:
                        P.op(DVE, lambda e, sl=sl, hf=hf, g=g: e.tensor_copy(out=sl[:, hf * 1024:(hf + 1) * 1024], in_=stage[g][:, :]),
                             reads=[bstage[g]], writes=[bsl])
                bscr = Buf("scr%d" % pi)
                pc["bscr"] = bscr
                P.dma(POOL, scr_d[pi], sl[:, :], bsl, reads=[bsl], writes=[bscr])
                state["inscr"].add(pi)
            else:
                P.dma(SP, sl[:, :], scr_d[pi], bsl, reads=[pc["bscr"]], writes=[bsl])
            return sl, bsl

        def convert_rest():
            order = [("V", 0), ("V", 1), ("R", 0), ("RV", 0), ("R", 1), ("RV", 1), ("A", 0), ("A", 1), ("R", 2), ("RV", 2),
                     ("R", 3), ("RV", 3), ("A", 2), ("A", 3), ("AP", 0), ("RP", 0), ("GA", 0), ("GR", 0), ("GA", 1), ("GR", 1),
                     ("AP", 1), ("RP", 1), ("GA", 2), ("GR", 2), ("GA", 3), ("GR", 3)] + [("WO", i) for i in range(4)]
            for i in range(11):
                order += [("ffn2", "g", i), ("ffn2", "u", i)]
            order += [("ffn2", "d", i) for i in range(11)]
            convbufs = [Buf("conv%d" % i) for i in range(44)]
            ci = 0
            for key in order:
                pc = pieces[key]
                pi = pc["idx"]
                if pi in state["inscr"]:
                    continue
                cb = convbufs[ci % len(convbufs)]
                ci += 1
                parts = pc["parts"]
                nrc = parts[0][2]
                dv = scr_d[pi].rearrange("p (k c) -> p k c", k=nrc)
                bscr = Buf("scr%d" % pi)
                pc["bscr"] = bscr
                co = 0
                for (W, r0, _, c0, ncol) in parts:
                    src = W[r0 * 128:(r0 + nrc) * 128, c0:c0 + ncol].rearrange("(k p) c -> p k c", p=128)
                    P.dma(POOL, dv[:, :, co:co + ncol], src, cb, writes=[bscr, cb])
                    co += ncol
                state["inscr"].add(pi)
                yield

        P.dma(POOL, cst[:, :], cst_d, bcst, writes=[bcst])
        P.op(POOL, lambda e: e.tensor_copy(out=cbf[:, :], in_=cst[:, 0:256]), reads=[bcst], writes=[bcbf])
        for gi, n in enumerate(("ffn1_pre_g", "mix_pre_g", "ffn2_pre_g")):
            P.dma(POOL, gT[:, gi * 8:(gi + 1) * 8], gains[n].rearrange("o (k p) -> p (o k)", p=128), bgT, writes=[bgT],
                  allow_slow_non_contiguous=True)
        P.op(POOL, lambda e: e.tensor_scalar(out=gT[:, :], in0=gT[:, :], scalar1=float(math.sqrt(D)), scalar2=None, op0=ALU.mult),
             reads=[bgT], writes=[bgT])
        for i, n in enumerate(("lambda_q1", "lambda_k1", "lambda_q2", "lambda_k2")):
            P.dma(POOL, lamt[:, i, :], lam_d[n].partition_broadcast(128), blamt, writes=[blamt])
        P.op(DVE, lambda e: e.tensor_tensor(out=lamt[:, 0, :], in0=lamt[:, 0, :], in1=lamt[:, 1, :], op=ALU.mult), reads=[blamt], writes=[blamt])
        P.op(DVE, lambda e: e.tensor_tensor(out=lamt[:, 2, :], in0=lamt[:, 2, :], in1=lamt[:, 3, :], op=ALU.mult), reads=[blamt], writes=[blamt])
        P.op(DVE, lambda e: e.tensor_reduce(out=lamv[:, 2:3], in_=lamt[:, 0, :], axis=AX.X, op=ALU.add), reads=[blamt], writes=[blamv])
        P.op(DVE, lambda e: e.tensor_reduce(out=lamv[:, 3:4], in_=lamt[:, 2, :], axis=AX.X, op=ALU.add), reads=[blamt], writes=[blamv])
        P.op(ACT, lambda e: e.activation(out=lamv[:, 4:6], in_=lamv[:, 2:4], func=AF.Exp), reads=[blamv], writes=[blamv])
        P.op(DVE, lambda e: e.scalar_tensor_tensor(out=lamv[:, 0:1], in0=lamv[:, 5:6], scalar=float(-LAMBDA_INIT), in1=lamv[:, 4:5],
                                                   op0=ALU.add, op1=ALU.subtract), reads=[blamv], writes=[blamv])
        P.op(POOL, lambda e: e.memset(mhalf[:, :], -0.5), writes=[bmhalf])
        P.dma(POOL, gsub[:, :], subg_d.partition_broadcast(128), bgsub, writes=[bgsub])
        P.op(POOL, lambda e: e.tensor_scalar(out=gsub[:, :], in0=gsub[:, :], scalar1=float((1.0 - LAMBDA_INIT) * math.sqrt(128.0)),
                                             scalar2=None, op0=ALU.mult), reads=[bgsub], writes=[bgsub])
        P.op(POOL, lambda e: e.memset(Vc[:, :, :, :].rearrange("p a b c -> p (a b c)"), 1.0), writes=bV)
        P.op(POOL, lambda e: e.memset(Sst[:, :], 0.0), writes=bS)
        P.op(POOL, lambda e: e.memset(Sbf[:, :], 0.0), writes=bSbf)
        neglam = lamv[:, 0:1]

        def rsqrt_pool(dst, bdst, src, bsrc, n=1):
            P.op(POOL, lambda e: e.tensor_tensor(out=dst, in0=src, in1=mhalf[:, 0:n], op=ALU.pow),
                 reads=[bsrc, bmhalf], writes=[bdst])

        xnb = [sb("xnb%d" % i, [128, D], BF16) for i in range(NSUB)]
        bxnb = [Buf("xnb%d" % i) for i in range(NSUB)]
        XB = [xs_, stage]
        bXB = [bx, bstage]

        def pn_elem(s, X, bX):
            ss, bss = sm()
            se, bse = sm()
            r, br = sm()
            xb_ = xnb[s]
            P.op(ACT, lambda e: e.activation(out=xb_[:, :], in_=X[s][:, :], func=AF.Square, accum_out=ss),
                 reads=[bX[s]], writes=[bxnb[s], bss])
            P.op(DVE, lambda e: e.tensor_scalar(out=se, in0=ss, scalar1=float(NORM_EPS * D), scalar2=None, op0=ALU.add),
                 reads=[bss], writes=[bse])
            rsqrt_pool(r, br, se, bse)
            P.op(ACT, lambda e: e.activation(out=xb_[:, :], in_=X[s][:, :], func=AF.Copy, scale=r),
                 reads=[bX[s], br], writes=[bxnb[s]])

        def pn_pe(gi, s, bank):
            xb_ = xnb[s]
            for k in range(8):
                P.op(PE, lambda e, k=k: e.transpose(out=pbh[bank][:, k * 128:(k + 1) * 128], in_=xb_[:, k * 128:(k + 1) * 128], identity=ident),
                     reads=[bxnb[s], bcbf], writes=[bpb[bank]])
            P.op(DVE, lambda e: e.tensor_tensor(
                out=Bp[:, 0:8, s * 128:(s + 1) * 128],
                in0=pbh[bank][:, :].rearrange("p (k t) -> p k t", k=8),
                in1=gT[:, gi * 8:(gi + 1) * 8].unsqueeze(2).broadcast_to([128, 8, 128]), op=ALU.mult),
                reads=[bpb[bank], bgT], writes=bB[0:8])

        def load_gpost(name, coef):
            P.dma(POOL, gpost[:, :], gains[name].partition_broadcast(128), bgpost, writes=[bgpost])
            P.op(ACT, lambda e: e.activation(out=gpost[:, :], in_=gpost[:, :], func=AF.Copy, scale=float(coef * math.sqrt(D))),
                 reads=[bgpost], writes=[bgpost])

        def post_norm(s, bankA, bankB, fscale, X, bX):
            ssA, bssA = sm()
            ssB, bssB = sm()
            se, bse = sm()
            r, br = sm()
            for (bank, ss, bss) in ((bankA, ssA, bssA), (bankB, ssB, bssB)):
                junk, bjunk = tf()
                P.op(ACT, lambda e, bank=bank, ss=ss, junk=junk: e.activation(out=junk.bitcast(BF16)[:, 0:512], in_=pb[bank][:, :],
                                                                            func=AF.Square, accum_out=ss),
                     reads=[bpb[bank]], writes=[bjunk, bss])
            P.op(DVE, lambda e: e.tensor_scalar(out=se, in0=ssA, scalar1=ssB, scalar2=float(NORM_EPS * D / (fscale * fscale)),
                                                op0=ALU.add, op1=ALU.add), reads=[bssA, bssB], writes=[bse])
            rsqrt_pool(r, br, se, bse)
            for hf, bank in enumerate((bankA, bankB)):
                t, bt = tf()
                P.op(DVE, lambda e, bank=bank, t=t, hf=hf: e.scalar_tensor_tensor(out=t[:, :], in0=pb[bank][:, :], scalar=r,
                                                                                  in1=gpost[:, hf * 512:(hf + 1) * 512],
                                                                                  op0=ALU.mult, op1=ALU.mult),
                     reads=[bpb[bank], br, bgpost], writes=[bt])
                P.op(DVE, lambda e, t=t, hf=hf: e.tensor_tensor(out=X[s][:, hf * 512:(hf + 1) * 512], in0=X[s][:, hf * 512:(hf + 1) * 512],
                                                                in1=t[:, :], op=ALU.add),
                     reads=[bt, bX[s]], writes=[bX[s]])

        def ffn(pre, post_name, X, bX, early=None, early_pe=None, after_post=None, mid_b=None, tail=None):
            U0 = 8
            for fi in range(11):
                wg, bwg = get_piece((pre, "g", fi))
                wu, bwu = get_piece((pre, "u", fi))
                for c in range(2):
                    f = 2 * fi + c
                    gb_, ub_ = f % 2, 2 + f % 2
                    for (w, bw, bank) in ((wg, bwg, gb_), (wu, bwu, ub_)):
                        for k in range(8):
                            P.op(PE, lambda e, w=w, bank=bank, k=k, c=c: e.matmul(out=pb[bank][:, :], lhsT=w[:, k * 256 + c * 128:k * 256 + c * 128 + 128],
                                                                               rhs=Bv(k), start=(k == 0), stop=(k == 7)),
                                 reads=[bw, bB[k]], writes=[bpb[bank]])
                    t, bt = tf()
                    v, bv = tf()
                    P.op(ACT, lambda e, t=t, gb_=gb_: e.activation(out=t[:, :], in_=pb[gb_][:, :], func=AF.Tanh, scale=0.5),
                         reads=[bpb[gb_]], writes=[bt])
                    P.op(DVE, lambda e, t=t, v=v, gb_=gb_: e.scalar_tensor_tensor(out=v[:, :], in0=t[:, :], scalar=1.0, in1=pb[gb_][:, :],
                                                                                  op0=ALU.add, op1=ALU.mult),
                         reads=[bt, bpb[gb_]], writes=[bv])
                    P.op(DVE, lambda e, v=v, ub_=ub_, f=f: e.tensor_tensor(out=Bv(U0 + f), in0=v[:, :], in1=pb[ub_][:, :], op=ALU.mult),
                         reads=[bv, bpb[ub_]], writes=[bB[U0 + f]])
            load_gpost(post_name, 0.5)
            for ps_ in range(2):
                base = 4 if ps_ == 0 else 0
                for fi in range(11):
                    if ps_ == 0 and fi == 0 and early is not None:
                        early()
                    if ps_ == 0 and fi == 6 and early_pe is not None:
                        early_pe()
                    if ps_ == 1 and fi == 5 and mid_b is not None:
                        mid_b()
                    wd, bwd = get_piece((pre, "d", fi))
                    for c in range(2):
                        f = 2 * fi + c
                        for sl in range(2):
                            s = 2 * ps_ + sl
                            for hf in range(2):
                                bank = base + sl * 2 + hf
                                P.op(PE, lambda e, wd=wd, bank=bank, f=f, s=s, c=c, hf=hf: e.matmul(
                                    out=pb[bank][:, :], lhsT=Bp[:, U0 + f, s * 128:(s + 1) * 128],
                                    rhs=wd[:, c * 1024 + hf * 512:c * 1024 + hf * 512 + 512], start=(f == 0), stop=(f == 21)),
                                    reads=[bwd, bB[U0 + f]], writes=[bpb[bank]])
                for sl in range(2):
                    s = 2 * ps_ + sl
                    post_norm(s, base + sl * 2, base + sl * 2 + 1, 0.5, X, bX)
                if after_post is not None:
                    for sl in range(2):
                        after_post(2 * ps_ + sl)
            if tail is not None:
                tail()

        QT0, DAT0, YRT0, MG0, PT0 = 16, 20, 24, 8, 28
        cnt = dict(pt=0, rot=0)


        PTS = [28, 29, 30, 31]

        def att_gen(j):
            pend_f2 = None
            pend_fin = None
            for q_ in (16, 18):
                P.op(POOL, lambda e, q_=q_: e.memset(Bp[64:128, q_, :], 0.0), writes=[bB[q_]])
                P.op(POOL, lambda e, q_=q_: e.memset(Bp[0:64, q_ + 1, :], 0.0), writes=[bB[q_ + 1]])
            for h in range(4):
                wa, bwa = get_piece(("A", h))
                qi = QT0 + (h % 2)
                for part in range(2):
                    for k in range(8):
                        P.op(PE, lambda e, part=part, k=k, wa=wa: e.matmul(out=pb[part][:, :], lhsT=wa[:, k * 256 + part * 128:k * 256 + part * 128 + 128],
                                                                           rhs=Bv(k), start=(k == 0), stop=(k == 7)),
                             reads=[bwa, bB[k]], writes=[bpb[part]])
                qz = (QT0 + 2 * (h % 2), QT0 + 2 * (h % 2) + 1)
                P.op(ACT, lambda e, qz=qz: e.activation(out=Bp[0:64, qz[0], :], in_=pb[0][0:64, :], func=AF.Copy), reads=[bpb[0]], writes=[bB[qz[0]]])
                P.op(DVE, lambda e, qz=qz: e.tensor_copy(out=Bp[64:128, qz[1], :], in_=pb[0][64:128, :]), reads=[bpb[0]], writes=[bB[qz[1]]])
                P.op(DVE, lambda e, h=h: e.tensor_copy(out=Kc[:, h, j * TT:(j + 1) * TT], in_=pb[1][:, :]), reads=[bpb[1]], writes=[bK[j][h]])
                yield
                if pend_f2 is not None:
                    pend_f2()
                    pend_f2 = None
                nkt = 4 * j + 4

                def acc(c, qs, lo, hi):
                    return pb[2 + 2 * c + qs // 2][:, (qs % 2) * 256 + lo:(qs % 2) * 256 + hi]

                info = {}

                def stageA(kt, c, info=info, h=h, qz=qz):
                    cdiag = kt - 4 * j
                    q0 = max(cdiag, 0)
                    ncols = 512 - q0 * 128
                    jk = kt // 4
                    qi = qz[c]
                    P.op(PE, lambda e: e.matmul(out=pb[c][:, 0:ncols], lhsT=Kc[:, h, kt * 128:(kt + 1) * 128],
                                                rhs=Bp[:, qi, q0 * 128:512], start=True, stop=True),
                         reads=[bK[jk][h], bB[qi]], writes=[bpb[c]])
                    pt = PTS[cnt["pt"] % 4]
                    cnt["pt"] += 1
                    P.op(ACT, lambda e: e.activation(out=Bp[:, pt, 0:ncols], in_=pb[c][:, 0:ncols], func=AF.Exp, scale=0.125),
                         reads=[bpb[c]], writes=[bB[pt]])
                    if cdiag >= 0:
                        P.op(POOL, lambda e: e.tensor_tensor(out=Bp[:, pt, 0:128], in0=Bp[:, pt, 0:128], in1=tri, op=ALU.mult),
                             reads=[bB[pt], bcbf], writes=[bB[pt]])
                    info[(kt, c)] = (pt, q0)

                def stageB(kt, c, info=info, h=h):
                    pt, q0 = info[(kt, c)]
                    for qs in range(q0, 4):
                        off = (qs - q0) * 128
                        P.op(PE, lambda e, qs=qs, off=off: e.matmul(
                            out=acc(c, qs, 0, 129), lhsT=Bp[:, pt, off:off + 128], rhs=Vc[:, kt, h, 0:129],
                            start=(kt == 0 and qs % 2 == 0), stop=(kt == 4 * j + qs), skip_group_check=True),
                            reads=[bB[pt], bV[kt]], writes=[bpb[2 + 2 * c + qs // 2]])

                stageA(0, 0)
                stageA(0, 1)
                for kt in range(nkt):
                    for c in range(2):
                        if kt + 1 < nkt:
                            stageA(kt + 1, c)
                        stageB(kt, c)
                    if kt == min(1, nkt - 1) and pend_fin is not None:
                        pend_fin()
                        pend_fin = None
                    yield
                for c in range(2):
                    for qs in range(4):
                        rl, brl = sm()
                        ab = bpb[2 + 2 * c + qs // 2]
                        P.op(DVE, lambda e, rl=rl, qs=qs, c=c: e.reciprocal(out=rl, in_=acc(c, qs, 128, 129)), reads=[ab], writes=[brl])
                        if c == 0:
                            P.op(DVE, lambda e, rl=rl, qs=qs, c=c: e.tensor_scalar(out=yaE[:, qs, :], in0=acc(c, qs, 0, 128), scalar1=rl, scalar2=None, op0=ALU.mult),
                                 reads=[ab, brl], writes=[byaE[qs]])
                        else:
                            cf, bcf = sm()
                            P.op(DVE, lambda e, rl=rl, cf=cf: e.tensor_tensor(out=cf, in0=rl, in1=neglam, op=ALU.mult), reads=[brl, blamv], writes=[bcf])
                            P.op(DVE, lambda e, cf=cf, qs=qs, c=c: e.scalar_tensor_tensor(out=yaE[:, qs, :], in0=acc(c, qs, 0, 128), scalar=cf, in1=yaE[:, qs, :],
                                                                                        op0=ALU.mult, op1=ALU.add),
                                 reads=[ab, bcf, byaE[qs]], writes=[byaE[qs]])

                def f2():
                    jk_, bjk_ = tf()
                    sc = [(sm(), sm(), sm()) for _ in range(4)]
                    for qs in range(4):
                        ss, bss = sc[qs][0]
                        P.op(DVE, lambda e, ss=ss, qs=qs: e.scalar_tensor_tensor(out=jk_[:, qs * 128:(qs + 1) * 128], in0=yaE[:, qs, :], scalar=1.0,
                                                                                in1=yaE[:, qs, :], op0=ALU.mult, op1=ALU.mult, accum_out=ss),
                             reads=[byaE[qs]], writes=[bjk_, bss])
                    for qs in range(4):
                        (ss, bss), (se, bse), _ = sc[qs]
                        P.op(DVE, lambda e, se=se, ss=ss: e.tensor_scalar(out=se, in0=ss, scalar1=float(SUBLN_EPS * 128), scalar2=None, op0=ALU.add),
                             reads=[bss], writes=[bse])
                    for qs in range(4):
                        _, (se, bse), (r, br) = sc[qs]
                        rsqrt_pool(r, br, se, bse)
                    for qs in range(4):
                        r, br = sc[qs][2]
                        P.op(DVE, lambda e, r=r, qs=qs: e.scalar_tensor_tensor(out=datok[:, qs, :], in0=yaE[:, qs, :], scalar=r, in1=gsub[:, :],
                                                                              op0=ALU.mult, op1=ALU.mult),
                             reads=[byaE[qs], br, bgsub], writes=[bdatok[qs]])

                def fin(h=h):
                    for qs in range(4):
                        P.op(PE, lambda e, qs=qs: e.transpose(out=pbh[1][:, qs * 128:(qs + 1) * 128], in_=datok[:, qs, :], identity=ident),
                             reads=[bdatok[qs], bcbf], writes=[bpb[1]])
                    P.op(ACT, lambda e: e.activation(out=Bv(DAT0 + h), in_=pbh[1][:, 0:512], func=AF.Copy), reads=[bpb[1]], writes=[bB[DAT0 + h]])
                pend_f2, pend_fin = f2, fin
                yield
            if pend_f2 is not None:
                pend_f2()
            if pend_fin is not None:
                pend_fin()

        def ret_gen(j, heads, p):
            pj = mb = 6 + p
            rs = rsm[p]
            brs = brsm[p]
            g1 = g1t[p]
            rt, brt = rott[p], brott[p]
            bA, bBk = bpbr[p]["A"], bpbr[p]["B"]
            bq = bpbr[p]
            qkrot, qkT = rs[:, 0:256], rs[:, 256:512]
            vtok, vz, sTm, yr = rs[:, 512:640], rs[:, 640:768], rs[:, 768:896], rs[:, 896:1024]
            for h in heads:
                wr, bwr = get_piece(("R", h))
                wrv, bwrv = get_piece(("RV", h))
                cd = float(RET_GAMMA[h] ** 128)
                hs = slice(h * 128, (h + 1) * 128)
                for s in range(NSUB):
                    tok = j * TT + s * 128
                    P.dma(POOL, rt[:, :], rot_d[h * T + tok:h * T + tok + 128, :], brt, writes=[brt])
                    for (w, bw, lo, bb) in ((wr, bwr, 0, bA), (wrv, bwrv, 256, bBk)):
                        for k in range(8):
                            P.op(PE, lambda e, w=w, lo=lo, k=k, s=s: e.matmul(
                                out=pb[pj][:, lo:lo + 256], lhsT=Bp[:, k, s * 128:(s + 1) * 128], rhs=w[:, k * 256:(k + 1) * 256],
                                start=(k == 0), stop=(k == 7)), reads=[bw, bB[k]], writes=[bb])
                    t1, bt1 = tf()
                    t2, bt2 = tf()
                    P.op(DVE, lambda e, t1=t1: e.tensor_tensor(out=t1[:, 0:256], in0=pb[pj][:, 0:256], in1=rt[:, 0:256], op=ALU.mult),
                         reads=[bA, brt], writes=[bt1])
                    P.op(DVE, lambda e, t2=t2: e.tensor_tensor(out=t2[:, 0:256:2], in0=pb[pj][:, 1:256:2], in1=rt[:, 256:512:2], op=ALU.mult),
                         reads=[bA, brt], writes=[bt2])
                    P.op(DVE, lambda e, t2=t2: e.tensor_tensor(out=t2[:, 1:256:2], in0=pb[pj][:, 0:256:2], in1=rt[:, 257:512:2], op=ALU.mult),
                         reads=[bA, brt], writes=[bt2])
                    P.op(DVE, lambda e, t1=t1, t2=t2: e.tensor_tensor(out=qkrot, in0=t1[:, 0:256], in1=t2[:, 0:256], op=ALU.add),
                         reads=[bt1, bt2], writes=[brs["qkrot"]])
                    P.op(DVE, lambda e: e.tensor_copy(out=vtok, in_=pb[pj][:, 256:384]), reads=[bBk], writes=[brs["vtok"]])
                    tg, btg = tf()
                    P.op(ACT, lambda e, tg=tg: e.activation(out=tg[:, 0:128], in_=pb[pj][:, 384:512], func=AF.Tanh, scale=0.5),
                         reads=[bBk], writes=[btg])
                    P.op(DVE, lambda e, tg=tg: e.scalar_tensor_tensor(out=g1[:, :], in0=tg[:, 0:128], scalar=1.0, in1=pb[pj][:, 384:512],
                                                                      op0=ALU.add, op1=ALU.mult),
                         reads=[btg, bBk], writes=[brs["g1"]])
                    yield
                    for i in range(2):
                        P.op(PE, lambda e, i=i: e.transpose(out=pbh[pj][:, i * 128:(i + 1) * 128], in_=rs[:, i * 128:(i + 1) * 128], identity=ident),
                             reads=[brs["qkrot"], bcbf], writes=[bA])
                    P.op(DVE, lambda e: e.tensor_copy(out=qkT, in_=pbh[pj][:, 0:256]), reads=[bA], writes=[brs["qkT"]])
                    yield
                    P.op(PE, lambda e: e.matmul(out=pb[mb][:, 256:384], lhsT=rs[:, 384:512], rhs=rs[:, 256:384], start=True, stop=True),
                         reads=[brs["qkT"]], writes=[bq["S"]])
                    P.op(DVE, lambda e: e.tensor_tensor(out=sTm, in0=pb[mb][:, 256:384], in1=cst[:, C_TRI:C_TRI + 128], op=ALU.mult),
                         reads=[bq["S"], bcst], writes=[brs["sTm"]])
                    yield
                    P.op(PE, lambda e: e.matmul(out=pb[mb][:, 0:128], lhsT=sTm, rhs=vtok, start=True, stop=False),
                         reads=[brs["sTm"], brs["vtok"]], writes=[bq["I"]])
                    P.op(PE, lambda e, hs=hs: e.matmul(out=pb[mb][:, 0:128], lhsT=rs[:, 256:384], rhs=Sbf[:, hs], start=False, stop=True),
                         reads=[brs["qkT"], bSbf[h]], writes=[bq["I"]])
                    P.op(PE, lambda e: e.matmul(out=pb[mb][:, 384:512], lhsT=rs[:, 128:256], rhs=vtok, start=True, stop=True),
                         reads=[brs["qkrot"], brs["vtok"]], writes=[bq["KV"]])
                    P.op(DVE, lambda e, hs=hs, cd=cd: e.scalar_tensor_tensor(out=Sst[:, hs], in0=Sst[:, hs], scalar=cd, in1=pb[mb][:, 384:512],
                                                                           op0=ALU.mult, op1=ALU.add),
                         reads=[bS[h], bq["KV"]], writes=[bS[h]])
                    P.op(POOL, lambda e, hs=hs, cd=cd: e.tensor_scalar(out=Sbf[:, hs], in0=Sst[:, hs], scalar1=cd, scalar2=None, op0=ALU.mult),
                         reads=[bS[h]], writes=[bSbf[h]])
                    jk2, bjk2 = tf()
                    ss, bss = sm()
                    se, bse = sm()
                    r2, br2 = sm()
                    cn = 0.5 * math.sqrt(128.0)
                    P.op(ACT, lambda e, jk2=jk2, ss=ss: e.activation(out=jk2[:, 0:128], in_=pb[mb][:, 0:128], func=AF.Square, scale=float(1.0 / cn), accum_out=ss),
                         reads=[bq["I"]], writes=[bjk2, bss])
                    P.op(DVE, lambda e, se=se, ss=ss: e.tensor_scalar(out=se, in0=ss, scalar1=float(NORM_EPS * 128 / (cn * cn)), scalar2=None, op0=ALU.add),
                         reads=[bss], writes=[bse])
                    rsqrt_pool(r2, br2, se, bse)
                    P.op(DVE, lambda e, r2=r2: e.scalar_tensor_tensor(out=yr, in0=pb[mb][:, 0:128], scalar=r2, in1=g1[:, :], op0=ALU.mult, op1=ALU.mult),
                         reads=[bq["I"], br2, brs["g1"]], writes=[brs["yr"]])
                    yield
                    yield
                    P.op(PE, lambda e: e.transpose(out=pbh[pj][:, 512:640], in_=yr, identity=ident), reads=[brs["yr"], bcbf], writes=[bBk])
                    P.op(DVE, lambda e, h=h, s=s: e.tensor_copy(out=Bp[:, YRT0 + h, s * 128:(s + 1) * 128], in_=pbh[pj][:, 512:640]),
                         reads=[bBk], writes=[bB[YRT0 + h]])
                    yield

        def mixer(j, X, bX, pre_tail=None):
            wv = [get_piece(("V", i)) for i in range(2)]

            def vproj(s):
                bank = s
                for i in range(2):
                    for k in range(8):
                        P.op(PE, lambda e, bank=bank, i=i, k=k, s=s: e.matmul(
                            out=pb[bank][:, i * 256:(i + 1) * 256], lhsT=Bp[:, k, s * 128:(s + 1) * 128],
                            rhs=wv[i][0][:, k * 256:(k + 1) * 256], start=(k == 0), stop=(k == 7)),
                            reads=[wv[i][1], bB[k]], writes=[bpb[bank]])
                kt = 4 * j + s
                P.op(ACT, lambda e, bank=bank, kt=kt: e.activation(out=Vc[:, kt, :, 0:128],
                                                                   in_=pb[bank][:, :].rearrange("p (h e) -> p h e", h=4), func=AF.Copy),
                     reads=[bpb[bank]], writes=[bV[kt]])

            gens = [ret_gen(j, (0, 2), 0), ret_gen(j, (1, 3), 1)]
            vproj(0)
            vproj(1)
            for g_ in gens:
                next(g_)
            if pre_tail is not None:
                pre_tail()
            vproj(2)
            vproj(3)

            n_att = 4 * (4 * j + 6)
            ratio = max(1, int(n_att / 48.0 + 0.5))
            ga_ = att_gen(j)
            att_live = True
            while gens or att_live:
                for _ in range(ratio if gens else 1):
                    if att_live:
                        try:
                            next(ga_)
                        except StopIteration:
                            att_live = False
                for g_ in list(gens):
                    try:
                        next(g_)
                    except StopIteration:
                        gens.remove(g_)
                if conv_gen[0] is not None:
                    next(conv_gen[0], None)

            load_gpost("mix_post_g", 1.0)
            gate_w = {}
            tq = {}

            def stage1(g):
                if g % 2 == 0:
                    gate_w["ga"] = get_piece(("GA", g // 2))
                    gate_w["gr"] = get_piece(("GR", g // 2))
                par = (g % 2) * 4
                cc = g % 2
                ts_ = []
                for (w, bank) in ((gate_w["ga"], par + 2), (gate_w["gr"], par + 3)):
                    for k in range(8):
                        P.op(PE, lambda e, w=w, bank=bank, k=k, cc=cc: e.matmul(
                            out=pb[bank][:, :], lhsT=w[0][:, k * 256 + cc * 128:k * 256 + cc * 128 + 128], rhs=Bv(k),
                            start=(k == 0), stop=(k == 7)), reads=[w[1], bB[k]], writes=[bpb[bank]])
                    t, bt = tf()
                    P.op(ACT, lambda e, t=t, bank=bank: e.activation(out=t[:, :], in_=pb[bank][:, :], func=AF.Tanh, scale=0.5),
                         reads=[bpb[bank]], writes=[bt])
                    ts_.append((t, bt))
                tq[g] = ts_

            proj_w = {}

            def stage2(g):
                if g % 4 == 0:
                    proj_w["ap"] = get_piece(("AP", g // 4))
                    proj_w["rp"] = get_piece(("RP", g // 4))
                par = (g % 2) * 4
                cg = g % 4
                for (w, bank, src0) in ((proj_w["ap"], par + 0, DAT0), (proj_w["rp"], par + 1, YRT0)):
                    for c4 in range(4):
                        P.op(PE, lambda e, w=w, bank=bank, src0=src0, c4=c4, cg=cg: e.matmul(
                            out=pb[bank][:, :], lhsT=w[0][:, c4 * 512 + cg * 128:c4 * 512 + cg * 128 + 128], rhs=Bv(src0 + c4),
                            start=(c4 == 0), stop=(c4 == 3)), reads=[w[1], bB[src0 + c4]], writes=[bpb[bank]])
                ts_ = tq.pop(g)
                for (t, bt), ybank in zip(ts_, (par + 0, par + 1)):
                    P.op(DVE, lambda e, t=t, ybank=ybank: e.scalar_tensor_tensor(out=t[:, :], in0=t[:, :], scalar=1.0, in1=pb[ybank][:, :],
                                                                              op0=ALU.add, op1=ALU.mult),
                         reads=[bt, bpb[ybank]], writes=[bt])
                P.op(DVE, lambda e, ts_=ts_, g=g: e.tensor_tensor(out=Bv(MG0 + g), in0=ts_[0][0][:, :], in1=ts_[1][0][:, :], op=ALU.add),
                     reads=[ts_[0][1], ts_[1][1]], writes=[bB[MG0 + g]])

            stage1(0)
            for g in range(8):
                if g + 1 < 8:
                    stage1(g + 1)
                stage2(g)
            wo = [get_piece(("WO", i)) for i in range(4)]
            for s in range(NSUB):
                bA, bB2 = (s % 2) * 2, (s % 2) * 2 + 1
                for i in range(4):
                    bank = bA if i < 2 else bB2
                    col = (i % 2) * 256
                    for k in range(8):
                        P.op(PE, lambda e, bank=bank, col=col, k=k, s=s, i=i: e.matmul(
                            out=pb[bank][:, col:col + 256], lhsT=Bp[:, MG0 + k, s * 128:(s + 1) * 128], rhs=wo[i][0][:, k * 256:(k + 1) * 256],
                            start=(k == 0), stop=(k == 7)), reads=[wo[i][1], bB[MG0 + k]], writes=[bpb[bank]])
                if s >= 2:
                    pn_pe(2, s - 2, 4 + (s - 2) % 2)
                post_norm(s, bA, bB2, 0.5, X, bX)
                if s >= 1:
                    pn_elem(s - 1, X, bX)
            pn_elem(NSUB - 1, X, bX)
            pn_pe(2, NSUB - 2, 4 + (NSUB - 2) % 2)
            pn_pe(2, NSUB - 1, 4 + (NSUB - 1) % 2)

        stores = []
        conv_gen = [None]

        def load_x(j, X, bX):
            for s in range(NSUB):
                r0 = (j * NSUB + s) * 128
                P.dma(POOL, X[s][:, :], x_d[r0:r0 + 128, :], bX[s], writes=[bX[s]])

        load_x(0, XB[0], bXB[0])
        for s in range(NSUB):
            pn_elem(s, XB[0], bXB[0])
            pn_pe(0, s, 6 + s % 2)
        for j in range(NTILES):
            X, bX = XB[j % 2], bXB[j % 2]
            Xn, bXn = XB[(j + 1) % 2], bXB[(j + 1) % 2]
            last = (j == NTILES - 1)

            def f1_after(s, X=X, bX=bX):
                pn_elem(s, X, bX)

            def f1_mid():
                pn_pe(1, 0, 4)
                pn_pe(1, 1, 6)

            def f1_tail():
                pn_pe(1, 2, 5)
                pn_pe(1, 3, 7)

            def f1_early():
                conv_gen[0] = convert_rest()
                for _ in range(14):
                    next(conv_gen[0], None)

            ffn("ffn1", "ffn1_post_g", X, bX, early=f1_early if j == 0 else None, after_post=f1_after, mid_b=f1_mid)
            if debug:
                for s in range(NSUB):
                    r0 = (j * NSUB + s) * 128
                    stores.append(P.dma(POOL, dbg_d["dbg1"][r0:r0 + 128, :], X[s][:, :], bX[s], reads=[bX[s]]))
            mixer(j, X, bX, pre_tail=f1_tail)
            if conv_gen[0] is not None:
                for _ in conv_gen[0]:
                    pass
                conv_gen[0] = None
            if debug:
                for s in range(NSUB):
                    r0 = (j * NSUB + s) * 128
                    stores.append(P.dma(POOL, dbg_d["dbg2"][r0:r0 + 128, :], X[s][:, :], bX[s], reads=[bX[s]]))
            overlap = (not last) and j >= 1
            if overlap:
                load_x(j + 1, Xn, bXn)

            def f2_early(Xn=Xn, bXn=bXn):
                for s in range(NSUB):
                    pn_elem(s, Xn, bXn)

            def f2_early_pe():
                for s in range(NSUB):
                    pn_pe(0, s, s)

            def f2_after(s, j=j, X=X, bX=bX):
                r0 = (j * NSUB + s) * 128
                stores.append(P.dma(POOL, out_d[r0:r0 + 128, :], X[s][:, :], bX[s], reads=[bX[s]]))

            ffn("ffn2", "ffn2_post_g", X, bX, early=f2_early if overlap else None, early_pe=f2_early_pe if overlap else None,
                after_post=f2_after)
            if (not last) and not overlap:
                load_x(j + 1, Xn, bXn)
                for s in range(NSUB):
                    pn_elem(s, Xn, bXn)
                    pn_pe(0, s, 6 + s % 2)
        if debug:
            print("SBUF bytes/partition", sbuf_bytes[0])
        P.final_wait(POOL, stores)
        P.emit()
    return nc


W2D = ("ffn1_w_gate", "ffn1_w_up", "ffn1_w_down", "w_in", "w_attn_proj", "w_ret_proj", "w_out",
       "ffn2_w_gate", "ffn2_w_up", "ffn2_w_down")
VECS = ("ffn1_pre_g", "ffn1_post_g", "mix_pre_g", "mix_post_g", "ffn2_pre_g", "ffn2_post_g",
        "lambda_q1", "lambda_k1", "lambda_q2", "lambda_k2", "diff_subln_g")


def make_in_maps(inputs, T, ncores, debug=False):
    cst, rot = _const_tables(T)
    shared = {"cst": cst, "rot": rot}
    for n in W2D:
        shared[n] = np.ascontiguousarray(np.asarray(inputs[n], dtype=np.float32)[0])
    for n in VECS:
        shared[n] = np.ascontiguousarray(np.asarray(inputs[n], dtype=np.float32).reshape(1, -1))
    x = np.asarray(inputs["x"], dtype=np.float32)
    maps = []
    for c in range(ncores):
        m = dict(shared)
        m["x"] = np.ascontiguousarray(x[c, :T])
        maps.append(m)
    return maps


def kernel(**inputs):
    nc = build(SEQ)
    in_maps = make_in_maps(inputs, SEQ, NCORES)
    res = run_bass_kernel_spmd(nc, in_maps, core_ids=list(range(NCORES)))
    out = np.stack([np.asarray(res.results[c]["out"]) for c in range(NCORES)], axis=0)
    return out.astype(np.float32, copy=False)
```
